# Optimizing a Trainium2 kernel written in Bass

```python
import math
import jax, jax.numpy as jnp
from jax import lax
import numpy as np

D_MODEL = 2048
BATCH = 8
SEQ = 2048
DEPTH = 4

MEM_LEN = 256
HEAD_DIM = 128
A_HEADS = 6
IDX_HEADS = 16
IDX_DIM = 64
TOPK_MAX = 256
B_HEADS = 6
Q_LORA = 512
KV_LORA = 512
NOPE_DIM = 128
ROPE_DIM = 64
V_DIM = 128
ROPE_THETA = 10000.0
C_HEADS = 4
N_BRANCH = 3
REL_BUCKETS = 32
REL_MAX_DIST = 128
D_FF = 5632
Q_BLOCK = 128
LN_EPS = 1e-5
RMS_EPS = 1e-6
DEEPNORM_ALPHA = (2 * DEPTH) ** 0.25
DEEPNORM_BETA = (8 * DEPTH) ** -0.25
A_WIDTH = A_HEADS * HEAD_DIM
B_WIDTH = B_HEADS * V_DIM
C_WIDTH = C_HEADS * HEAD_DIM
MIX_WIDTH = A_WIDTH + B_WIDTH + C_WIDTH
IN_SIZES = (A_WIDTH, HEAD_DIM, HEAD_DIM,
            IDX_HEADS * IDX_DIM, IDX_DIM, IDX_HEADS,
            Q_LORA, KV_LORA, ROPE_DIM,
            C_WIDTH,
            N_BRANCH * D_MODEL)
IN_COLS = sum(IN_SIZES)

kernel_name = 'hybrid_dsa_mla_memory_macaron_deepnorm'


def layer_norm(x, g, b):
    xf = x.astype(jnp.float32)
    mu = jnp.mean(xf, axis=-1, keepdims=True)
    var = jnp.mean(jnp.square(xf - mu), axis=-1, keepdims=True)
    return ((xf - mu) * lax.rsqrt(var + LN_EPS) * g + b).astype(x.dtype)


def rms_norm(x, g):
    xf = x.astype(jnp.float32)
    return (xf * lax.rsqrt(jnp.mean(jnp.square(xf), axis=-1, keepdims=True) + RMS_EPS) * g).astype(x.dtype)


def swiglu(x, w_up, w_down):
    gate, up = jnp.split(x @ w_up, 2, axis=-1)
    return (jax.nn.silu(gate) * up) @ w_down


def rope(x, cos, sin):
    x1, x2 = jnp.split(x, 2, axis=-1)
    cos = cos.astype(x.dtype)
    sin = sin.astype(x.dtype)
    return jnp.concatenate([x1 * cos - x2 * sin, x1 * sin + x2 * cos], axis=-1)


def rel_bucket(dist):
    n = jnp.maximum(dist, 0)
    max_exact = REL_BUCKETS // 2
    nf = jnp.maximum(n, 1).astype(jnp.float32)
    large = max_exact + (jnp.log(nf / max_exact) / math.log(REL_MAX_DIST / max_exact)
                         * (REL_BUCKETS - max_exact)).astype(jnp.int32)
    large = jnp.minimum(large, REL_BUCKETS - 1)
    return jnp.where(n < max_exact, n, large)


_gather_rows = jax.vmap(lambda arr, idx: arr[idx])


def dsa_attention(qa, ka, va, iq, ik, iw, pos, rel_bias):
    B, S = qa.shape[0], qa.shape[1]
    k_sel = min(TOPK_MAX, S // 4)
    n_blk = S // Q_BLOCK
    key_idx = jnp.arange(S)
    scale = HEAD_DIM ** -0.5

    def block(i):
        t0 = i * Q_BLOCK
        q = lax.dynamic_slice_in_dim(qa, t0, Q_BLOCK, axis=1)
        qi = lax.dynamic_slice_in_dim(iq, t0, Q_BLOCK, axis=1)
        wi = lax.dynamic_slice_in_dim(iw, t0, Q_BLOCK, axis=1)
        pq = lax.dynamic_slice_in_dim(pos, t0, Q_BLOCK, axis=1)
        tq = t0 + jnp.arange(Q_BLOCK)
        dots = jnp.einsum('bthd,bsd->bths', qi, ik)
        score = jnp.einsum('bth,bths->bts', wi, jax.nn.relu(dots)).astype(jnp.float32)
        causal = key_idx[None, :] <= tq[:, None]
        score = jnp.where(causal[None], score, -jnp.inf)
        _, sel = lax.top_k(score, k_sel)
        valid = sel <= tq[None, :, None]
        k_g = _gather_rows(ka, sel)
        v_g = _gather_rows(va, sel)
        p_g = _gather_rows(pos, sel)
        logits = jnp.einsum('bthd,btkd->bhtk', q, k_g).astype(jnp.float32) * scale
        bias = rel_bias[rel_bucket(pq[:, :, None] - p_g)]
        logits = logits + jnp.moveaxis(bias, -1, 1).astype(jnp.float32)
        logits = jnp.where(valid[:, None], logits, -jnp.inf)
        p = jax.nn.softmax(logits, axis=-1).astype(v_g.dtype)
        return jnp.einsum('bhtk,btkd->bthd', p, v_g)

    out = lax.map(block, jnp.arange(n_blk))
    return jnp.transpose(out, (1, 0, 2, 3, 4)).reshape(B, S, A_HEADS * HEAD_DIM)


def causal_block_attention(q, k, v, scale):
    B, S, H = q.shape[0], q.shape[1], q.shape[2]
    n_blk = S // Q_BLOCK
    key_idx = jnp.arange(S)

    def block(i):
        t0 = i * Q_BLOCK
        qb = lax.dynamic_slice_in_dim(q, t0, Q_BLOCK, axis=1)
        tq = t0 + jnp.arange(Q_BLOCK)
        logits = jnp.einsum('bthd,bshd->bhts', qb, k).astype(jnp.float32) * scale
        logits = jnp.where((key_idx[None, :] <= tq[:, None])[None, None], logits, -jnp.inf)
        p = jax.nn.softmax(logits, axis=-1).astype(v.dtype)
        return jnp.einsum('bhts,bshd->bthd', p, v)

    out = lax.map(block, jnp.arange(n_blk))
    return jnp.transpose(out, (1, 0, 2, 3, 4)).reshape(B, S, H * v.shape[-1])


def token_mixing(h, mem, pos, cos, sin, rel_bias, w_in, q_norm, kv_norm, w_uq, w_ukv,
                 w_mem_kv, w_branch, w_out):
    B, S, D = h.shape
    splits = [int(c) for c in np.cumsum(IN_SIZES)[:-1]]
    (a_q, a_k, a_v, i_q, i_k, i_w, b_cq, b_ckv, b_kr, c_q, gates) = jnp.split(h @ w_in, splits, axis=-1)

    o_a = dsa_attention(a_q.reshape(B, S, A_HEADS, HEAD_DIM), a_k, a_v,
                        i_q.reshape(B, S, IDX_HEADS, IDX_DIM), i_k,
                        i_w * (IDX_HEADS * IDX_DIM) ** -0.5, pos, rel_bias)

    q = (rms_norm(b_cq, q_norm) @ w_uq).reshape(B, S, B_HEADS, NOPE_DIM + ROPE_DIM)
    q_nope, q_rope = jnp.split(q, [NOPE_DIM], axis=-1)
    q_rope = rope(q_rope, cos[:, :, None, :], sin[:, :, None, :])
    kv = (rms_norm(b_ckv, kv_norm) @ w_ukv).reshape(B, S, B_HEADS, NOPE_DIM + V_DIM)
    k_nope, v_b = jnp.split(kv, [NOPE_DIM], axis=-1)
    k_rope = rope(b_kr, cos, sin)
    k_rope = jnp.broadcast_to(k_rope[:, :, None, :], (B, S, B_HEADS, ROPE_DIM))
    o_b = causal_block_attention(jnp.concatenate([q_nope, q_rope], axis=-1),
                                 jnp.concatenate([k_nope, k_rope], axis=-1),
                                 v_b, (NOPE_DIM + ROPE_DIM) ** -0.5)

    mk, mv = jnp.split((mem @ w_mem_kv).reshape(B, mem.shape[1], 2, C_HEADS, HEAD_DIM), 2, axis=2)
    mk, mv = mk[:, :, 0], mv[:, :, 0]
    logits = jnp.einsum('bthd,bmhd->bhtm', c_q.reshape(B, S, C_HEADS, HEAD_DIM), mk).astype(jnp.float32)
    p = jax.nn.softmax(logits * HEAD_DIM ** -0.5, axis=-1).astype(mv.dtype)
    o_c = jnp.einsum('bhtm,bmhd->bthd', p, mv).reshape(B, S, C_WIDTH)

    y = jnp.stack([o_a @ w_branch[:A_WIDTH],
                   o_b @ w_branch[A_WIDTH:A_WIDTH + B_WIDTH],
                   o_c @ w_branch[A_WIDTH + B_WIDTH:]], axis=-2)
    g = jax.nn.sigmoid(gates.reshape(B, S, N_BRANCH, D))
    return jnp.sum(g * y, axis=-2) @ w_out


def setup_inputs(seed: int = 0) -> dict:
    key = jax.random.key(seed)
    ks = jax.random.split(key, 20)

    def nrm(k, shape, scale):
        return jax.random.normal(k, shape, jnp.float32) * scale

    offset = jax.random.randint(ks[2], (BATCH, 1), 0, 1024, dtype=jnp.int32)
    positions = offset + jnp.arange(SEQ, dtype=jnp.int32)[None, :]
    beta = DEEPNORM_BETA
    return {
        'x': nrm(ks[0], (BATCH, SEQ, D_MODEL), 1.0),
        'mem': nrm(ks[1], (BATCH, MEM_LEN, D_MODEL), 1.0),
        'positions': positions,
        'rel_bias': nrm(ks[3], (REL_BUCKETS, A_HEADS), 0.5),
        'ln_g': 1.0 + nrm(ks[4], (DEPTH, 3, D_MODEL), 0.02),
        'ln_b': nrm(ks[5], (DEPTH, 3, D_MODEL), 0.02),
        'ffn1_up': nrm(ks[6], (DEPTH, D_MODEL, 2 * D_FF), D_MODEL ** -0.5),
        'ffn1_down': nrm(ks[7], (DEPTH, D_FF, D_MODEL), beta * D_FF ** -0.5),
        'w_in': nrm(ks[8], (DEPTH, D_MODEL, IN_COLS), D_MODEL ** -0.5),
        'q_norm': 1.0 + nrm(ks[9], (DEPTH, Q_LORA), 0.02),
        'kv_norm': 1.0 + nrm(ks[10], (DEPTH, KV_LORA), 0.02),
        'w_uq': nrm(ks[11], (DEPTH, Q_LORA, B_HEADS * (NOPE_DIM + ROPE_DIM)), Q_LORA ** -0.5),
        'w_ukv': nrm(ks[12], (DEPTH, KV_LORA, B_HEADS * (NOPE_DIM + V_DIM)), KV_LORA ** -0.5),
        'w_mem_kv': nrm(ks[13], (DEPTH, D_MODEL, 2 * C_WIDTH), D_MODEL ** -0.5),
        'w_branch': nrm(ks[14], (DEPTH, MIX_WIDTH, D_MODEL), beta * HEAD_DIM ** -0.5 * 0.5),
        'w_out': nrm(ks[15], (DEPTH, D_MODEL, D_MODEL), beta * D_MODEL ** -0.5),
        'ffn2_up': nrm(ks[16], (DEPTH, D_MODEL, 2 * D_FF), D_MODEL ** -0.5),
        'ffn2_down': nrm(ks[17], (DEPTH, D_FF, D_MODEL), beta * D_FF ** -0.5),
    }


def reference(x, mem, positions, rel_bias, ln_g, ln_b, ffn1_up, ffn1_down, w_in, q_norm, kv_norm,
              w_uq, w_ukv, w_mem_kv, w_branch, w_out, ffn2_up, ffn2_down):
    inv_freq = ROPE_THETA ** (-jnp.arange(0, ROPE_DIM, 2, dtype=jnp.float32) / ROPE_DIM)
    ang = positions.astype(jnp.float32)[..., None] * inv_freq
    cos, sin = jnp.cos(ang), jnp.sin(ang)
    for l in range(DEPTH):
        x = layer_norm(DEEPNORM_ALPHA * x + 0.5 * swiglu(x, ffn1_up[l], ffn1_down[l]), ln_g[l, 0], ln_b[l, 0])
        mix = token_mixing(x, mem, positions, cos, sin, rel_bias, w_in[l], q_norm[l], kv_norm[l],
                           w_uq[l], w_ukv[l], w_mem_kv[l], w_branch[l], w_out[l])
        x = layer_norm(DEEPNORM_ALPHA * x + mix, ln_g[l, 1], ln_b[l, 1])
        x = layer_norm(DEEPNORM_ALPHA * x + 0.5 * swiglu(x, ffn2_up[l], ffn2_down[l]), ln_g[l, 2], ln_b[l, 2])
    return x
```

```python
import math
from contextlib import ExitStack

import numpy as np
import concourse.bass as bass
import concourse.mybir as mybir
from concourse.bass_utils import run_bass_kernel_spmd

F32 = mybir.dt.float32
BF16 = mybir.dt.bfloat16
I32 = mybir.dt.int32
AF = mybir.ActivationFunctionType
ALU = mybir.AluOpType

D = 2048
S = 2048
DEPTH = 4
DFF = 5632
NFC = DFF // 128
MEM = 256
A_H = 6
I_H = 16
I_D = 64
B_H = 6
C_H = 4
TOPK = 256
ALPHA = (2 * DEPTH) ** 0.25
LN_EPS = 1e-5
RMS_EPS = 1e-6
NEG = -1.0e30
O_AQ, O_AK, O_AV, O_IQ, O_IK, O_IW, O_CQ, O_CKV, O_KR, O_XQ, O_G = 0, 768, 896, 1024, 2048, 2112, 2128, 2640, 3152, 3216, 3728
IN_COLS = 9872


class Sched:
    def __init__(self, nc, es):
        self.nc = nc
        self.eng = {"pe": nc.tensor, "act": nc.scalar, "dve": nc.vector, "pool": nc.gpsimd, "sp": nc.sync}
        self.es = es
        self.semobj = {}
        self.cnt = {}
        for e in ("pe", "act", "dve", "pool"):
            self.semobj[e] = es.enter_context(nc.semaphore("s_" + e))
            self.cnt[e] = 0
        self.seen = {e: {} for e in self.eng}
        self.lw = {}
        self.rd = {}
        self.n_ops = 0
        self.dead = False

    def _deps(self, reads, writes):
        deps = {}
        def add(t):
            if t is not None and deps.get(t[0], 0) < t[1]:
                deps[t[0]] = t[1]
        for r in reads:
            add(self.lw.get(r))
        for w in writes:
            add(self.lw.get(w))
            for t in self.rd.get(w, ()):
                add(t)
        return deps

    def _wait(self, eng, deps):
        e = self.eng[eng]
        seen = self.seen[eng]
        for key, val in deps.items():
            if key == eng and eng == "pe":
                continue
            if seen.get(key, 0) >= val:
                continue
            e.wait_ge(self.semobj[key], val)
            seen[key] = val

    def _commit(self, tok, reads, writes):
        for r in reads:
            self.rd.setdefault(r, []).append(tok)
        for w in writes:
            self.lw[w] = tok
            self.rd[w] = []

    def op(self, eng, fn, reads=(), writes=()):
        if self.dead:
            return
        self._wait(eng, self._deps(reads, writes))
        ins = fn(self.eng[eng])
        self.cnt[eng] += 1
        ins.then_inc(self.semobj[eng], 1)
        self._commit((eng, self.cnt[eng]), reads, writes)
        self.n_ops += 1

    def group(self, eng, fns, reads=(), writes=()):
        if self.dead:
            return
        self._wait(eng, self._deps(reads, writes))
        ins = None
        for fn in fns:
            ins = fn(self.eng[eng])
        self.cnt[eng] += 1
        ins.then_inc(self.semobj[eng], 1)
        self._commit((eng, self.cnt[eng]), reads, writes)
        self.n_ops += len(fns)

    def dma(self, q, out, in_, reads=(), writes=()):
        if self.dead:
            return
        key = "d:" + writes[0]
        if key not in self.semobj:
            self.semobj[key] = self.es.enter_context(self.nc.semaphore("d_" + writes[0]))
            self.cnt[key] = 0
        self._wait(q, self._deps(reads, writes))
        self.eng[q].dma_start(out=out, in_=in_).then_inc(self.semobj[key], 16)
        self.cnt[key] += 16
        self._commit((key, self.cnt[key]), reads, writes)
        self.n_ops += 1

    def barrier(self):
        if self.dead:
            return
        allk = {k: v for k, v in self.cnt.items() if v > 0}
        for e in self.eng:
            self._wait(e, {k: v for k, v in allk.items() if not (k == e and e == "pe")})
        self.lw.clear()
        self.rd.clear()

    def finish(self):
        self.dead = False
        self._wait("sp", {k: v for k, v in self.cnt.items() if v > 0})


class Ctx:
    pass


class StopBuild(Exception):
    pass


def ck(cx, name):
    if cx.stop_at == name:
        cx.sc.barrier()
        cx.sc.dead = True


def bucket_thresholds():
    n = np.arange(0, 4096)
    max_exact = 16
    nf = np.maximum(n, 1).astype(np.float32)
    large = max_exact + (np.log(nf / np.float32(max_exact)) / np.float32(math.log(128 / max_exact))
                         * np.float32(32 - max_exact)).astype(np.int32)
    large = np.minimum(large, 31)
    b = np.where(n < max_exact, n, large)
    return [int(np.min(n[b >= j])) for j in range(1, 32)]


def host_consts():
    c = {}
    c["ident"] = np.eye(128, dtype=np.float32)
    sl = np.arange(128)[:, None, None]
    r = np.arange(2)[None, :, None]
    tl = np.arange(256)[None, None, :]
    c["cm"] = (tl - sl >= 128 * r).astype(np.float32)
    t = np.arange(128)[:, None]
    s = np.arange(128)[None, :]
    c["negtri"] = np.where(s <= t, 0.0, NEG).astype(np.float32)
    m = np.arange(512)[None, :]
    c["dist"] = (m - 128 - np.arange(128)[:, None]).astype(np.float32)
    inv = (10000.0 ** (-np.arange(0, 64, 2, dtype=np.float32) / 64)).astype(np.float32)
    c["invf"] = np.concatenate([inv, inv])[:, None].astype(np.float32)
    c["sgn"] = np.concatenate([-np.ones(32), np.ones(32)])[:, None].astype(np.float32)
    return c


def build_program(n_layers=DEPTH, stop_after=None, stop_at=None, nb=1):
    nc = bass.Bass("TRN2", target_bir_lowering=False)
    T = {}
    def din(name, shape, dt=F32):
        T[name] = nc.dram_tensor(name, list(shape), dt, kind="ExternalInput").ap()
        return T[name]
    din("x", [nb * S, D]); din("mem", [nb * MEM, D]); din("positions", [nb, S], I32); din("rel_bias", [32, A_H])
    NL = n_layers
    din("ln_g", [NL, 3, D]); din("ln_b", [NL, 3, D])
    din("ffn1_up", [NL, D, 2 * DFF]); din("ffn1_down", [NL, DFF, D])
    din("w_in", [NL, D, IN_COLS]); din("q_norm", [NL, 512]); din("kv_norm", [NL, 512])
    din("w_uq", [NL, 512, 1152]); din("w_ukv", [NL, 512, 1536]); din("w_mem_kv", [NL, D, 1024])
    din("w_branch", [NL, D, D]); din("w_out", [NL, D, D])
    din("ffn2_up", [NL, D, 2 * DFF]); din("ffn2_down", [NL, DFF, D])
    hc = host_consts()
    for k, v in hc.items():
        din("c_" + k, v.shape)
    out = nc.dram_tensor("out", [nb * S, D], F32, kind="ExternalOutput").ap()
    X = nc.dram_tensor("Xs", [S, D], F32, kind="Internal").ap()

    with ExitStack() as es:
        sc = Sched(nc, es)
        cx = Ctx()
        cx.nc, cx.sc, cx.T, cx.X, cx.out = nc, sc, T, X, out
        cx.stop_at = stop_at
        cx.ident = es.enter_context(nc.sbuf_tensor("ident", [128, 128], BF16))
        sc.dma("pool", cx.ident[:], T["c_ident"][:, :], writes=["ident"])
        setup_globals(cx, es)
        sc.barrier()
        cx.dram_cache = {}
        cx.bg = []
        cx.cvt = CVT
        cx.Wb = {}
        if CVT:
            for nm in BIGW:
                shp = list(T[nm].shape)
                cx.Wb[nm] = nc.dram_tensor(nm + "_bf", shp, BF16, kind="Internal").ap()
            convert_layer(cx, 0, now=True)
        stages = []
        for l in range(n_layers):
            stages.append(("ffn", l, 1))
            stages.append(("mix", l, 0))
            stages.append(("ffn", l, 2))
        for b in range(nb):
            cx.b = b
            if sc.dead:
                break
            setup_rope(cx, b)
            xin = T["x"][b * S:(b + 1) * S, :]
            xout = out[b * S:(b + 1) * S, :]
            first = True
            for si, (kind, l, which) in enumerate(stages):
                if stop_at == "globals":
                    break
                last = si == len(stages) - 1 or (stop_after is not None and si == stop_after)
                src = xin if first else X
                dst = xout if last else X
                if CVT and b == 0 and kind == "ffn" and which == 1 and l + 1 < n_layers:
                    convert_layer(cx, l + 1, now=False)
                if kind == "ffn":
                    ffn_phase(cx, l, which, src, dst)
                else:
                    mix_phase(cx, l, src, dst)
                first = False
                sc.barrier()
                if last:
                    break
        sc.finish()
    return nc


def ln_tile(cx, y, yreg, gt, bt, st, streg, junk, junkreg, greg="lng", breg="lnb"):
    sc = cx.sc
    s1, nm, s2, rs = st[:, 0:1], st[:, 1:2], st[:, 2:3], st[:, 3:4]
    sc.op("dve", lambda e: e.memset(st[:, 0:4], 0.0), writes=[streg])
    sc.op("act", lambda e: e.activation(out=junk, in_=y, func=AF.Identity, accum_out=s1),
          reads=[yreg], writes=[junkreg, streg])
    sc.op("dve", lambda e: e.tensor_scalar(out=nm, in0=s1, scalar1=-1.0 / D, scalar2=None, op0=ALU.mult),
          reads=[streg], writes=[streg])
    sc.op("act", lambda e: e.activation(out=junk, in_=y, func=AF.Square, bias=nm, scale=1.0, accum_out=s2),
          reads=[yreg, streg], writes=[junkreg, streg])
    sc.op("dve", lambda e: e.tensor_scalar(out=rs, in0=s2, scalar1=1.0 / D, scalar2=LN_EPS, op0=ALU.mult, op1=ALU.add),
          reads=[streg], writes=[streg])
    sc.op("act", lambda e: e.activation(out=rs, in_=rs, func=AF.Sqrt), reads=[streg], writes=[streg])
    sc.op("dve", lambda e: e.reciprocal(out=rs, in_=rs), reads=[streg], writes=[streg])
    sc.op("dve", lambda e: e.tensor_scalar(out=y, in0=y, scalar1=nm, scalar2=rs, op0=ALU.add, op1=ALU.mult),
          reads=[yreg, streg], writes=[yreg])
    sc.op("dve", lambda e: e.tensor_tensor(out=y, in0=y, in1=gt, op=ALU.mult), reads=[yreg, greg], writes=[yreg])
    sc.op("dve", lambda e: e.tensor_tensor(out=y, in0=y, in1=bt, op=ALU.add), reads=[yreg, breg], writes=[yreg])


def load_x_chunk(cx, src, r0, ntile, xt, xb, xT, tp, alpha_scale=True):
    sc = cx.sc
    sc.dma("sp", xt[:, :, :], src[r0:r0 + 128 * ntile, :].rearrange("(t p) d -> p t d", p=128),
           reads=["X%d" % (r0 // 256 + i) for i in range(max(1, ntile // 2))] if src is cx.X else [], writes=["xt"])
    for t in range(ntile):
        b = xb[0]
        breg = "xb0"
        sc.op("dve", lambda e: e.tensor_copy(out=b[:, :], in_=xt[:, t, :]), reads=["xt"], writes=[breg])
        if alpha_scale:
            sc.op("act", lambda e: e.mul(out=xt[:, t, :], in_=xt[:, t, :], mul=ALPHA), reads=["xt", breg], writes=["xt"])
        for h in range(2):
            p = tp[h]
            preg = "tp%d" % h
            sc.group("pe", [(lambda e, kk=kk: e.transpose(out=p[:, kk * 128:(kk + 1) * 128],
                                                          in_=b[:, (h * 8 + kk) * 128:(h * 8 + kk + 1) * 128],
                                                          identity=cx.ident[:, :])) for kk in range(8)],
                     reads=[breg, "ident"], writes=[preg])
            sc.op("act" if h == 0 else "dve",
                  (lambda e: e.activation(out=xT[:, h * 8:(h + 1) * 8, t * 128:(t + 1) * 128],
                                          in_=p[:, :].rearrange("p (k n) -> p k n", k=8), func=AF.Copy)) if h == 0 else
                  (lambda e: e.tensor_copy(out=xT[:, h * 8:(h + 1) * 8, t * 128:(t + 1) * 128],
                                           in_=p[:, :].rearrange("p (k n) -> p k n", k=8))),
                  reads=[preg], writes=["xT"])


def ffn_phase(cx, l, which, src, dst):
    nc, sc, T = cx.nc, cx.sc, cx.T
    wup, wreg = wsrc(cx, "ffn%d_up" % which, l)
    wdn, _ = wsrc(cx, "ffn%d_down" % which, l)
    lni = 0 if which == 1 else 2
    TC = 512
    NT = TC // 128
    with ExitStack() as es:
        def sb(name, shape, dt):
            return es.enter_context(nc.sbuf_tensor("%s_%d_%d_%d" % (name, cx.b, l, which), list(shape), dt))
        def ps(name, shape, dt):
            return es.enter_context(nc.psum_tensor("%s_%d_%d_%d" % (name, cx.b, l, which), list(shape), dt))
        xt = sb("f_xt", [128, NT, D], F32)
        xb = [sb("f_xb0", [128, D], BF16)]
        xb.append(xb[0])
        xT = sb("f_xT", [128, 16, TC], BF16)
        actT = sb("f_actT", [128, NFC, TC], BF16)
        wu = [sb("f_wu%d" % i, [128, 16, 512], BF16) for i in range(2)]
        wd = [sb("f_wd%d" % i, [128, NFC, 256], BF16) for i in range(2)]
        gt = sb("f_gt", [128, D], F32)
        bt = sb("f_bt", [128, D], F32)
        sg = [sb("f_sg%d" % i, [128, TC], BF16) for i in range(2)]
        st = sb("f_st", [128, 8], F32)
        junk = xb[0]
        tp = [ps("f_tp%d" % i, [128, 1024], BF16) for i in range(2)]
        pG = [ps("f_pG%d" % i, [128, 512], F32) for i in range(2)]
        pU = [ps("f_pU%d" % i, [128, 512], F32) for i in range(2)]
        pD = [ps("f_pD%d" % i, [128, 256], F32) for i in range(2)]

        sc.dma("sp", gt[:, :], T["ln_g"][l, lni:lni + 1, :].partition_broadcast(128), writes=["lng"])
        sc.dma("sp", bt[:, :], T["ln_b"][l, lni:lni + 1, :].partition_broadcast(128), writes=["lnb"])

        def load_wu(j):
            b = wu[j % 2]
            reg = "wu%d" % (j % 2)
            sc.dma("pool", b[:, :, 0:256], wup[:, j * 256:(j + 1) * 256].rearrange("(kc p) n -> p kc n", p=128), reads=wreg, writes=[reg])
            sc.dma("pool", b[:, :, 256:512], wup[:, DFF + j * 256:DFF + (j + 1) * 256].rearrange("(kc p) n -> p kc n", p=128), reads=wreg, writes=[reg])
            bg_step(cx)

        def load_wd(g):
            sc.dma("pool", wd[g % 2][:, :, :], wdn[:, g * 256:(g + 1) * 256].rearrange("(fc p) n -> p fc n", p=128),
                   reads=wreg, writes=["wd%d" % (g % 2)])
            bg_step(cx)

        for c in range(S // TC):
            r0 = c * TC
            load_x_chunk(cx, src, r0, NT, xt, xb, xT, tp)
            load_wu(0); load_wu(1)
            for j in range(NFC // 2):
                b = wu[j % 2]
                reg = "wu%d" % (j % 2)
                for jj in range(2):
                    f = 2 * j + jj
                    q = f % 2
                    sc.group("pe", [(lambda e, kc=kc: e.matmul(pG[q][:, :], b[:, kc, jj * 128:(jj + 1) * 128], xT[:, kc, :],
                                                               start=(kc == 0), stop=(kc == 15))) for kc in range(16)],
                             reads=[reg, "xT"], writes=["pG%d" % q])
                    sc.group("pe", [(lambda e, kc=kc: e.matmul(pU[q][:, :], b[:, kc, 256 + jj * 128:256 + (jj + 1) * 128], xT[:, kc, :],
                                                               start=(kc == 0), stop=(kc == 15))) for kc in range(16)],
                             reads=[reg, "xT"], writes=["pU%d" % q])
                    sc.op("act", lambda e: e.activation(out=sg[q][:, :], in_=pG[q][:, :], func=AF.Silu),
                          reads=["pG%d" % q], writes=["sg%d" % q])
                    sc.op("dve", lambda e: e.tensor_tensor(out=actT[:, f, :], in0=sg[q][:, :], in1=pU[q][:, :], op=ALU.mult),
                          reads=["sg%d" % q, "pU%d" % q], writes=["actT"])
                if j + 2 < NFC // 2:
                    load_wu(j + 2)
            load_wd(0); load_wd(1)
            for g in range(D // 256):
                b = wd[g % 2]
                reg = "wd%d" % (g % 2)
                for t in range(NT):
                    q = (g * NT + t) % 2
                    sc.group("pe", [(lambda e, f=f: e.matmul(pD[q][:, :], actT[:, f, t * 128:(t + 1) * 128], b[:, f, :],
                                                             start=(f == 0), stop=(f == NFC - 1))) for f in range(NFC)],
                             reads=[reg, "actT"], writes=["pD%d" % q])
                    sc.op("dve", lambda e: e.scalar_tensor_tensor(out=xt[:, t, g * 256:(g + 1) * 256], in0=pD[q][:, :], scalar=0.5,
                                                                  in1=xt[:, t, g * 256:(g + 1) * 256], op0=ALU.mult, op1=ALU.add),
                          reads=["pD%d" % q, "xt"], writes=["xt"])
                if g + 2 < D // 256:
                    load_wd(g + 2)
            for t in range(NT):
                ln_tile(cx, xt[:, t, :], "xt", gt[:, :], bt[:, :], st, "st", junk[:, :], "xb0")
            sc.dma("sp", dst[r0:r0 + TC, :].rearrange("(t p) d -> p t d", p=128), xt[:, :, :],
                   reads=["xt"], writes=["X%d" % (r0 // 256), "X%d" % (r0 // 256 + 1)])


CVT = True
BIGW = ["ffn1_up", "ffn1_down", "w_in", "w_uq", "w_ukv", "w_mem_kv", "w_branch", "w_out", "ffn2_up", "ffn2_down"]


def convert_layer(cx, l, now):
    T, sc = cx.T, cx.sc
    jobs = []
    for nm in BIGW:
        src = T[nm][l]
        dstw = cx.Wb[nm][l]
        R = src.shape[0]
        for r0 in range(0, R, 128):
            jobs.append((dstw[r0:r0 + 128, :], src[r0:r0 + 128, :], l))
    if now:
        for (o, i, ll) in jobs:
            sc.dma("pool", o, i, writes=["cv_L%d" % ll])
    else:
        cx.bg.extend(jobs)


def bg_step(cx, n=1):
    for _ in range(n):
        if cx.bg:
            o, i, ll = cx.bg.pop(0)
            cx.sc.dma("pool", o, i, writes=["cv_L%d" % ll])


def wsrc(cx, nm, l):
    if cx.cvt:
        return cx.Wb[nm][l], ["cv_L%d" % l]
    return cx.T[nm][l], []


def setup_globals(cx, es):
    nc, sc, T = cx.nc, cx.sc, cx.T
    def gsb(name, shape, dt):
        return es.enter_context(nc.sbuf_tensor(name, list(shape), dt))
    cx.ones = gsb("ones", [128, 128], BF16)
    sc.op("dve", lambda e: e.memset(cx.ones[:, :], 1.0), writes=["ones"])
    cx.cm = gsb("cm", [128, 2, 256], BF16)
    sc.dma("pool", cx.cm[:, :, :], T["c_cm"][:, :, :], writes=["cm"])
    cx.negtri = gsb("negtri", [128, 128], F32)
    sc.dma("sp", cx.negtri[:, :], T["c_negtri"][:, :], writes=["negtri"])
    cx.thrlow = gsb("thrlow", [128, 1], F32)
    sc.op("dve", lambda e: e.memset(cx.thrlow[:, :], -1.0e29), writes=["thrlow"])
    cx.CS = gsb("ropeCS", [64, S], BF16)
    cx.SN = gsb("ropeSN", [64, S], BF16)
    cx.TH = gsb("ebias", [128, A_H, 512], BF16)
    cx.rb31 = gsb("rb31", [128, A_H], F32)
    with ExitStack() as ts:
        def tsb(name, shape, dt):
            return ts.enter_context(nc.sbuf_tensor(name, list(shape), dt))
        RB = tsb("g_rb", [128, 32, A_H], F32)
        dRB = tsb("g_drb", [128, 32, A_H], F32)
        dist = tsb("g_dist", [128, 512], F32)
        G = tsb("g_G", [128, 512], F32)
        G2 = tsb("g_G2", [128, 512], F32)
        sc.dma("sp", RB[:, :, :], T["rel_bias"].rearrange("(o j) h -> o j h", o=1).partition_broadcast(128), writes=["RB"])
        sc.dma("sp", dist[:, :], T["c_dist"][:, :], writes=["dist"])
        sc.op("dve", lambda e: e.tensor_tensor(out=dRB[:, 1:32, :], in0=RB[:, 1:32, :], in1=RB[:, 0:31, :], op=ALU.subtract),
              reads=["RB"], writes=["dRB"])
        sc.op("dve", lambda e: e.tensor_copy(out=cx.rb31[:, :], in_=RB[:, 31, :]), reads=["RB"], writes=["rb31"])
        thr = bucket_thresholds()
        for h in range(A_H):
            for j in range(1, 32):
                if j == 1:
                    sc.op("dve", lambda e: e.tensor_scalar(out=G[:, :], in0=dist[:, :], scalar1=float(thr[j - 1]) - 0.5,
                                                           scalar2=dRB[:, j, h:h + 1], op0=ALU.is_ge, op1=ALU.mult),
                          reads=["dist", "dRB"], writes=["G"])
                else:
                    sc.op("dve", lambda e: e.tensor_scalar(out=G2[:, :], in0=dist[:, :], scalar1=float(thr[j - 1]) - 0.5,
                                                           scalar2=dRB[:, j, h:h + 1], op0=ALU.is_ge, op1=ALU.mult),
                          reads=["dist", "dRB"], writes=["G2"])
                    sc.op("dve", lambda e: e.tensor_tensor(out=G[:, :], in0=G[:, :], in1=G2[:, :], op=ALU.add),
                          reads=["G", "G2"], writes=["G"])
            sc.op("act", lambda e: e.activation(out=cx.TH[:, h, :], in_=G[:, :], func=AF.Exp, bias=RB[:, 0, h:h + 1], scale=1.0),
                  reads=["G", "RB"], writes=["ebias"])
        sc.barrier()


def setup_rope(cx, b):
    nc, sc, T = cx.nc, cx.sc, cx.T
    with ExitStack() as ts:
        def tsb(name, shape, dt):
            return ts.enter_context(nc.sbuf_tensor("%s_b%d" % (name, b), list(shape), dt))
        posi = tsb("g_posi", [64, S], I32)
        ang = tsb("g_ang", [64, S], F32)
        tmp = tsb("g_tmp", [64, S], F32)
        invf = tsb("g_invf", [64, 1], F32)
        sgn = tsb("g_sgn", [64, 1], F32)
        sc.dma("sp", posi[:, :], T["positions"][b:b + 1, :].partition_broadcast(64), writes=["posi"])
        sc.dma("sp", invf[:, :], T["c_invf"][:, :], writes=["invf"])
        sc.dma("sp", sgn[:, :], T["c_sgn"][:, :], writes=["sgn"])
        sc.op("dve", lambda e: e.tensor_copy(out=ang[:, :], in_=posi[:, :]), reads=["posi"], writes=["ang"])
        sc.op("dve", lambda e: e.tensor_scalar(out=ang[:, :], in0=ang[:, :], scalar1=invf[:, 0:1], scalar2=None, op0=ALU.mult),
              reads=["ang", "invf"], writes=["ang"])
        TWO_PI = 2 * math.pi
        def sin_of(shift, out_ap, out_reg, post_sgn):
            sc.op("dve", lambda e: e.tensor_scalar(out=tmp[:, :], in0=ang[:, :], scalar1=shift, scalar2=1.0 / TWO_PI, op0=ALU.add, op1=ALU.mult),
                  reads=["ang"], writes=["tmp"])
            sc.op("dve", lambda e: e.tensor_copy(out=posi[:, :], in_=tmp[:, :]), reads=["tmp"], writes=["posi"])
            sc.op("dve", lambda e: e.tensor_copy(out=tmp[:, :], in_=posi[:, :]), reads=["posi"], writes=["tmp"])
            sc.op("dve", lambda e: e.scalar_tensor_tensor(out=tmp[:, :], in0=tmp[:, :], scalar=-TWO_PI, in1=ang[:, :], op0=ALU.mult, op1=ALU.add),
                  reads=["tmp", "ang"], writes=["tmp"])
            sc.op("dve", lambda e: e.tensor_scalar(out=tmp[:, :], in0=tmp[:, :], scalar1=shift, scalar2=None, op0=ALU.add),
                  reads=["tmp"], writes=["tmp"])
            sc.op("dve", lambda e: e.tensor_scalar(out=tmp2[:, :], in0=tmp[:, :], scalar1=math.pi, scalar2=-TWO_PI, op0=ALU.is_gt, op1=ALU.mult),
                  reads=["tmp"], writes=["tmp2"])
            sc.op("dve", lambda e: e.tensor_tensor(out=tmp[:, :], in0=tmp[:, :], in1=tmp2[:, :], op=ALU.add), reads=["tmp", "tmp2"], writes=["tmp"])
            sc.op("dve", lambda e: e.tensor_scalar(out=tmp2[:, :], in0=tmp[:, :], scalar1=-math.pi, scalar2=TWO_PI, op0=ALU.is_lt, op1=ALU.mult),
                  reads=["tmp"], writes=["tmp2"])
            sc.op("dve", lambda e: e.tensor_tensor(out=tmp[:, :], in0=tmp[:, :], in1=tmp2[:, :], op=ALU.add), reads=["tmp", "tmp2"], writes=["tmp"])
            if post_sgn:
                sc.op("act", lambda e: e.activation(out=tmp[:, :], in_=tmp[:, :], func=AF.Sin), reads=["tmp"], writes=["tmp"])
                sc.op("dve", lambda e: e.tensor_scalar(out=out_ap, in0=tmp[:, :], scalar1=sgn[:, 0:1], scalar2=None, op0=ALU.mult),
                      reads=["tmp", "sgn"], writes=[out_reg])
            else:
                sc.op("act", lambda e: e.activation(out=out_ap, in_=tmp[:, :], func=AF.Sin), reads=["tmp"], writes=[out_reg])
        tmp2 = tsb("g_tmp2", [64, S], F32)
        sin_of(0.0, cx.SN[:, :], "ropeSN", True)
        sin_of(0.5 * math.pi, cx.CS[:, :], "ropeCS", False)
        sc.barrier()


def mix_phase(cx, l, src, dst):
    nc, sc, T = cx.nc, cx.sc, cx.T
    TC = 256
    NT = 2
    NCH = S // TC
    if "KN" not in cx.dram_cache:
        cx.dram_cache["KN"] = nc.dram_tensor("KNs", [B_H, 128, S], BF16, kind="Internal").ap()
        cx.dram_cache["VB"] = nc.dram_tensor("VBs", [B_H, S, 128], BF16, kind="Internal").ap()
    KN = cx.dram_cache["KN"]
    VBd = cx.dram_cache["VB"]
    uid = [0]
    def alloc(stack, name, shape, dt, psum=False):
        uid[0] += 1
        nm = "m%d_%d_%s_%d" % (cx.b, l, name, uid[0])
        if psum:
            return stack.enter_context(nc.psum_tensor(nm, list(shape), dt))
        return stack.enter_context(nc.sbuf_tensor(nm, list(shape), dt))
    with ExitStack() as es:
        sb = lambda name, shape, dt: alloc(es, name, shape, dt)
        ps = lambda name, shape, dt: alloc(es, name, shape, dt, True)
        KA_T = sb("KA_T", [128, S], BF16)
        VA = sb("VA", [128, S // 128, 128], BF16)
        IK_T2 = sb("IK_T2", [128, S], BF16)
        KR_T = sb("KR_T", [64, S], BF16)
        MK_T = sb("MK_T", [128, C_H, MEM], BF16)
        MV = sb("MV", [128, 2, 512], BF16)
        gq = sb("gq", [128, 4], F32)
        gkv = sb("gkv", [128, 4], F32)
        ones_f = sb("ones_f", [128, 128], F32)
        st = sb("st", [128, 8], F32)
        mx = sb("mx", [128, 16], F32)
        xt = sb("xt", [128, NT, D], F32)
        xT = sb("xT", [128, 16, TC], BF16)
        wbufs = [sb("wb%d" % i, [128, 16, 512], BF16) for i in range(2)]
        QA_T = sb("QA_T", [128, A_H, TC], BF16)
        IQ_T = sb("IQ_T", [128, 8, TC], BF16)
        IW = sb("IW", [128, NT, 16], F32)
        QN_T = sb("QN_T", [128, B_H, TC], BF16)
        QR_T = sb("QR_T", [64, B_H, TC], BF16)
        XQ_T = sb("XQ_T", [128, C_H, TC], BF16)
        OT = sb("OT", [128, 16, TC], BF16)
        MT = sb("MTk", [128, S // 128, TC], BF16)
        tp = ps("tp", [128, 1024], BF16)
        pj = [ps("pj%d" % i, [128, 512], F32) for i in range(2)]
        sT = [ps("sT%d" % i, [128, 512], F32) for i in range(2)]
        oT = ps("oT", [128, 512], F32)
        rsum = ps("rs", [128, 512], F32)
        pr = ps("pr", [128, 512], F32)

        sc.op("dve", lambda e: e.memset(ones_f[:, :], 1.0), writes=["ones_f"])
        for k in range(4):
            sc.dma("sp", gq[:, k:k + 1], T["q_norm"][l, k * 128:(k + 1) * 128].rearrange("(p o) -> p o", o=1), writes=["gq"])
            sc.dma("sp", gkv[:, k:k + 1], T["kv_norm"][l, k * 128:(k + 1) * 128].rearrange("(p o) -> p o", o=1), writes=["gkv"])

        if cx.stop_at == "mixalloc":
            sc.barrier()
            return
        wcount = [0]
        def wload(ranges, wname="w_in", kcn=16):
            i = wcount[0] % len(wbufs)
            wcount[0] += 1
            reg = "wb%d" % i
            wsrc_, wreg = wsrc(cx, wname, l)
            bview = wbufs[i][:, :, :] if kcn == 16 else wbufs[i][:, :, :].rearrange("p a b -> p (a b)").rearrange("p (k n) -> p k n", k=kcn)
            o = 0
            for (c0, c1) in ranges:
                sc.dma("pool", bview[:, :, o:o + (c1 - c0)], wsrc_[:, c0:c1].rearrange("(kc p) n -> p kc n", p=128), reads=wreg, writes=[reg])
                o += c1 - c0
            bg_step(cx)
            return bview, reg

        def evac(eng, out, in_, reads, writes):
            if eng == "act":
                sc.op("act", lambda e: e.activation(out=out, in_=in_, func=AF.Copy), reads=reads, writes=writes)
            else:
                sc.op("dve", lambda e: e.tensor_copy(out=out, in_=in_), reads=reads, writes=writes)

        pjc = [0]
        def proj_fm(bv, reg, col0, ncols, rhs, rhs_regs, kcn=16):
            q = pjc[0] % 2
            pjc[0] += 1
            p = pj[q]
            n = rhs(0).shape[-1]
            sc.group("pe", [(lambda e, kc=kc: e.matmul(p[0:ncols, 0:n], bv[:, kc, col0:col0 + ncols], rhs(kc),
                                                       start=(kc == 0), stop=(kc == kcn - 1))) for kc in range(kcn)],
                     reads=[reg] + rhs_regs, writes=["pj%d" % q])
            return p[0:ncols, 0:n], "pj%d" % q

        def proj_tm(bv, reg, col0, ncols, lhs, lhs_regs, kcn=16):
            q = pjc[0] % 2
            pjc[0] += 1
            p = pj[q]
            sc.group("pe", [(lambda e, kc=kc: e.matmul(p[:, 0:ncols], lhs(kc), bv[:, kc, col0:col0 + ncols],
                                                       start=(kc == 0), stop=(kc == kcn - 1))) for kc in range(kcn)],
                     reads=[reg] + lhs_regs, writes=["pj%d" % q])
            return p[:, 0:ncols], "pj%d" % q

        def transpose_rows(srcb, sreg, dstT, dreg, t):
            for h in range(2):
                sc.group("pe", [(lambda e, kk=kk: e.transpose(out=tp[:, kk * 128:(kk + 1) * 128],
                                                              in_=srcb[:, (h * 8 + kk) * 128:(h * 8 + kk + 1) * 128],
                                                              identity=cx.ident[:, :])) for kk in range(8)],
                         reads=[sreg, "ident"], writes=["tp"])
                evac("act" if h == 0 else "dve", dstT[:, h * 8:(h + 1) * 8, t * 128:(t + 1) * 128],
                     tp[:, :].rearrange("p (k n) -> p k n", k=8), ["tp"], [dreg])

        with ExitStack() as ms:
            memt = alloc(ms, "memt", [128, 2, D], F32)
            memb = alloc(ms, "memb", [128, D], BF16)
            memT = alloc(ms, "memT", [128, 16, MEM], BF16)
            sc.dma("sp", memt[:, :, :], T["mem"][cx.b * MEM:(cx.b + 1) * MEM, :].rearrange("(t p) d -> p t d", p=128), writes=["memt"])
            for t in range(2):
                sc.op("dve", lambda e: e.tensor_copy(out=memb[:, :], in_=memt[:, t, :]), reads=["memt"], writes=["memb"])
                transpose_rows(memb, "memb", memT, "memT", t)
            if cx.stop_at == "memT":
                sc.barrier()
                return
            bv, reg = wload([(0, 512)], "w_mem_kv")
            if cx.stop_at == "memW":
                sc.barrier()
                return
            for h in range(C_H):
                p, preg = proj_fm(bv, reg, h * 128, 128, lambda kc: memT[:, kc, :], ["memT"])
                evac("act" if h % 2 else "dve", MK_T[:, h, :], p, [preg], ["MK_T"])
            if cx.stop_at == "mk":
                sc.barrier()
                return
            bv, reg = wload([(512, 1024)], "w_mem_kv")
            for t in range(2):
                p, preg = proj_tm(bv, reg, 0, 512, lambda kc: memT[:, kc, t * 128:(t + 1) * 128], ["memT"])
                evac("act" if t % 2 else "dve", MV[:, t, :], p, [preg], ["MV"])
            sc.barrier()
        if cx.stop_at == "memkv":
            return

        acnt = [0]

        for c in range(NCH):
            r0 = c * TC
            nkt = 2 * c + 2
            xrhs = lambda kc: xT[:, kc, :]
            with ExitStack() as P:
                xb = alloc(P, "xb", [128, D], BF16)
                CN = [alloc(P, "CQN", [128, 4, TC], BF16), alloc(P, "CKVN", [128, 4, TC], BF16)]
                cf = alloc(P, "cf", [128, 4, TC], F32)
                sq = alloc(P, "sq", [128, 4, TC], BF16)
                rstd = alloc(P, "rstd", [128, TC], F32)
                rt = [alloc(P, "rt%d" % i, [64, TC], F32) for i in range(2)]
                vst = alloc(P, "vst", [128, 768], BF16)

                sc.dma("sp", xt[:, :, :], src[r0:r0 + TC, :].rearrange("(t p) d -> p t d", p=128),
                       reads=["X%d" % c] if src is cx.X else [], writes=["xt"])
                for t in range(NT):
                    sc.op("dve", lambda e: e.tensor_copy(out=xb[:, :], in_=xt[:, t, :]), reads=["xt"], writes=["xb"])
                    sc.op("act", lambda e: e.mul(out=xt[:, t, :], in_=xt[:, t, :], mul=ALPHA), reads=["xt"], writes=["xt"])
                    transpose_rows(xb, "xb", xT, "xT", t)

                def rope_combine(p1, r1, p2, r2, out, outreg):
                    sc.op("dve", lambda e: e.tensor_tensor(out=rt[0][:, :], in0=p1, in1=cx.CS[:, r0:r0 + TC], op=ALU.mult),
                          reads=[r1, "ropeCS"], writes=["rt0"])
                    sc.op("dve", lambda e: e.tensor_tensor(out=rt[1][:, :], in0=p2, in1=cx.SN[:, r0:r0 + TC], op=ALU.mult),
                          reads=[r2, "ropeSN"], writes=["rt1"])
                    sc.op("dve", lambda e: e.tensor_tensor(out=out, in0=rt[0][:, :], in1=rt[1][:, :], op=ALU.add),
                          reads=["rt0", "rt1"], writes=[outreg])

                ck(cx, "P0")
                bv, reg = wload([(0, 512)])
                for h in range(4):
                    p, preg = proj_fm(bv, reg, h * 128, 128, xrhs, ["xT"])
                    evac("act" if h % 2 else "dve", QA_T[:, h, :], p, [preg], ["QA_T"])
                bv, reg = wload([(512, 1024)])
                for h in range(4, 6):
                    p, preg = proj_fm(bv, reg, (h - 4) * 128, 128, xrhs, ["xT"])
                    evac("act" if h % 2 else "dve", QA_T[:, h, :], p, [preg], ["QA_T"])
                p, preg = proj_fm(bv, reg, 256, 128, xrhs, ["xT"])
                evac("act", KA_T[:, r0:r0 + TC], p, [preg], ["KA_T"])
                for t in range(NT):
                    p, preg = proj_tm(bv, reg, 384, 128, lambda kc: xT[:, kc, t * 128:(t + 1) * 128], ["xT"])
                    evac("dve", VA[:, 2 * c + t, :], p, [preg], ["VA"])
                ck(cx, "P1")
                for half in range(2):
                    bv, reg = wload([(O_IQ + half * 512, O_IQ + (half + 1) * 512)])
                    for k in range(4):
                        p, preg = proj_fm(bv, reg, k * 128, 128, xrhs, ["xT"])
                        evac("act" if k % 2 else "dve", IQ_T[:, half * 4 + k, :], p, [preg], ["IQ_T"])
                ck(cx, "P3")
                bv, reg = wload([(O_IK, O_IK + 64), (O_IK, O_IK + 64), (O_KR, O_KR + 64), (O_KR + 32, O_KR + 64), (O_KR, O_KR + 32),
                                 (O_IW, O_IW + 16)])
                p, preg = proj_fm(bv, reg, 0, 128, xrhs, ["xT"])
                evac("act", IK_T2[:, r0:r0 + TC], p, [preg], ["IK_T2"])
                p1, r1 = proj_fm(bv, reg, 128, 64, xrhs, ["xT"])
                p2, r2 = proj_fm(bv, reg, 192, 64, xrhs, ["xT"])
                rope_combine(p1, r1, p2, r2, KR_T[:, r0:r0 + TC], "KR_T")
                for t in range(NT):
                    p, preg = proj_tm(bv, reg, 256, 16, lambda kc: xT[:, kc, t * 128:(t + 1) * 128], ["xT"])
                    evac("dve", IW[:, t, :], p, [preg], ["IW"])
                ck(cx, "P4")
                for which, (o0, gcol, greg_) in enumerate(((O_CQ, gq, "gq"), (O_CKV, gkv, "gkv"))):
                    bv, reg = wload([(o0, o0 + 512)])
                    for k in range(4):
                        p, preg = proj_fm(bv, reg, k * 128, 128, xrhs, ["xT"])
                        sc.op("dve", lambda e: e.tensor_copy(out=cf[:, k, :], in_=p), reads=[preg], writes=["cf"])
                        sc.op("dve", lambda e: e.tensor_tensor(out=sq[:, k, :], in0=cf[:, k, :], in1=cf[:, k, :], op=ALU.mult),
                              reads=["cf"], writes=["sq"])
                    ck(cx, "P5a")
                    sc.group("pe", [(lambda e, k=k: e.matmul(pr[:, 0:TC], cx.ones[:, :], sq[:, k, :], start=(k == 0), stop=(k == 3))) for k in range(4)],
                             reads=["ones", "sq"], writes=["pr"])
                    sc.op("dve", lambda e: e.tensor_scalar(out=rstd[:, :], in0=pr[:, 0:TC], scalar1=1.0 / 512, scalar2=RMS_EPS,
                                                           op0=ALU.mult, op1=ALU.add), reads=["pr"], writes=["rstd"])
                    ck(cx, "P5b")
                    sc.op("act", lambda e: e.activation(out=rstd[:, :], in_=rstd[:, :], func=AF.Sqrt), reads=["rstd"], writes=["rstd"])
                    sc.op("dve", lambda e: e.reciprocal(out=rstd[:, :], in_=rstd[:, :]), reads=["rstd"], writes=["rstd"])
                    ck(cx, "P5c")
                    for k in range(4):
                        sc.op("dve", lambda e: e.scalar_tensor_tensor(out=CN[which][:, k, :], in0=cf[:, k, :], scalar=gcol[:, k:k + 1],
                                                                      in1=rstd[:, :], op0=ALU.mult, op1=ALU.mult),
                              reads=["cf", "rstd", greg_], writes=["CN%d" % which])
                ck(cx, "P6")
                bv, reg = wload([(O_XQ, O_XQ + 512)])
                for h in range(C_H):
                    p, preg = proj_fm(bv, reg, h * 128, 128, xrhs, ["xT"])
                    evac("act" if h % 2 else "dve", XQ_T[:, h, :], p, [preg], ["XQ_T"])
                rngs = [(0, 1152)]
                for h in range(B_H):
                    rngs += [(h * 192 + 160, h * 192 + 192), (h * 192 + 128, h * 192 + 160)]
                bv, reg = wload(rngs, "w_uq", kcn=4)
                qrhs = lambda kc: CN[0][:, kc, :]
                for h in range(B_H):
                    p, preg = proj_fm(bv, reg, h * 192, 128, qrhs, ["CN0"], kcn=4)
                    evac("act", QN_T[:, h, :], p, [preg], ["QN_T"])
                    p1, r1 = proj_fm(bv, reg, h * 192 + 128, 64, qrhs, ["CN0"], kcn=4)
                    p2, r2 = proj_fm(bv, reg, 1152 + h * 64, 64, qrhs, ["CN0"], kcn=4)
                    rope_combine(p1, r1, p2, r2, QR_T[:, h, :], "QR_T")
                ck(cx, "P8")
                rngs = [(h * 256, h * 256 + 128) for h in range(B_H)] + [(h * 256 + 128, h * 256 + 256) for h in range(B_H)]
                bv, reg = wload(rngs, "w_ukv", kcn=4)
                krhs = lambda kc: CN[1][:, kc, :]
                for h in range(B_H):
                    p, preg = proj_fm(bv, reg, h * 128, 128, krhs, ["CN1"], kcn=4)
                    hv = (h % 2) * TC
                    evac("act" if h % 2 else "dve", vst[:, hv:hv + TC], p, [preg], ["vstk%d" % (h % 2)])
                    sc.dma("sp", KN[h, :, r0:r0 + TC], vst[:, hv:hv + TC], reads=["vstk%d" % (h % 2)], writes=["KN%d" % h])
                for t in range(NT):
                    p, preg = proj_tm(bv, reg, 768, 512, lambda kc: CN[1][:, kc, t * 128:(t + 1) * 128], ["CN1"], kcn=4)
                    evac("act", vst[:, 0:512], p, [preg], ["vstk0", "vstk1"])
                    p, preg = proj_tm(bv, reg, 768 + 512, 256, lambda kc: CN[1][:, kc, t * 128:(t + 1) * 128], ["CN1"], kcn=4)
                    evac("dve", vst[:, 512:768], p, [preg], ["vstv"])
                    for h in range(B_H):
                        sc.dma("sp", VBd[h, r0 + t * 128:r0 + (t + 1) * 128, :], vst[:, h * 128:(h + 1) * 128],
                               reads=["vstk0", "vstk1", "vstv"], writes=["VB%d" % h])
                sc.barrier()
            if cx.stop_at == "P":
                return

            with ExitStack() as I_:
                acc = alloc(I_, "acc", [128, S], F32)
                wk = alloc(I_, "wk", [128, S], F32)
                Mk = alloc(I_, "Mk", [128, S], BF16)
                rl = [alloc(I_, "rl%d" % i, [128, 512], F32) for i in range(2)]
                for tt in range(NT):
                    i = 2 * c + tt
                    L = (i + 1) * 128
                    for hh in range(I_H):
                        pb = (hh % 2) * 64
                        for k4 in range((L + 511) // 512):
                            n = min(512, L - k4 * 512)
                            q = pjc[0] % 2
                            pjc[0] += 1
                            sc.group("pe", [lambda e: e.matmul(pj[q][:, 0:n], IQ_T[pb:pb + 64, hh // 2, tt * 128:(tt + 1) * 128],
                                                               IK_T2[pb:pb + 64, k4 * 512:k4 * 512 + n], start=True, stop=True)],
                                     reads=["IQ_T", "IK_T2"], writes=["pj%d" % q])
                            sc.op("act", lambda e: e.activation(out=rl[q][:, 0:n], in_=pj[q][:, 0:n], func=AF.Relu),
                                  reads=["pj%d" % q], writes=["rl%d" % q])
                            a = acc[:, k4 * 512:k4 * 512 + n]
                            if hh == 0:
                                sc.op("dve", lambda e: e.tensor_scalar(out=a, in0=rl[q][:, 0:n], scalar1=IW[:, tt, hh:hh + 1], scalar2=None,
                                                                       op0=ALU.mult), reads=["rl%d" % q, "IW"], writes=["acc"])
                            else:
                                sc.op("dve", lambda e: e.scalar_tensor_tensor(out=a, in0=rl[q][:, 0:n], scalar=IW[:, tt, hh:hh + 1], in1=a,
                                                                              op0=ALU.mult, op1=ALU.add),
                                      reads=["rl%d" % q, "IW", "acc"], writes=["acc"])
                    sc.op("dve", lambda e: e.tensor_tensor(out=acc[:, L - 128:L], in0=acc[:, L - 128:L], in1=cx.negtri[:, :], op=ALU.add),
                          reads=["acc", "negtri"], writes=["acc"])
                    if i >= 2:
                        for r in range(32):
                            srcv = acc if r == 0 else wk
                            sreg = "acc" if r == 0 else "wk"
                            sc.op("dve", lambda e: e.max(out=mx[:, 0:8], in_=srcv[:, 0:L]), reads=[sreg], writes=["mx"])
                            if r < 31:
                                sc.op("dve", lambda e: e.match_replace(out=wk[:, 0:L], in_to_replace=mx[:, 0:8], in_values=srcv[:, 0:L],
                                                                       imm_value=NEG), reads=[sreg, "mx"], writes=["wk"])
                        sc.op("dve", lambda e: e.tensor_reduce(out=mx[:, 8:9], in_=mx[:, 0:8], axis=mybir.AxisListType.X, op=ALU.min),
                              reads=["mx"], writes=["mx"])
                        thr_ap, thr_reg = mx[:, 8:9], "mx"
                    else:
                        thr_ap, thr_reg = cx.thrlow[:, 0:1], "thrlow"
                    sc.op("dve", lambda e: e.tensor_scalar(out=Mk[:, 0:L], in0=acc[:, 0:L], scalar1=thr_ap, scalar2=None, op0=ALU.is_ge),
                          reads=["acc", thr_reg], writes=["Mk"])
                    for j0 in range(0, i + 1, 8):
                        nj = min(8, i + 1 - j0)
                        sc.group("pe", [(lambda e, jj=jj: e.transpose(out=tp[:, jj * 128:(jj + 1) * 128],
                                                                      in_=Mk[:, (j0 + jj) * 128:(j0 + jj + 1) * 128],
                                                                      identity=cx.ident[:, :])) for jj in range(nj)],
                                 reads=["Mk", "ident"], writes=["tp"])
                        evac("act", MT[:, j0:j0 + nj, tt * 128:(tt + 1) * 128], tp[:, 0:nj * 128].rearrange("p (k n) -> p k n", k=nj),
                             ["tp"], ["MT"])
                    if tt == 0:
                        sc.op("dve", lambda e: e.memset(MT[:, 2 * c + 1, 0:128], 0.0), writes=["MT"])
                sc.barrier()
            if cx.stop_at == "I":
                return

            with ExitStack() as A_:
                pT = [alloc(A_, "pT%d" % i, [128, TC], BF16) for i in range(2)]
                rc = alloc(A_, "rc", [128, TC], F32)
                KNh = alloc(A_, "KNh", [128, S], BF16)
                VBh = alloc(A_, "VBh", [128, S // 128, 128], BF16)

                def attend(nk, qk_fns, qk_reads, v_of, v_reads, scale, post, out_ap):
                    for j in range(nk):
                        q = acnt[0] % 2
                        acnt[0] += 1
                        sc.group("pe", qk_fns(j, sT[q][:, 0:TC]), reads=qk_reads, writes=["sT%d" % q])
                        masks, bias = post(j)
                        if bias is None:
                            sc.op("act", lambda e: e.activation(out=pT[q][:, :], in_=sT[q][:, 0:TC], func=AF.Exp, scale=scale),
                                  reads=["sT%d" % q], writes=["pT%d" % q])
                        else:
                            sc.op("act", lambda e: e.activation(out=pT[q][:, :], in_=sT[q][:, 0:TC], func=AF.Exp, bias=bias, scale=scale),
                                  reads=["sT%d" % q, "rb31"], writes=["pT%d" % q])
                        for (map_, mregs) in masks:
                            sc.op("dve", lambda e: e.tensor_tensor(out=pT[q][:, :], in0=pT[q][:, :], in1=map_, op=ALU.mult),
                                  reads=["pT%d" % q] + mregs, writes=["pT%d" % q])
                        sc.group("pe", [lambda e: e.matmul(oT[:, 0:TC], v_of(j), pT[q][:, :], start=(j == 0), stop=(j == nk - 1)),
                                        lambda e: e.matmul(rsum[:, 0:TC], cx.ones[:, :], pT[q][:, :], start=(j == 0), stop=(j == nk - 1))],
                                 reads=["pT%d" % q, "ones"] + v_reads, writes=["oT", "rsum"])
                    sc.op("dve", lambda e: e.reciprocal(out=rc[:, :], in_=rsum[:, 0:TC]), reads=["rsum"], writes=["rc"])
                    sc.op("dve", lambda e: e.tensor_tensor(out=out_ap, in0=oT[:, 0:TC], in1=rc[:, :], op=ALU.mult),
                          reads=["oT", "rc"], writes=["OT"])

                sca = 128 ** -0.5
                for h in range(A_H):
                    def post(j, h=h):
                        Dd = r0 - 128 * j
                        if Dd <= 128:
                            return [(cx.TH[:, h, Dd + 128:Dd + 128 + TC], ["ebias"]), (MT[:, j, :], ["MT"])], None
                        return [(MT[:, j, :], ["MT"])], cx.rb31[:, h:h + 1]
                    attend(nkt, lambda j, o, h=h: [lambda e: e.matmul(o, KA_T[:, j * 128:(j + 1) * 128], QA_T[:, h, :], start=True, stop=True)],
                           ["KA_T", "QA_T"], lambda j: VA[:, j, :], ["VA"], sca, post, OT[:, h, :])
                scb = 192 ** -0.5
                for h in range(B_H):
                    sc.dma("sp", KNh[:, 0:nkt * 128], KN[h, :, 0:nkt * 128], reads=["KN%d" % h], writes=["KNh"])
                    sc.dma("sp", VBh[:, 0:nkt, :], VBd[h, 0:nkt * 128, :].rearrange("(t p) d -> p t d", p=128), reads=["VB%d" % h], writes=["VBh"])
                    def postb(j):
                        if j >= 2 * c:
                            return [(cx.cm[:, j - 2 * c, :], ["cm"])], None
                        return [], None
                    attend(nkt, lambda j, o, h=h: [lambda e: e.matmul(o, KNh[:, j * 128:(j + 1) * 128], QN_T[:, h, :], start=True, stop=False),
                                                   lambda e: e.matmul(o, KR_T[:, j * 128:(j + 1) * 128], QR_T[:, h, :], start=False, stop=True)],
                           ["KNh", "KR_T", "QN_T", "QR_T"], lambda j: VBh[:, j, :], ["VBh"], scb, postb, OT[:, 6 + h, :])
                for h in range(C_H):
                    attend(2, lambda j, o, h=h: [lambda e: e.matmul(o, MK_T[:, h, j * 128:(j + 1) * 128], XQ_T[:, h, :], start=True, stop=True)],
                           ["MK_T", "XQ_T"], lambda j, h=h: MV[:, j, h * 128:(h + 1) * 128], ["MV"], sca, lambda j: ([], None), OT[:, 12 + h, :])
                sc.barrier()
            if cx.stop_at == "A":
                return

            with ExitStack() as M_:
                wbufs.append(alloc(M_, "wm0", [128, 16, 512], BF16))
                wbufs.append(alloc(M_, "wm1", [128, 16, 512], BF16))
                Mm = alloc(M_, "Mm", [128, 16, TC], BF16)
                macc = alloc(M_, "macc", [128, 4, TC], F32)
                gs = [alloc(M_, "gs%d" % i, [128, TC], F32) for i in range(2)]
                mtmp = alloc(M_, "mtmp", [128, TC], F32)
                gt = alloc(M_, "gt", [128, D], F32)
                bt = alloc(M_, "bt", [128, D], F32)
                junk = alloc(M_, "junk", [128, D], BF16)
                sc.dma("sp", gt[:, :], T["ln_g"][l, 1:2, :].partition_broadcast(128), writes=["lng"])
                sc.dma("sp", bt[:, :], T["ln_b"][l, 1:2, :].partition_broadcast(128), writes=["lnb"])
                for n4 in range(4):
                    cols = (n4 * 512, (n4 + 1) * 512)
                    blocks = {}
                    blocks["g0"] = wload([(O_G + cols[0], O_G + cols[1])])
                    blocks["br"] = wload([cols], "w_branch")
                    blocks["g1"] = wload([(O_G + D + cols[0], O_G + D + cols[1])])
                    blocks["g2"] = wload([(O_G + 2 * D + cols[0], O_G + 2 * D + cols[1])])
                    wbr, wbr_reg = blocks["br"]
                    for b, (ra, rb_) in enumerate(((0, 6), (6, 12), (12, 16))):
                        bv, reg = blocks["g%d" % b]
                        for nn in range(4):
                            n = n4 * 4 + nn
                            p, preg = proj_fm(bv, reg, nn * 128, 128, xrhs, ["xT"])
                            g_ = gs[nn % 2]
                            greg_ = "gs%d" % (nn % 2)
                            sc.op("act", lambda e: e.activation(out=g_[:, :], in_=p, func=AF.Sigmoid), reads=[preg], writes=[greg_])
                            q = acnt[0] % 2
                            acnt[0] += 1
                            sc.group("pe", [(lambda e, r=r: e.matmul(sT[q][:, 0:TC], wbr[:, r, nn * 128:(nn + 1) * 128], OT[:, r, :],
                                                                     start=(r == ra), stop=(r == rb_ - 1))) for r in range(ra, rb_)],
                                     reads=[wbr_reg, "OT"], writes=["sT%d" % q])
                            if b == 0:
                                sc.op("dve", lambda e: e.tensor_tensor(out=macc[:, nn, :], in0=g_[:, :], in1=sT[q][:, 0:TC], op=ALU.mult),
                                      reads=[greg_, "sT%d" % q], writes=["macc"])
                            else:
                                sc.op("dve", lambda e: e.tensor_tensor(out=mtmp[:, :], in0=g_[:, :], in1=sT[q][:, 0:TC], op=ALU.mult),
                                      reads=[greg_, "sT%d" % q], writes=["mtmp"])
                                if b == 1:
                                    sc.op("dve", lambda e: e.tensor_tensor(out=macc[:, nn, :], in0=macc[:, nn, :], in1=mtmp[:, :], op=ALU.add),
                                          reads=["macc", "mtmp"], writes=["macc"])
                                else:
                                    sc.op("dve", lambda e: e.tensor_tensor(out=Mm[:, n, :], in0=macc[:, nn, :], in1=mtmp[:, :], op=ALU.add),
                                          reads=["macc", "mtmp"], writes=["Mm"])
                for cg in range(4):
                    bv, reg = wload([(cg * 512, (cg + 1) * 512)], "w_out")
                    for t in range(NT):
                        p, preg = proj_tm(bv, reg, 0, 512, lambda kc: Mm[:, kc, t * 128:(t + 1) * 128], ["Mm"])
                        sc.op("dve", lambda e: e.tensor_tensor(out=xt[:, t, cg * 512:(cg + 1) * 512], in0=xt[:, t, cg * 512:(cg + 1) * 512],
                                                               in1=p, op=ALU.add), reads=[preg, "xt"], writes=["xt"])
                for t in range(NT):
                    ln_tile(cx, xt[:, t, :], "xt", gt[:, :], bt[:, :], st, "st", junk[:, :], "junk")
                sc.dma("sp", dst[r0:r0 + TC, :].rearrange("(t p) d -> p t d", p=128), xt[:, :, :], reads=["xt"], writes=["X%d" % c])
                sc.barrier()
                wbufs.pop()
                wbufs.pop()
            if cx.stop_at == "M":
                return


WNAMES = ["ln_g", "ln_b", "ffn1_up", "ffn1_down", "w_in", "q_norm", "kv_norm", "w_uq", "w_ukv", "w_mem_kv",
          "w_branch", "w_out", "ffn2_up", "ffn2_down"]


def run_cores(inputs, n_cores=8, n_layers=DEPTH, stop_after=None, trace=False, stop_at=None, nb=1):
    nc = build_program(n_layers, stop_after, stop_at, nb)
    hc = host_consts()
    shared = {"rel_bias": np.ascontiguousarray(inputs["rel_bias"], dtype=np.float32)}
    for k in WNAMES:
        shared[k] = np.ascontiguousarray(inputs[k][:n_layers])
    for k, v in hc.items():
        shared["c_" + k] = v
    in_maps = []
    for c in range(n_cores):
        m = dict(shared)
        m["x"] = np.ascontiguousarray(inputs["x"][c * nb:(c + 1) * nb]).reshape(nb * S, D)
        m["mem"] = np.ascontiguousarray(inputs["mem"][c * nb:(c + 1) * nb]).reshape(nb * MEM, D)
        m["positions"] = np.ascontiguousarray(inputs["positions"][c * nb:(c + 1) * nb]).astype(np.int32)
        in_maps.append(m)
    res = run_bass_kernel_spmd(nc, in_maps, core_ids=list(range(n_cores)), trace=trace)
    outs = np.concatenate([np.asarray(r["out"]).reshape(nb, S, D) for r in res.results], axis=0)
    return outs, res


N_CORES = 4
N_BATCH_PER_CORE = 2


def kernel(**inputs):
    outs, _ = run_cores(inputs, N_CORES, DEPTH, nb=N_BATCH_PER_CORE)
    return outs.astype(np.float32)
```

```python
import math
from contextlib import ExitStack

import numpy as np
import concourse.bass as bass
import concourse.mybir as mybir
from concourse.bass_utils import run_bass_kernel_spmd

F32 = mybir.dt.float32
BF16 = mybir.dt.bfloat16
I32 = mybir.dt.int32
AF = mybir.ActivationFunctionType
ALU = mybir.AluOpType

D = 2048
S = 2048
DEPTH = 4
DFF = 5632
NFC = DFF // 128
MEM = 256
A_H = 6
I_H = 16
I_D = 64
B_H = 6
C_H = 4
TOPK = 256
ALPHA = (2 * DEPTH) ** 0.25
LN_EPS = 1e-5
RMS_EPS = 1e-6
NEG = -1.0e30
O_AQ, O_AK, O_AV, O_IQ, O_IK, O_IW, O_CQ, O_CKV, O_KR, O_XQ, O_G = 0, 768, 896, 1024, 2048, 2112, 2128, 2640, 3152, 3216, 3728
IN_COLS = 9872


class Sched:
    def __init__(self, nc, es):
        self.nc = nc
        self.eng = {"pe": nc.tensor, "act": nc.scalar, "dve": nc.vector, "pool": nc.gpsimd, "sp": nc.sync}
        self.es = es
        self.semobj = {}
        self.cnt = {}
        for e in ("pe", "act", "dve", "pool"):
            self.semobj[e] = es.enter_context(nc.semaphore("s_" + e))
            self.cnt[e] = 0
        self.seen = {e: {} for e in self.eng}
        self.lw = {}
        self.rd = {}
        self.n_ops = 0
        self.dead = False

    def _deps(self, reads, writes):
        deps = {}
        def add(t):
            if t is not None and deps.get(t[0], 0) < t[1]:
                deps[t[0]] = t[1]
        for r in reads:
            add(self.lw.get(r))
        for w in writes:
            add(self.lw.get(w))
            for t in self.rd.get(w, ()):
                add(t)
        return deps

    def _wait(self, eng, deps):
        e = self.eng[eng]
        seen = self.seen[eng]
        for key, val in deps.items():
            if key == eng and eng == "pe":
                continue
            if seen.get(key, 0) >= val:
                continue
            e.wait_ge(self.semobj[key], val)
            seen[key] = val

    def _commit(self, tok, reads, writes):
        for r in reads:
            self.rd.setdefault(r, []).append(tok)
        for w in writes:
            self.lw[w] = tok
            self.rd[w] = []

    def op(self, eng, fn, reads=(), writes=()):
        if self.dead:
            return
        self._wait(eng, self._deps(reads, writes))
        ins = fn(self.eng[eng])
        self.cnt[eng] += 1
        ins.then_inc(self.semobj[eng], 1)
        self._commit((eng, self.cnt[eng]), reads, writes)
        self.n_ops += 1

    def group(self, eng, fns, reads=(), writes=()):
        if self.dead:
            return
        self._wait(eng, self._deps(reads, writes))
        ins = None
        for fn in fns:
            ins = fn(self.eng[eng])
        self.cnt[eng] += 1
        ins.then_inc(self.semobj[eng], 1)
        self._commit((eng, self.cnt[eng]), reads, writes)
        self.n_ops += len(fns)

    def dma(self, q, out, in_, reads=(), writes=()):
        if self.dead:
            return
        key = "d:" + writes[0]
        if key not in self.semobj:
            self.semobj[key] = self.es.enter_context(self.nc.semaphore("d_" + writes[0]))
            self.cnt[key] = 0
        self._wait(q, self._deps(reads, writes))
        self.eng[q].dma_start(out=out, in_=in_).then_inc(self.semobj[key], 16)
        self.cnt[key] += 16
        self._commit((key, self.cnt[key]), reads, writes)
        self.n_ops += 1

    def barrier(self):
        if self.dead:
            return
        allk = {k: v for k, v in self.cnt.items() if v > 0}
        for e in self.eng:
            self._wait(e, {k: v for k, v in allk.items() if not (k == e and e == "pe")})
        self.lw.clear()
        self.rd.clear()

    def finish(self):
        self.dead = False
        self._wait("sp", {k: v for k, v in self.cnt.items() if v > 0})


class Ctx:
    pass


class StopBuild(Exception):
    pass


def ck(cx, name):
    if cx.stop_at == name:
        cx.sc.barrier()
        cx.sc.dead = True


def bucket_thresholds():
    n = np.arange(0, 4096)
    max_exact = 16
    nf = np.maximum(n, 1).astype(np.float32)
    large = max_exact + (np.log(nf / np.float32(max_exact)) / np.float32(math.log(128 / max_exact))
                         * np.float32(32 - max_exact)).astype(np.int32)
    large = np.minimum(large, 31)
    b = np.where(n < max_exact, n, large)
    return [int(np.min(n[b >= j])) for j in range(1, 32)]


def host_consts():
    c = {}
    c["ident"] = np.eye(128, dtype=np.float32)
    sl = np.arange(128)[:, None, None]
    r = np.arange(2)[None, :, None]
    tl = np.arange(256)[None, None, :]
    c["cm"] = (tl - sl >= 128 * r).astype(np.float32)
    t = np.arange(128)[:, None]
    s = np.arange(128)[None, :]
    c["negtri"] = np.where(s <= t, 0.0, NEG).astype(np.float32)
    m = np.arange(512)[None, :]
    c["dist"] = (m - 128 - np.arange(128)[:, None]).astype(np.float32)
    inv = (10000.0 ** (-np.arange(0, 64, 2, dtype=np.float32) / 64)).astype(np.float32)
    c["invf"] = np.concatenate([inv, inv])[:, None].astype(np.float32)
    c["sgn"] = np.concatenate([-np.ones(32), np.ones(32)])[:, None].astype(np.float32)
    return c


def build_program(n_layers=DEPTH, stop_after=None, stop_at=None, nb=1):
    nc = bass.Bass("TRN2", target_bir_lowering=False)
    T = {}
    def din(name, shape, dt=F32):
        T[name] = nc.dram_tensor(name, list(shape), dt, kind="ExternalInput").ap()
        return T[name]
    din("x", [nb * S, D]); din("mem", [nb * MEM, D]); din("positions", [nb, S], I32); din("rel_bias", [32, A_H])
    NL = n_layers
    din("ln_g", [NL, 3, D]); din("ln_b", [NL, 3, D])
    din("q_norm", [NL, 512]); din("kv_norm", [NL, 512])
    for nm, shp in (("ffn1_up", [D, 2 * DFF]), ("ffn1_down", [DFF, D]), ("w_in", [D, IN_COLS]), ("w_uq", [512, 1152]),
                    ("w_ukv", [512, 1536]), ("w_mem_kv", [D, 1024]), ("w_branch", [D, D]), ("w_out", [D, D]),
                    ("ffn2_up", [D, 2 * DFF]), ("ffn2_down", [DFF, D])):
        T[nm] = [nc.dram_tensor("%s_%d" % (nm, li), shp, F32, kind="ExternalInput").ap() for li in range(NL)]
    hc = host_consts()
    for k, v in hc.items():
        din("c_" + k, v.shape)
    out = nc.dram_tensor("out", [nb * S, D], F32, kind="ExternalOutput").ap()
    X = nc.dram_tensor("Xs", [S, D], F32, kind="Internal").ap()

    with ExitStack() as es:
        sc = Sched(nc, es)
        cx = Ctx()
        cx.nc, cx.sc, cx.T, cx.X, cx.out = nc, sc, T, X, out
        cx.stop_at = stop_at
        cx.ident = es.enter_context(nc.sbuf_tensor("ident", [128, 128], BF16))
        sc.dma("pool", cx.ident[:], T["c_ident"][:, :], writes=["ident"])
        setup_globals(cx, es)
        sc.barrier()
        cx.dram_cache = {}
        cx.bg = []
        cx.cvt = CVT
        cx.Wb = {}
        if CVT:
            for nm in BIGW:
                shp = [NL] + list(T[nm][0].shape)
                cx.Wb[nm] = nc.dram_tensor(nm + "_bf", shp, BF16, kind="Internal").ap()
            convert_layer(cx, 0, now=True)
        stages = []
        for l in range(n_layers):
            stages.append(("ffn", l, 1))
            stages.append(("mix", l, 0))
            stages.append(("ffn", l, 2))
        for b in range(nb):
            cx.b = b
            if sc.dead:
                break
            setup_rope(cx, b)
            xin = T["x"][b * S:(b + 1) * S, :]
            xout = out[b * S:(b + 1) * S, :]
            first = True
            for si, (kind, l, which) in enumerate(stages):
                if stop_at == "globals":
                    break
                last = si == len(stages) - 1 or (stop_after is not None and si == stop_after)
                src = xin if first else X
                dst = xout if last else X
                if CVT and b == 0 and kind == "ffn" and which == 1 and l + 1 < n_layers:
                    convert_layer(cx, l + 1, now=False)
                if kind == "ffn":
                    ffn_phase(cx, l, which, src, dst)
                else:
                    mix_phase(cx, l, src, dst)
                first = False
                sc.barrier()
                if last:
                    break
        sc.finish()
    return nc


def ln_tile(cx, y, yreg, gt, bt, st, streg, junk, junkreg, greg="lng", breg="lnb"):
    sc = cx.sc
    s1, nm, s2, rs = st[:, 0:1], st[:, 1:2], st[:, 2:3], st[:, 3:4]
    sc.op("dve", lambda e: e.memset(st[:, 0:4], 0.0), writes=[streg])
    sc.op("act", lambda e: e.activation(out=junk, in_=y, func=AF.Identity, accum_out=s1),
          reads=[yreg], writes=[junkreg, streg])
    sc.op("dve", lambda e: e.tensor_scalar(out=nm, in0=s1, scalar1=-1.0 / D, scalar2=None, op0=ALU.mult),
          reads=[streg], writes=[streg])
    sc.op("act", lambda e: e.activation(out=junk, in_=y, func=AF.Square, bias=nm, scale=1.0, accum_out=s2),
          reads=[yreg, streg], writes=[junkreg, streg])
    sc.op("dve", lambda e: e.tensor_scalar(out=rs, in0=s2, scalar1=1.0 / D, scalar2=LN_EPS, op0=ALU.mult, op1=ALU.add),
          reads=[streg], writes=[streg])
    sc.op("act", lambda e: e.activation(out=rs, in_=rs, func=AF.Sqrt), reads=[streg], writes=[streg])
    sc.op("dve", lambda e: e.reciprocal(out=rs, in_=rs), reads=[streg], writes=[streg])
    sc.op("dve", lambda e: e.tensor_scalar(out=y, in0=y, scalar1=nm, scalar2=rs, op0=ALU.add, op1=ALU.mult),
          reads=[yreg, streg], writes=[yreg])
    sc.op("dve", lambda e: e.tensor_tensor(out=y, in0=y, in1=gt, op=ALU.mult), reads=[yreg, greg], writes=[yreg])
    sc.op("dve", lambda e: e.tensor_tensor(out=y, in0=y, in1=bt, op=ALU.add), reads=[yreg, breg], writes=[yreg])


def load_x_chunk(cx, src, r0, ntile, xt, xb, xT, tp, alpha_scale=True):
    sc = cx.sc
    sc.dma("sp", xt[:, :, :], src[r0:r0 + 128 * ntile, :].rearrange("(t p) d -> p t d", p=128),
           reads=["X%d" % (r0 // 256 + i) for i in range(max(1, ntile // 2))] if src is cx.X else [], writes=["xt"])
    for t in range(ntile):
        b = xb[0]
        breg = "xb0"
        sc.op("dve", lambda e: e.tensor_copy(out=b[:, :], in_=xt[:, t, :]), reads=["xt"], writes=[breg])
        if alpha_scale:
            sc.op("act", lambda e: e.mul(out=xt[:, t, :], in_=xt[:, t, :], mul=ALPHA), reads=["xt", breg], writes=["xt"])
        for h in range(2):
            p = tp[h]
            preg = "tp%d" % h
            sc.group("pe", [(lambda e, kk=kk: e.transpose(out=p[:, kk * 128:(kk + 1) * 128],
                                                          in_=b[:, (h * 8 + kk) * 128:(h * 8 + kk + 1) * 128],
                                                          identity=cx.ident[:, :])) for kk in range(8)],
                     reads=[breg, "ident"], writes=[preg])
            sc.op("act" if h == 0 else "dve",
                  (lambda e: e.activation(out=xT[:, h * 8:(h + 1) * 8, t * 128:(t + 1) * 128],
                                          in_=p[:, :].rearrange("p (k n) -> p k n", k=8), func=AF.Copy)) if h == 0 else
                  (lambda e: e.tensor_copy(out=xT[:, h * 8:(h + 1) * 8, t * 128:(t + 1) * 128],
                                           in_=p[:, :].rearrange("p (k n) -> p k n", k=8))),
                  reads=[preg], writes=["xT"])


def ffn_phase(cx, l, which, src, dst):
    nc, sc, T = cx.nc, cx.sc, cx.T
    wup, wreg = wsrc(cx, "ffn%d_up" % which, l)
    wdn, _ = wsrc(cx, "ffn%d_down" % which, l)
    lni = 0 if which == 1 else 2
    TC = 512
    NT = TC // 128
    with ExitStack() as es:
        def sb(name, shape, dt):
            return es.enter_context(nc.sbuf_tensor("%s_%d_%d_%d" % (name, cx.b, l, which), list(shape), dt))
        def ps(name, shape, dt):
            return es.enter_context(nc.psum_tensor("%s_%d_%d_%d" % (name, cx.b, l, which), list(shape), dt))
        xt = sb("f_xt", [128, NT, D], F32)
        xb = [sb("f_xb0", [128, D], BF16)]
        xb.append(xb[0])
        xT = sb("f_xT", [128, 16, TC], BF16)
        actT = sb("f_actT", [128, NFC, TC], BF16)
        wu = [sb("f_wu%d" % i, [128, 16, 512], BF16) for i in range(2)]
        wd = [sb("f_wd%d" % i, [128, NFC, 256], BF16) for i in range(2)]
        gt = sb("f_gt", [128, D], F32)
        bt = sb("f_bt", [128, D], F32)
        sg = [sb("f_sg%d" % i, [128, TC], BF16) for i in range(2)]
        st = sb("f_st", [128, 8], F32)
        junk = xb[0]
        tp = [ps("f_tp%d" % i, [128, 1024], BF16) for i in range(2)]
        pG = [ps("f_pG%d" % i, [128, 512], F32) for i in range(2)]
        pU = [ps("f_pU%d" % i, [128, 512], F32) for i in range(2)]
        pD = [ps("f_pD%d" % i, [128, 256], F32) for i in range(2)]

        sc.dma("sp", gt[:, :], T["ln_g"][l, lni:lni + 1, :].partition_broadcast(128), writes=["lng"])
        sc.dma("sp", bt[:, :], T["ln_b"][l, lni:lni + 1, :].partition_broadcast(128), writes=["lnb"])

        def load_wu(j):
            b = wu[j % 2]
            reg = "wu%d" % (j % 2)
            sc.dma("pool", b[:, :, 0:256], wup[:, j * 256:(j + 1) * 256].rearrange("(kc p) n -> p kc n", p=128), reads=wreg, writes=[reg])
            sc.dma("pool", b[:, :, 256:512], wup[:, DFF + j * 256:DFF + (j + 1) * 256].rearrange("(kc p) n -> p kc n", p=128), reads=wreg, writes=[reg])
            bg_step(cx)

        def load_wd(g):
            sc.dma("pool", wd[g % 2][:, :, :], wdn[:, g * 256:(g + 1) * 256].rearrange("(fc p) n -> p fc n", p=128),
                   reads=wreg, writes=["wd%d" % (g % 2)])
            bg_step(cx)

        for c in range(S // TC):
            r0 = c * TC
            load_x_chunk(cx, src, r0, NT, xt, xb, xT, tp)
            load_wu(0); load_wu(1)
            for j in range(NFC // 2):
                b = wu[j % 2]
                reg = "wu%d" % (j % 2)
                for jj in range(2):
                    f = 2 * j + jj
                    q = f % 2
                    sc.group("pe", [(lambda e, kc=kc: e.matmul(pG[q][:, :], b[:, kc, jj * 128:(jj + 1) * 128], xT[:, kc, :],
                                                               start=(kc == 0), stop=(kc == 15))) for kc in range(16)],
                             reads=[reg, "xT"], writes=["pG%d" % q])
                    sc.group("pe", [(lambda e, kc=kc: e.matmul(pU[q][:, :], b[:, kc, 256 + jj * 128:256 + (jj + 1) * 128], xT[:, kc, :],
                                                               start=(kc == 0), stop=(kc == 15))) for kc in range(16)],
                             reads=[reg, "xT"], writes=["pU%d" % q])
                    sc.op("act", lambda e: e.activation(out=sg[q][:, :], in_=pG[q][:, :], func=AF.Silu),
                          reads=["pG%d" % q], writes=["sg%d" % q])
                    sc.op("dve", lambda e: e.tensor_tensor(out=actT[:, f, :], in0=sg[q][:, :], in1=pU[q][:, :], op=ALU.mult),
                          reads=["sg%d" % q, "pU%d" % q], writes=["actT"])
                if j + 2 < NFC // 2:
                    load_wu(j + 2)
            load_wd(0); load_wd(1)
            for g in range(D // 256):
                b = wd[g % 2]
                reg = "wd%d" % (g % 2)
                for t in range(NT):
                    q = (g * NT + t) % 2
                    sc.group("pe", [(lambda e, f=f: e.matmul(pD[q][:, :], actT[:, f, t * 128:(t + 1) * 128], b[:, f, :],
                                                             start=(f == 0), stop=(f == NFC - 1))) for f in range(NFC)],
                             reads=[reg, "actT"], writes=["pD%d" % q])
                    sc.op("dve", lambda e: e.scalar_tensor_tensor(out=xt[:, t, g * 256:(g + 1) * 256], in0=pD[q][:, :], scalar=0.5,
                                                                  in1=xt[:, t, g * 256:(g + 1) * 256], op0=ALU.mult, op1=ALU.add),
                          reads=["pD%d" % q, "xt"], writes=["xt"])
                if g + 2 < D // 256:
                    load_wd(g + 2)
            for t in range(NT):
                ln_tile(cx, xt[:, t, :], "xt", gt[:, :], bt[:, :], st, "st", junk[:, :], "xb0")
            sc.dma("sp", dst[r0:r0 + TC, :].rearrange("(t p) d -> p t d", p=128), xt[:, :, :],
                   reads=["xt"], writes=["X%d" % (r0 // 256), "X%d" % (r0 // 256 + 1)])


CVT = True
BIGW = ["ffn1_up", "ffn1_down", "w_in", "w_uq", "w_ukv", "w_mem_kv", "w_branch", "w_out", "ffn2_up", "ffn2_down"]


def convert_layer(cx, l, now):
    T, sc = cx.T, cx.sc
    jobs = []
    for nm in BIGW:
        src = T[nm][l]
        dstw = cx.Wb[nm][l]
        R = src.shape[0]
        for r0 in range(0, R, 128):
            jobs.append((dstw[r0:r0 + 128, :], src[r0:r0 + 128, :], l))
    if now:
        for (o, i, ll) in jobs:
            sc.dma("pool", o, i, writes=["cv_L%d" % ll])
    else:
        cx.bg.extend(jobs)


def bg_step(cx, n=1):
    for _ in range(n):
        if cx.bg:
            o, i, ll = cx.bg.pop(0)
            cx.sc.dma("pool", o, i, writes=["cv_L%d" % ll])


def wsrc(cx, nm, l):
    if cx.cvt:
        return cx.Wb[nm][l], ["cv_L%d" % l]
    return cx.T[nm][l], []


def setup_globals(cx, es):
    nc, sc, T = cx.nc, cx.sc, cx.T
    def gsb(name, shape, dt):
        return es.enter_context(nc.sbuf_tensor(name, list(shape), dt))
    cx.ones = gsb("ones", [128, 128], BF16)
    sc.op("dve", lambda e: e.memset(cx.ones[:, :], 1.0), writes=["ones"])
    cx.cm = gsb("cm", [128, 2, 256], BF16)
    sc.dma("pool", cx.cm[:, :, :], T["c_cm"][:, :, :], writes=["cm"])
    cx.negtri = gsb("negtri", [128, 128], F32)
    sc.dma("sp", cx.negtri[:, :], T["c_negtri"][:, :], writes=["negtri"])
    cx.thrlow = gsb("thrlow", [128, 1], F32)
    sc.op("dve", lambda e: e.memset(cx.thrlow[:, :], -1.0e29), writes=["thrlow"])
    cx.CS = gsb("ropeCS", [64, S], BF16)
    cx.SN = gsb("ropeSN", [64, S], BF16)
    cx.TH = gsb("ebias", [128, A_H, 512], BF16)
    cx.rb31 = gsb("rb31", [128, A_H], F32)
    with ExitStack() as ts:
        def tsb(name, shape, dt):
            return ts.enter_context(nc.sbuf_tensor(name, list(shape), dt))
        RB = tsb("g_rb", [128, 32, A_H], F32)
        dRB = tsb("g_drb", [128, 32, A_H], F32)
        dist = tsb("g_dist", [128, 512], F32)
        G = tsb("g_G", [128, 512], F32)
        G2 = tsb("g_G2", [128, 512], F32)
        sc.dma("sp", RB[:, :, :], T["rel_bias"].rearrange("(o j) h -> o j h", o=1).partition_broadcast(128), writes=["RB"])
        sc.dma("sp", dist[:, :], T["c_dist"][:, :], writes=["dist"])
        sc.op("dve", lambda e: e.tensor_tensor(out=dRB[:, 1:32, :], in0=RB[:, 1:32, :], in1=RB[:, 0:31, :], op=ALU.subtract),
              reads=["RB"], writes=["dRB"])
        sc.op("dve", lambda e: e.tensor_copy(out=cx.rb31[:, :], in_=RB[:, 31, :]), reads=["RB"], writes=["rb31"])
        thr = bucket_thresholds()
        for h in range(A_H):
            for j in range(1, 32):
                if j == 1:
                    sc.op("dve", lambda e: e.tensor_scalar(out=G[:, :], in0=dist[:, :], scalar1=float(thr[j - 1]) - 0.5,
                                                           scalar2=dRB[:, j, h:h + 1], op0=ALU.is_ge, op1=ALU.mult),
                          reads=["dist", "dRB"], writes=["G"])
                else:
                    sc.op("dve", lambda e: e.tensor_scalar(out=G2[:, :], in0=dist[:, :], scalar1=float(thr[j - 1]) - 0.5,
                                                           scalar2=dRB[:, j, h:h + 1], op0=ALU.is_ge, op1=ALU.mult),
                          reads=["dist", "dRB"], writes=["G2"])
                    sc.op("dve", lambda e: e.tensor_tensor(out=G[:, :], in0=G[:, :], in1=G2[:, :], op=ALU.add),
                          reads=["G", "G2"], writes=["G"])
            sc.op("act", lambda e: e.activation(out=cx.TH[:, h, :], in_=G[:, :], func=AF.Exp, bias=RB[:, 0, h:h + 1], scale=1.0),
                  reads=["G", "RB"], writes=["ebias"])
        sc.barrier()


def setup_rope(cx, b):
    nc, sc, T = cx.nc, cx.sc, cx.T
    with ExitStack() as ts:
        def tsb(name, shape, dt):
            return ts.enter_context(nc.sbuf_tensor("%s_b%d" % (name, b), list(shape), dt))
        posi = tsb("g_posi", [64, S], I32)
        ang = tsb("g_ang", [64, S], F32)
        tmp = tsb("g_tmp", [64, S], F32)
        invf = tsb("g_invf", [64, 1], F32)
        sgn = tsb("g_sgn", [64, 1], F32)
        sc.dma("sp", posi[:, :], T["positions"][b:b + 1, :].partition_broadcast(64), writes=["posi"])
        sc.dma("sp", invf[:, :], T["c_invf"][:, :], writes=["invf"])
        sc.dma("sp", sgn[:, :], T["c_sgn"][:, :], writes=["sgn"])
        sc.op("dve", lambda e: e.tensor_copy(out=ang[:, :], in_=posi[:, :]), reads=["posi"], writes=["ang"])
        sc.op("dve", lambda e: e.tensor_scalar(out=ang[:, :], in0=ang[:, :], scalar1=invf[:, 0:1], scalar2=None, op0=ALU.mult),
              reads=["ang", "invf"], writes=["ang"])
        TWO_PI = 2 * math.pi
        def sin_of(shift, out_ap, out_reg, post_sgn):
            sc.op("dve", lambda e: e.tensor_scalar(out=tmp[:, :], in0=ang[:, :], scalar1=shift, scalar2=1.0 / TWO_PI, op0=ALU.add, op1=ALU.mult),
                  reads=["ang"], writes=["tmp"])
            sc.op("dve", lambda e: e.tensor_copy(out=posi[:, :], in_=tmp[:, :]), reads=["tmp"], writes=["posi"])
            sc.op("dve", lambda e: e.tensor_copy(out=tmp[:, :], in_=posi[:, :]), reads=["posi"], writes=["tmp"])
            sc.op("dve", lambda e: e.scalar_tensor_tensor(out=tmp[:, :], in0=tmp[:, :], scalar=-TWO_PI, in1=ang[:, :], op0=ALU.mult, op1=ALU.add),
                  reads=["tmp", "ang"], writes=["tmp"])
            sc.op("dve", lambda e: e.tensor_scalar(out=tmp[:, :], in0=tmp[:, :], scalar1=shift, scalar2=None, op0=ALU.add),
                  reads=["tmp"], writes=["tmp"])
            sc.op("dve", lambda e: e.tensor_scalar(out=tmp2[:, :], in0=tmp[:, :], scalar1=math.pi, scalar2=-TWO_PI, op0=ALU.is_gt, op1=ALU.mult),
                  reads=["tmp"], writes=["tmp2"])
            sc.op("dve", lambda e: e.tensor_tensor(out=tmp[:, :], in0=tmp[:, :], in1=tmp2[:, :], op=ALU.add), reads=["tmp", "tmp2"], writes=["tmp"])
            sc.op("dve", lambda e: e.tensor_scalar(out=tmp2[:, :], in0=tmp[:, :], scalar1=-math.pi, scalar2=TWO_PI, op0=ALU.is_lt, op1=ALU.mult),
                  reads=["tmp"], writes=["tmp2"])
            sc.op("dve", lambda e: e.tensor_tensor(out=tmp[:, :], in0=tmp[:, :], in1=tmp2[:, :], op=ALU.add), reads=["tmp", "tmp2"], writes=["tmp"])
            if post_sgn:
                sc.op("act", lambda e: e.activation(out=tmp[:, :], in_=tmp[:, :], func=AF.Sin), reads=["tmp"], writes=["tmp"])
                sc.op("dve", lambda e: e.tensor_scalar(out=out_ap, in0=tmp[:, :], scalar1=sgn[:, 0:1], scalar2=None, op0=ALU.mult),
                      reads=["tmp", "sgn"], writes=[out_reg])
            else:
                sc.op("act", lambda e: e.activation(out=out_ap, in_=tmp[:, :], func=AF.Sin), reads=["tmp"], writes=[out_reg])
        tmp2 = tsb("g_tmp2", [64, S], F32)
        sin_of(0.0, cx.SN[:, :], "ropeSN", True)
        sin_of(0.5 * math.pi, cx.CS[:, :], "ropeCS", False)
        sc.barrier()


def mix_phase(cx, l, src, dst):
    nc, sc, T = cx.nc, cx.sc, cx.T
    TC = 256
    NT = 2
    NCH = S // TC
    if "KN" not in cx.dram_cache:
        cx.dram_cache["KN"] = nc.dram_tensor("KNs", [B_H, 128, S], BF16, kind="Internal").ap()
        cx.dram_cache["VB"] = nc.dram_tensor("VBs", [B_H, S, 128], BF16, kind="Internal").ap()
    KN = cx.dram_cache["KN"]
    VBd = cx.dram_cache["VB"]
    uid = [0]
    def alloc(stack, name, shape, dt, psum=False):
        uid[0] += 1
        nm = "m%d_%d_%s_%d" % (cx.b, l, name, uid[0])
        if psum:
            return stack.enter_context(nc.psum_tensor(nm, list(shape), dt))
        return stack.enter_context(nc.sbuf_tensor(nm, list(shape), dt))
    with ExitStack() as es:
        sb = lambda name, shape, dt: alloc(es, name, shape, dt)
        ps = lambda name, shape, dt: alloc(es, name, shape, dt, True)
        KA_T = sb("KA_T", [128, S], BF16)
        VA = sb("VA", [128, S // 128, 128], BF16)
        IK_T2 = sb("IK_T2", [128, S], BF16)
        KR_T = sb("KR_T", [64, S], BF16)
        MK_T = sb("MK_T", [128, C_H, MEM], BF16)
        MV = sb("MV", [128, 2, 512], BF16)
        gq = sb("gq", [128, 4], F32)
        gkv = sb("gkv", [128, 4], F32)
        ones_f = sb("ones_f", [128, 128], F32)
        st = sb("st", [128, 8], F32)
        mx = sb("mx", [128, 16], F32)
        xt = sb("xt", [128, NT, D], F32)
        xT = sb("xT", [128, 16, TC], BF16)
        wbufs = [sb("wb%d" % i, [128, 16, 512], BF16) for i in range(2)]
        QA_T = sb("QA_T", [128, A_H, TC], BF16)
        IQ_T = sb("IQ_T", [128, 8, TC], BF16)
        IW = sb("IW", [128, NT, 16], F32)
        QN_T = sb("QN_T", [128, B_H, TC], BF16)
        QR_T = sb("QR_T", [64, B_H, TC], BF16)
        XQ_T = sb("XQ_T", [128, C_H, TC], BF16)
        OT = sb("OT", [128, 16, TC], BF16)
        MT = sb("MTk", [128, S // 128, TC], BF16)
        tp = ps("tp", [128, 1024], BF16)
        pj = [ps("pj%d" % i, [128, 512], F32) for i in range(2)]
        sT = [ps("sT%d" % i, [128, 512], F32) for i in range(2)]
        oT = ps("oT", [128, 512], F32)
        rsum = ps("rs", [128, 512], F32)
        pr = ps("pr", [128, 512], F32)

        sc.op("dve", lambda e: e.memset(ones_f[:, :], 1.0), writes=["ones_f"])
        for k in range(4):
            sc.dma("sp", gq[:, k:k + 1], T["q_norm"][l, k * 128:(k + 1) * 128].rearrange("(p o) -> p o", o=1), writes=["gq"])
            sc.dma("sp", gkv[:, k:k + 1], T["kv_norm"][l, k * 128:(k + 1) * 128].rearrange("(p o) -> p o", o=1), writes=["gkv"])

        if cx.stop_at == "mixalloc":
            sc.barrier()
            return
        wcount = [0]
        def wload(ranges, wname="w_in", kcn=16):
            i = wcount[0] % len(wbufs)
            wcount[0] += 1
            reg = "wb%d" % i
            wsrc_, wreg = wsrc(cx, wname, l)
            bview = wbufs[i][:, :, :] if kcn == 16 else wbufs[i][:, :, :].rearrange("p a b -> p (a b)").rearrange("p (k n) -> p k n", k=kcn)
            o = 0
            for (c0, c1) in ranges:
                sc.dma("pool", bview[:, :, o:o + (c1 - c0)], wsrc_[:, c0:c1].rearrange("(kc p) n -> p kc n", p=128), reads=wreg, writes=[reg])
                o += c1 - c0
            bg_step(cx)
            return bview, reg

        def evac(eng, out, in_, reads, writes):
            if eng == "act":
                sc.op("act", lambda e: e.activation(out=out, in_=in_, func=AF.Copy), reads=reads, writes=writes)
            else:
                sc.op("dve", lambda e: e.tensor_copy(out=out, in_=in_), reads=reads, writes=writes)

        pjc = [0]
        def proj_fm(bv, reg, col0, ncols, rhs, rhs_regs, kcn=16):
            q = pjc[0] % 2
            pjc[0] += 1
            p = pj[q]
            n = rhs(0).shape[-1]
            sc.group("pe", [(lambda e, kc=kc: e.matmul(p[0:ncols, 0:n], bv[:, kc, col0:col0 + ncols], rhs(kc),
                                                       start=(kc == 0), stop=(kc == kcn - 1))) for kc in range(kcn)],
                     reads=[reg] + rhs_regs, writes=["pj%d" % q])
            return p[0:ncols, 0:n], "pj%d" % q

        def proj_tm(bv, reg, col0, ncols, lhs, lhs_regs, kcn=16):
            q = pjc[0] % 2
            pjc[0] += 1
            p = pj[q]
            sc.group("pe", [(lambda e, kc=kc: e.matmul(p[:, 0:ncols], lhs(kc), bv[:, kc, col0:col0 + ncols],
                                                       start=(kc == 0), stop=(kc == kcn - 1))) for kc in range(kcn)],
                     reads=[reg] + lhs_regs, writes=["pj%d" % q])
            return p[:, 0:ncols], "pj%d" % q

        def transpose_rows(srcb, sreg, dstT, dreg, t):
            for h in range(2):
                sc.group("pe", [(lambda e, kk=kk: e.transpose(out=tp[:, kk * 128:(kk + 1) * 128],
                                                              in_=srcb[:, (h * 8 + kk) * 128:(h * 8 + kk + 1) * 128],
                                                              identity=cx.ident[:, :])) for kk in range(8)],
                         reads=[sreg, "ident"], writes=["tp"])
                evac("act" if h == 0 else "dve", dstT[:, h * 8:(h + 1) * 8, t * 128:(t + 1) * 128],
                     tp[:, :].rearrange("p (k n) -> p k n", k=8), ["tp"], [dreg])

        with ExitStack() as ms:
            memt = alloc(ms, "memt", [128, 2, D], F32)
            memb = alloc(ms, "memb", [128, D], BF16)
            memT = alloc(ms, "memT", [128, 16, MEM], BF16)
            sc.dma("sp", memt[:, :, :], T["mem"][cx.b * MEM:(cx.b + 1) * MEM, :].rearrange("(t p) d -> p t d", p=128), writes=["memt"])
            for t in range(2):
                sc.op("dve", lambda e: e.tensor_copy(out=memb[:, :], in_=memt[:, t, :]), reads=["memt"], writes=["memb"])
                transpose_rows(memb, "memb", memT, "memT", t)
            if cx.stop_at == "memT":
                sc.barrier()
                return
            bv, reg = wload([(0, 512)], "w_mem_kv")
            if cx.stop_at == "memW":
                sc.barrier()
                return
            for h in range(C_H):
                p, preg = proj_fm(bv, reg, h * 128, 128, lambda kc: memT[:, kc, :], ["memT"])
                evac("act" if h % 2 else "dve", MK_T[:, h, :], p, [preg], ["MK_T"])
            if cx.stop_at == "mk":
                sc.barrier()
                return
            bv, reg = wload([(512, 1024)], "w_mem_kv")
            for t in range(2):
                p, preg = proj_tm(bv, reg, 0, 512, lambda kc: memT[:, kc, t * 128:(t + 1) * 128], ["memT"])
                evac("act" if t % 2 else "dve", MV[:, t, :], p, [preg], ["MV"])
            sc.barrier()
        if cx.stop_at == "memkv":
            return

        acnt = [0]

        for c in range(NCH):
            r0 = c * TC
            nkt = 2 * c + 2
            xrhs = lambda kc: xT[:, kc, :]
            with ExitStack() as P:
                xb = alloc(P, "xb", [128, D], BF16)
                CN = [alloc(P, "CQN", [128, 4, TC], BF16), alloc(P, "CKVN", [128, 4, TC], BF16)]
                cf = alloc(P, "cf", [128, 4, TC], F32)
                sq = alloc(P, "sq", [128, 4, TC], BF16)
                rstd = alloc(P, "rstd", [128, TC], F32)
                rt = [alloc(P, "rt%d" % i, [64, TC], F32) for i in range(2)]
                vst = alloc(P, "vst", [128, 768], BF16)

                sc.dma("sp", xt[:, :, :], src[r0:r0 + TC, :].rearrange("(t p) d -> p t d", p=128),
                       reads=["X%d" % c] if src is cx.X else [], writes=["xt"])
                for t in range(NT):
                    sc.op("dve", lambda e: e.tensor_copy(out=xb[:, :], in_=xt[:, t, :]), reads=["xt"], writes=["xb"])
                    sc.op("act", lambda e: e.mul(out=xt[:, t, :], in_=xt[:, t, :], mul=ALPHA), reads=["xt"], writes=["xt"])
                    transpose_rows(xb, "xb", xT, "xT", t)

                def rope_combine(p1, r1, p2, r2, out, outreg):
                    sc.op("dve", lambda e: e.tensor_tensor(out=rt[0][:, :], in0=p1, in1=cx.CS[:, r0:r0 + TC], op=ALU.mult),
                          reads=[r1, "ropeCS"], writes=["rt0"])
                    sc.op("dve", lambda e: e.tensor_tensor(out=rt[1][:, :], in0=p2, in1=cx.SN[:, r0:r0 + TC], op=ALU.mult),
                          reads=[r2, "ropeSN"], writes=["rt1"])
                    sc.op("dve", lambda e: e.tensor_tensor(out=out, in0=rt[0][:, :], in1=rt[1][:, :], op=ALU.add),
                          reads=["rt0", "rt1"], writes=[outreg])

                ck(cx, "P0")
                bv, reg = wload([(0, 512)])
                for h in range(4):
                    p, preg = proj_fm(bv, reg, h * 128, 128, xrhs, ["xT"])
                    evac("act" if h % 2 else "dve", QA_T[:, h, :], p, [preg], ["QA_T"])
                bv, reg = wload([(512, 1024)])
                for h in range(4, 6):
                    p, preg = proj_fm(bv, reg, (h - 4) * 128, 128, xrhs, ["xT"])
                    evac("act" if h % 2 else "dve", QA_T[:, h, :], p, [preg], ["QA_T"])
                p, preg = proj_fm(bv, reg, 256, 128, xrhs, ["xT"])
                evac("act", KA_T[:, r0:r0 + TC], p, [preg], ["KA_T"])
                for t in range(NT):
                    p, preg = proj_tm(bv, reg, 384, 128, lambda kc: xT[:, kc, t * 128:(t + 1) * 128], ["xT"])
                    evac("dve", VA[:, 2 * c + t, :], p, [preg], ["VA"])
                ck(cx, "P1")
                for half in range(2):
                    bv, reg = wload([(O_IQ + half * 512, O_IQ + (half + 1) * 512)])
                    for k in range(4):
                        p, preg = proj_fm(bv, reg, k * 128, 128, xrhs, ["xT"])
                        evac("act" if k % 2 else "dve", IQ_T[:, half * 4 + k, :], p, [preg], ["IQ_T"])
                ck(cx, "P3")
                bv, reg = wload([(O_IK, O_IK + 64), (O_IK, O_IK + 64), (O_KR, O_KR + 64), (O_KR + 32, O_KR + 64), (O_KR, O_KR + 32),
                                 (O_IW, O_IW + 16)])
                p, preg = proj_fm(bv, reg, 0, 128, xrhs, ["xT"])
                evac("act", IK_T2[:, r0:r0 + TC], p, [preg], ["IK_T2"])
                p1, r1 = proj_fm(bv, reg, 128, 64, xrhs, ["xT"])
                p2, r2 = proj_fm(bv, reg, 192, 64, xrhs, ["xT"])
                rope_combine(p1, r1, p2, r2, KR_T[:, r0:r0 + TC], "KR_T")
                for t in range(NT):
                    p, preg = proj_tm(bv, reg, 256, 16, lambda kc: xT[:, kc, t * 128:(t + 1) * 128], ["xT"])
                    evac("dve", IW[:, t, :], p, [preg], ["IW"])
                ck(cx, "P4")
                for which, (o0, gcol, greg_) in enumerate(((O_CQ, gq, "gq"), (O_CKV, gkv, "gkv"))):
                    bv, reg = wload([(o0, o0 + 512)])
                    for k in range(4):
                        p, preg = proj_fm(bv, reg, k * 128, 128, xrhs, ["xT"])
                        sc.op("dve", lambda e: e.tensor_copy(out=cf[:, k, :], in_=p), reads=[preg], writes=["cf"])
                        sc.op("dve", lambda e: e.tensor_tensor(out=sq[:, k, :], in0=cf[:, k, :], in1=cf[:, k, :], op=ALU.mult),
                              reads=["cf"], writes=["sq"])
                    ck(cx, "P5a")
                    sc.group("pe", [(lambda e, k=k: e.matmul(pr[:, 0:TC], cx.ones[:, :], sq[:, k, :], start=(k == 0), stop=(k == 3))) for k in range(4)],
                             reads=["ones", "sq"], writes=["pr"])
                    sc.op("dve", lambda e: e.tensor_scalar(out=rstd[:, :], in0=pr[:, 0:TC], scalar1=1.0 / 512, scalar2=RMS_EPS,
                                                           op0=ALU.mult, op1=ALU.add), reads=["pr"], writes=["rstd"])
                    ck(cx, "P5b")
                    sc.op("act", lambda e: e.activation(out=rstd[:, :], in_=rstd[:, :], func=AF.Sqrt), reads=["rstd"], writes=["rstd"])
                    sc.op("dve", lambda e: e.reciprocal(out=rstd[:, :], in_=rstd[:, :]), reads=["rstd"], writes=["rstd"])
                    ck(cx, "P5c")
                    for k in range(4):
                        sc.op("dve", lambda e: e.scalar_tensor_tensor(out=CN[which][:, k, :], in0=cf[:, k, :], scalar=gcol[:, k:k + 1],
                                                                      in1=rstd[:, :], op0=ALU.mult, op1=ALU.mult),
                              reads=["cf", "rstd", greg_], writes=["CN%d" % which])
                ck(cx, "P6")
                bv, reg = wload([(O_XQ, O_XQ + 512)])
                for h in range(C_H):
                    p, preg = proj_fm(bv, reg, h * 128, 128, xrhs, ["xT"])
                    evac("act" if h % 2 else "dve", XQ_T[:, h, :], p, [preg], ["XQ_T"])
                rngs = [(0, 1152)]
                for h in range(B_H):
                    rngs += [(h * 192 + 160, h * 192 + 192), (h * 192 + 128, h * 192 + 160)]
                bv, reg = wload(rngs, "w_uq", kcn=4)
                qrhs = lambda kc: CN[0][:, kc, :]
                for h in range(B_H):
                    p, preg = proj_fm(bv, reg, h * 192, 128, qrhs, ["CN0"], kcn=4)
                    evac("act", QN_T[:, h, :], p, [preg], ["QN_T"])
                    p1, r1 = proj_fm(bv, reg, h * 192 + 128, 64, qrhs, ["CN0"], kcn=4)
                    p2, r2 = proj_fm(bv, reg, 1152 + h * 64, 64, qrhs, ["CN0"], kcn=4)
                    rope_combine(p1, r1, p2, r2, QR_T[:, h, :], "QR_T")
                ck(cx, "P8")
                rngs = [(h * 256, h * 256 + 128) for h in range(B_H)] + [(h * 256 + 128, h * 256 + 256) for h in range(B_H)]
                bv, reg = wload(rngs, "w_ukv", kcn=4)
                krhs = lambda kc: CN[1][:, kc, :]
                for h in range(B_H):
                    p, preg = proj_fm(bv, reg, h * 128, 128, krhs, ["CN1"], kcn=4)
                    hv = (h % 2) * TC
                    evac("act" if h % 2 else "dve", vst[:, hv:hv + TC], p, [preg], ["vstk%d" % (h % 2)])
                    sc.dma("sp", KN[h, :, r0:r0 + TC], vst[:, hv:hv + TC], reads=["vstk%d" % (h % 2)], writes=["KN%d" % h])
                for t in range(NT):
                    p, preg = proj_tm(bv, reg, 768, 512, lambda kc: CN[1][:, kc, t * 128:(t + 1) * 128], ["CN1"], kcn=4)
                    evac("act", vst[:, 0:512], p, [preg], ["vstk0", "vstk1"])
                    p, preg = proj_tm(bv, reg, 768 + 512, 256, lambda kc: CN[1][:, kc, t * 128:(t + 1) * 128], ["CN1"], kcn=4)
                    evac("dve", vst[:, 512:768], p, [preg], ["vstv"])
                    for h in range(B_H):
                        sc.dma("sp", VBd[h, r0 + t * 128:r0 + (t + 1) * 128, :], vst[:, h * 128:(h + 1) * 128],
                               reads=["vstk0", "vstk1", "vstv"], writes=["VB%d" % h])
                sc.barrier()
            if cx.stop_at == "P":
                return

            with ExitStack() as I_:
                acc = alloc(I_, "acc", [128, S], F32)
                wk = alloc(I_, "wk", [128, S], F32)
                Mk = alloc(I_, "Mk", [128, S], BF16)
                rl = [alloc(I_, "rl%d" % i, [128, 512], F32) for i in range(2)]
                for tt in range(NT):
                    i = 2 * c + tt
                    L = (i + 1) * 128
                    for hh in range(I_H):
                        pb = (hh % 2) * 64
                        for k4 in range((L + 511) // 512):
                            n = min(512, L - k4 * 512)
                            q = pjc[0] % 2
                            pjc[0] += 1
                            sc.group("pe", [lambda e: e.matmul(pj[q][:, 0:n], IQ_T[pb:pb + 64, hh // 2, tt * 128:(tt + 1) * 128],
                                                               IK_T2[pb:pb + 64, k4 * 512:k4 * 512 + n], start=True, stop=True)],
                                     reads=["IQ_T", "IK_T2"], writes=["pj%d" % q])
                            sc.op("act", lambda e: e.activation(out=rl[q][:, 0:n], in_=pj[q][:, 0:n], func=AF.Relu),
                                  reads=["pj%d" % q], writes=["rl%d" % q])
                            a = acc[:, k4 * 512:k4 * 512 + n]
                            if hh == 0:
                                sc.op("dve", lambda e: e.tensor_scalar(out=a, in0=rl[q][:, 0:n], scalar1=IW[:, tt, hh:hh + 1], scalar2=None,
                                                                       op0=ALU.mult), reads=["rl%d" % q, "IW"], writes=["acc"])
                            else:
                                sc.op("dve", lambda e: e.scalar_tensor_tensor(out=a, in0=rl[q][:, 0:n], scalar=IW[:, tt, hh:hh + 1], in1=a,
                                                                              op0=ALU.mult, op1=ALU.add),
                                      reads=["rl%d" % q, "IW", "acc"], writes=["acc"])
                    sc.op("dve", lambda e: e.tensor_tensor(out=acc[:, L - 128:L], in0=acc[:, L - 128:L], in1=cx.negtri[:, :], op=ALU.add),
                          reads=["acc", "negtri"], writes=["acc"])
                    if i >= 2:
                        for r in range(32):
                            srcv = acc if r == 0 else wk
                            sreg = "acc" if r == 0 else "wk"
                            sc.op("dve", lambda e: e.max(out=mx[:, 0:8], in_=srcv[:, 0:L]), reads=[sreg], writes=["mx"])
                            if r < 31:
                                sc.op("dve", lambda e: e.match_replace(out=wk[:, 0:L], in_to_replace=mx[:, 0:8], in_values=srcv[:, 0:L],
                                                                       imm_value=NEG), reads=[sreg, "mx"], writes=["wk"])
                        sc.op("dve", lambda e: e.tensor_reduce(out=mx[:, 8:9], in_=mx[:, 0:8], axis=mybir.AxisListType.X, op=ALU.min),
                              reads=["mx"], writes=["mx"])
                        thr_ap, thr_reg = mx[:, 8:9], "mx"
                    else:
                        thr_ap, thr_reg = cx.thrlow[:, 0:1], "thrlow"
                    sc.op("dve", lambda e: e.tensor_scalar(out=Mk[:, 0:L], in0=acc[:, 0:L], scalar1=thr_ap, scalar2=None, op0=ALU.is_ge),
                          reads=["acc", thr_reg], writes=["Mk"])
                    for j0 in range(0, i + 1, 8):
                        nj = min(8, i + 1 - j0)
                        sc.group("pe", [(lambda e, jj=jj: e.transpose(out=tp[:, jj * 128:(jj + 1) * 128],
                                                                      in_=Mk[:, (j0 + jj) * 128:(j0 + jj + 1) * 128],
                                                                      identity=cx.ident[:, :])) for jj in range(nj)],
                                 reads=["Mk", "ident"], writes=["tp"])
                        evac("act", MT[:, j0:j0 + nj, tt * 128:(tt + 1) * 128], tp[:, 0:nj * 128].rearrange("p (k n) -> p k n", k=nj),
                             ["tp"], ["MT"])
                    if tt == 0:
                        sc.op("dve", lambda e: e.memset(MT[:, 2 * c + 1, 0:128], 0.0), writes=["MT"])
                sc.barrier()
            if cx.stop_at == "I":
                return

            with ExitStack() as A_:
                pT = [alloc(A_, "pT%d" % i, [128, TC], BF16) for i in range(2)]
                rc = alloc(A_, "rc", [128, TC], F32)
                KNh = alloc(A_, "KNh", [128, S], BF16)
                VBh = alloc(A_, "VBh", [128, S // 128, 128], BF16)

                def attend(nk, qk_fns, qk_reads, v_of, v_reads, scale, post, out_ap):
                    for j in range(nk):
                        q = acnt[0] % 2
                        acnt[0] += 1
                        sc.group("pe", qk_fns(j, sT[q][:, 0:TC]), reads=qk_reads, writes=["sT%d" % q])
                        masks, bias = post(j)
                        if bias is None:
                            sc.op("act", lambda e: e.activation(out=pT[q][:, :], in_=sT[q][:, 0:TC], func=AF.Exp, scale=scale),
                                  reads=["sT%d" % q], writes=["pT%d" % q])
                        else:
                            sc.op("act", lambda e: e.activation(out=pT[q][:, :], in_=sT[q][:, 0:TC], func=AF.Exp, bias=bias, scale=scale),
                                  reads=["sT%d" % q, "rb31"], writes=["pT%d" % q])
                        for (map_, mregs) in masks:
                            sc.op("dve", lambda e: e.tensor_tensor(out=pT[q][:, :], in0=pT[q][:, :], in1=map_, op=ALU.mult),
                                  reads=["pT%d" % q] + mregs, writes=["pT%d" % q])
                        sc.group("pe", [lambda e: e.matmul(oT[:, 0:TC], v_of(j), pT[q][:, :], start=(j == 0), stop=(j == nk - 1)),
                                        lambda e: e.matmul(rsum[:, 0:TC], cx.ones[:, :], pT[q][:, :], start=(j == 0), stop=(j == nk - 1))],
                                 reads=["pT%d" % q, "ones"] + v_reads, writes=["oT", "rsum"])
                    sc.op("dve", lambda e: e.reciprocal(out=rc[:, :], in_=rsum[:, 0:TC]), reads=["rsum"], writes=["rc"])
                    sc.op("dve", lambda e: e.tensor_tensor(out=out_ap, in0=oT[:, 0:TC], in1=rc[:, :], op=ALU.mult),
                          reads=["oT", "rc"], writes=["OT"])

                sca = 128 ** -0.5
                for h in range(A_H):
                    def post(j, h=h):
                        Dd = r0 - 128 * j
                        if Dd <= 128:
                            return [(cx.TH[:, h, Dd + 128:Dd + 128 + TC], ["ebias"]), (MT[:, j, :], ["MT"])], None
                        return [(MT[:, j, :], ["MT"])], cx.rb31[:, h:h + 1]
                    attend(nkt, lambda j, o, h=h: [lambda e: e.matmul(o, KA_T[:, j * 128:(j + 1) * 128], QA_T[:, h, :], start=True, stop=True)],
                           ["KA_T", "QA_T"], lambda j: VA[:, j, :], ["VA"], sca, post, OT[:, h, :])
                scb = 192 ** -0.5
                for h in range(B_H):
                    sc.dma("sp", KNh[:, 0:nkt * 128], KN[h, :, 0:nkt * 128], reads=["KN%d" % h], writes=["KNh"])
                    sc.dma("sp", VBh[:, 0:nkt, :], VBd[h, 0:nkt * 128, :].rearrange("(t p) d -> p t d", p=128), reads=["VB%d" % h], writes=["VBh"])
                    def postb(j):
                        if j >= 2 * c:
                            return [(cx.cm[:, j - 2 * c, :], ["cm"])], None
                        return [], None
                    attend(nkt, lambda j, o, h=h: [lambda e: e.matmul(o, KNh[:, j * 128:(j + 1) * 128], QN_T[:, h, :], start=True, stop=False),
                                                   lambda e: e.matmul(o, KR_T[:, j * 128:(j + 1) * 128], QR_T[:, h, :], start=False, stop=True)],
                           ["KNh", "KR_T", "QN_T", "QR_T"], lambda j: VBh[:, j, :], ["VBh"], scb, postb, OT[:, 6 + h, :])
                for h in range(C_H):
                    attend(2, lambda j, o, h=h: [lambda e: e.matmul(o, MK_T[:, h, j * 128:(j + 1) * 128], XQ_T[:, h, :], start=True, stop=True)],
                           ["MK_T", "XQ_T"], lambda j, h=h: MV[:, j, h * 128:(h + 1) * 128], ["MV"], sca, lambda j: ([], None), OT[:, 12 + h, :])
                sc.barrier()
            if cx.stop_at == "A":
                return

            with ExitStack() as M_:
                wbufs.append(alloc(M_, "wm0", [128, 16, 512], BF16))
                wbufs.append(alloc(M_, "wm1", [128, 16, 512], BF16))
                Mm = alloc(M_, "Mm", [128, 16, TC], BF16)
                macc = alloc(M_, "macc", [128, 4, TC], F32)
                gs = [alloc(M_, "gs%d" % i, [128, TC], F32) for i in range(2)]
                mtmp = alloc(M_, "mtmp", [128, TC], F32)
                gt = alloc(M_, "gt", [128, D], F32)
                bt = alloc(M_, "bt", [128, D], F32)
                junk = alloc(M_, "junk", [128, D], BF16)
                sc.dma("sp", gt[:, :], T["ln_g"][l, 1:2, :].partition_broadcast(128), writes=["lng"])
                sc.dma("sp", bt[:, :], T["ln_b"][l, 1:2, :].partition_broadcast(128), writes=["lnb"])
                for n4 in range(4):
                    cols = (n4 * 512, (n4 + 1) * 512)
                    blocks = {}
                    blocks["g0"] = wload([(O_G + cols[0], O_G + cols[1])])
                    blocks["br"] = wload([cols], "w_branch")
                    blocks["g1"] = wload([(O_G + D + cols[0], O_G + D + cols[1])])
                    blocks["g2"] = wload([(O_G + 2 * D + cols[0], O_G + 2 * D + cols[1])])
                    wbr, wbr_reg = blocks["br"]
                    for b, (ra, rb_) in enumerate(((0, 6), (6, 12), (12, 16))):
                        bv, reg = blocks["g%d" % b]
                        for nn in range(4):
                            n = n4 * 4 + nn
                            p, preg = proj_fm(bv, reg, nn * 128, 128, xrhs, ["xT"])
                            g_ = gs[nn % 2]
                            greg_ = "gs%d" % (nn % 2)
                            sc.op("act", lambda e: e.activation(out=g_[:, :], in_=p, func=AF.Sigmoid), reads=[preg], writes=[greg_])
                            q = acnt[0] % 2
                            acnt[0] += 1
                            sc.group("pe", [(lambda e, r=r: e.matmul(sT[q][:, 0:TC], wbr[:, r, nn * 128:(nn + 1) * 128], OT[:, r, :],
                                                                     start=(r == ra), stop=(r == rb_ - 1))) for r in range(ra, rb_)],
                                     reads=[wbr_reg, "OT"], writes=["sT%d" % q])
                            if b == 0:
                                sc.op("dve", lambda e: e.tensor_tensor(out=macc[:, nn, :], in0=g_[:, :], in1=sT[q][:, 0:TC], op=ALU.mult),
                                      reads=[greg_, "sT%d" % q], writes=["macc"])
                            else:
                                sc.op("dve", lambda e: e.tensor_tensor(out=mtmp[:, :], in0=g_[:, :], in1=sT[q][:, 0:TC], op=ALU.mult),
                                      reads=[greg_, "sT%d" % q], writes=["mtmp"])
                                if b == 1:
                                    sc.op("dve", lambda e: e.tensor_tensor(out=macc[:, nn, :], in0=macc[:, nn, :], in1=mtmp[:, :], op=ALU.add),
                                          reads=["macc", "mtmp"], writes=["macc"])
                                else:
                                    sc.op("dve", lambda e: e.tensor_tensor(out=Mm[:, n, :], in0=macc[:, nn, :], in1=mtmp[:, :], op=ALU.add),
                                          reads=["macc", "mtmp"], writes=["Mm"])
                for cg in range(4):
                    bv, reg = wload([(cg * 512, (cg + 1) * 512)], "w_out")
                    for t in range(NT):
                        p, preg = proj_tm(bv, reg, 0, 512, lambda kc: Mm[:, kc, t * 128:(t + 1) * 128], ["Mm"])
                        sc.op("dve", lambda e: e.tensor_tensor(out=xt[:, t, cg * 512:(cg + 1) * 512], in0=xt[:, t, cg * 512:(cg + 1) * 512],
                                                               in1=p, op=ALU.add), reads=[preg, "xt"], writes=["xt"])
                for t in range(NT):
                    ln_tile(cx, xt[:, t, :], "xt", gt[:, :], bt[:, :], st, "st", junk[:, :], "junk")
                sc.dma("sp", dst[r0:r0 + TC, :].rearrange("(t p) d -> p t d", p=128), xt[:, :, :], reads=["xt"], writes=["X%d" % c])
                sc.barrier()
                wbufs.pop()
                wbufs.pop()
            if cx.stop_at == "M":
                return


WNAMES = ["ln_g", "ln_b", "ffn1_up", "ffn1_down", "w_in", "q_norm", "kv_norm", "w_uq", "w_ukv", "w_mem_kv",
          "w_branch", "w_out", "ffn2_up", "ffn2_down"]


def run_cores(inputs, n_cores=8, n_layers=DEPTH, stop_after=None, trace=False, stop_at=None, nb=1):
    nc = build_program(n_layers, stop_after, stop_at, nb)
    hc = host_consts()
    shared = {"rel_bias": np.ascontiguousarray(inputs["rel_bias"], dtype=np.float32)}
    for k in WNAMES:
        if k in BIGW:
            for li in range(n_layers):
                shared["%s_%d" % (k, li)] = np.ascontiguousarray(inputs[k][li])
        else:
            shared[k] = np.ascontiguousarray(inputs[k][:n_layers])
    for k, v in hc.items():
        shared["c_" + k] = v
    in_maps = []
    for c in range(n_cores):
        m = dict(shared)
        m["x"] = np.ascontiguousarray(inputs["x"][c * nb:(c + 1) * nb]).reshape(nb * S, D)
        m["mem"] = np.ascontiguousarray(inputs["mem"][c * nb:(c + 1) * nb]).reshape(nb * MEM, D)
        m["positions"] = np.ascontiguousarray(inputs["positions"][c * nb:(c + 1) * nb]).astype(np.int32)
        in_maps.append(m)
    res = run_bass_kernel_spmd(nc, in_maps, core_ids=list(range(n_cores)), trace=trace)
    outs = np.concatenate([np.asarray(r["out"]).reshape(nb, S, D) for r in res.results], axis=0)
    return outs, res


N_CORES = 8
N_BATCH_PER_CORE = 1


def kernel(**inputs):
    outs, _ = run_cores(inputs, N_CORES, DEPTH, nb=N_BATCH_PER_CORE)
    return outs.astype(np.float32)
```

```python
import math
from contextlib import ExitStack

import numpy as np
import concourse.bass as bass
import concourse.mybir as mybir
from concourse.bass_utils import run_bass_kernel_spmd

F32 = mybir.dt.float32
BF16 = mybir.dt.bfloat16
I32 = mybir.dt.int32
AF = mybir.ActivationFunctionType
ALU = mybir.AluOpType

D = 2048
S = 2048
DEPTH = 4
DFF = 5632
NFC = DFF // 128
MEM = 256
A_H = 6
I_H = 16
I_D = 64
B_H = 6
C_H = 4
TOPK = 256
ALPHA = (2 * DEPTH) ** 0.25
LN_EPS = 1e-5
RMS_EPS = 1e-6
NEG = -1.0e30
O_AQ, O_AK, O_AV, O_IQ, O_IK, O_IW, O_CQ, O_CKV, O_KR, O_XQ, O_G = 0, 768, 896, 1024, 2048, 2112, 2128, 2640, 3152, 3216, 3728
IN_COLS = 9872


class Sched:
    def __init__(self, nc, es):
        self.nc = nc
        self.eng = {"pe": nc.tensor, "act": nc.scalar, "dve": nc.vector, "pool": nc.gpsimd, "sp": nc.sync}
        self.es = es
        self.semobj = {}
        self.cnt = {}
        for e in ("pe", "act", "dve", "pool"):
            self.semobj[e] = es.enter_context(nc.semaphore("s_" + e))
            self.cnt[e] = 0
        self.seen = {e: {} for e in self.eng}
        self.lw = {}
        self.rd = {}
        self.n_ops = 0
        self.dead = False

    def _deps(self, reads, writes):
        deps = {}
        def add(t):
            if t is not None and deps.get(t[0], 0) < t[1]:
                deps[t[0]] = t[1]
        for r in reads:
            add(self.lw.get(r))
        for w in writes:
            add(self.lw.get(w))
            for t in self.rd.get(w, ()):
                add(t)
        return deps

    def _wait(self, eng, deps):
        e = self.eng[eng]
        seen = self.seen[eng]
        for key, val in deps.items():
            if key == eng and eng == "pe":
                continue
            if seen.get(key, 0) >= val:
                continue
            e.wait_ge(self.semobj[key], val)
            seen[key] = val

    def _commit(self, tok, reads, writes):
        for r in reads:
            self.rd.setdefault(r, []).append(tok)
        for w in writes:
            self.lw[w] = tok
            self.rd[w] = []

    def op(self, eng, fn, reads=(), writes=()):
        if self.dead:
            return
        self._wait(eng, self._deps(reads, writes))
        ins = fn(self.eng[eng])
        self.cnt[eng] += 1
        ins.then_inc(self.semobj[eng], 1)
        self._commit((eng, self.cnt[eng]), reads, writes)
        self.n_ops += 1

    def group(self, eng, fns, reads=(), writes=()):
        if self.dead:
            return
        self._wait(eng, self._deps(reads, writes))
        ins = None
        for fn in fns:
            ins = fn(self.eng[eng])
        self.cnt[eng] += 1
        ins.then_inc(self.semobj[eng], 1)
        self._commit((eng, self.cnt[eng]), reads, writes)
        self.n_ops += len(fns)

    def dma(self, q, out, in_, reads=(), writes=()):
        if self.dead:
            return
        key = "d:" + writes[0]
        if key not in self.semobj:
            self.semobj[key] = self.es.enter_context(self.nc.semaphore("d_" + writes[0]))
            self.cnt[key] = 0
        self._wait(q, self._deps(reads, writes))
        self.eng[q].dma_start(out=out, in_=in_).then_inc(self.semobj[key], 16)
        self.cnt[key] += 16
        self._commit((key, self.cnt[key]), reads, writes)
        self.n_ops += 1

    def barrier(self):
        if self.dead:
            return
        allk = {k: v for k, v in self.cnt.items() if v > 0}
        for e in self.eng:
            self._wait(e, {k: v for k, v in allk.items() if not (k == e and e == "pe")})
        self.lw.clear()
        self.rd.clear()

    def finish(self):
        self.dead = False
        self._wait("sp", {k: v for k, v in self.cnt.items() if v > 0})


class Ctx:
    pass


class StopBuild(Exception):
    pass


def ck(cx, name):
    if cx.stop_at == name:
        cx.sc.barrier()
        cx.sc.dead = True


def bucket_thresholds():
    n = np.arange(0, 4096)
    max_exact = 16
    nf = np.maximum(n, 1).astype(np.float32)
    large = max_exact + (np.log(nf / np.float32(max_exact)) / np.float32(math.log(128 / max_exact))
                         * np.float32(32 - max_exact)).astype(np.int32)
    large = np.minimum(large, 31)
    b = np.where(n < max_exact, n, large)
    return [int(np.min(n[b >= j])) for j in range(1, 32)]


def host_consts():
    c = {}
    c["ident"] = np.eye(128, dtype=np.float32)
    sl = np.arange(128)[:, None, None]
    r = np.arange(2)[None, :, None]
    tl = np.arange(256)[None, None, :]
    c["cm"] = (tl - sl >= 128 * r).astype(np.float32)
    t = np.arange(128)[:, None]
    s = np.arange(128)[None, :]
    c["negtri"] = np.where(s <= t, 0.0, NEG).astype(np.float32)
    m = np.arange(512)[None, :]
    c["dist"] = (m - 128 - np.arange(128)[:, None]).astype(np.float32)
    inv = (10000.0 ** (-np.arange(0, 64, 2, dtype=np.float32) / 64)).astype(np.float32)
    c["invf"] = np.concatenate([inv, inv])[:, None].astype(np.float32)
    c["sgn"] = np.concatenate([-np.ones(32), np.ones(32)])[:, None].astype(np.float32)
    return c


def build_program(n_layers=DEPTH, stop_after=None, stop_at=None, nb=1):
    nc = bass.Bass("TRN2", target_bir_lowering=False)
    T = {}
    def din(name, shape, dt=F32):
        T[name] = nc.dram_tensor(name, list(shape), dt, kind="ExternalInput").ap()
        return T[name]
    din("x", [nb * S, D]); din("mem", [nb * MEM, D]); din("positions", [nb, S], I32); din("rel_bias", [32, A_H])
    NL = n_layers
    din("ln_g", [NL, 3, D]); din("ln_b", [NL, 3, D])
    din("q_norm", [NL, 512]); din("kv_norm", [NL, 512])
    for nm, shp in (("ffn1_up", [D, 2 * DFF]), ("ffn1_down", [DFF, D]), ("w_in", [D, IN_COLS]), ("w_uq", [512, 1152]),
                    ("w_ukv", [512, 1536]), ("w_mem_kv", [D, 1024]), ("w_branch", [D, D]), ("w_out", [D, D]),
                    ("ffn2_up", [D, 2 * DFF]), ("ffn2_down", [DFF, D])):
        T[nm] = [nc.dram_tensor("%s_%d" % (nm, li), shp, F32, kind="ExternalInput").ap() for li in range(NL)]
    hc = host_consts()
    for k, v in hc.items():
        din("c_" + k, v.shape)
    out = nc.dram_tensor("out", [nb * S, D], F32, kind="ExternalOutput").ap()
    X = nc.dram_tensor("Xs", [S, D], F32, kind="Internal").ap()

    with ExitStack() as es:
        sc = Sched(nc, es)
        cx = Ctx()
        cx.nc, cx.sc, cx.T, cx.X, cx.out = nc, sc, T, X, out
        cx.stop_at = stop_at
        cx.ident = es.enter_context(nc.sbuf_tensor("ident", [128, 128], BF16))
        sc.dma("pool", cx.ident[:], T["c_ident"][:, :], writes=["ident"])
        setup_globals(cx, es)
        sc.barrier()
        cx.dram_cache = {}
        cx.bg = []
        cx.cvt = CVT
        cx.Wb = {}
        if CVT:
            for nm in BIGW:
                shp = [NL] + list(T[nm][0].shape)
                cx.Wb[nm] = nc.dram_tensor(nm + "_bf", shp, BF16, kind="Internal").ap()
            convert_layer(cx, 0, now=True)
        stages = []
        for l in range(n_layers):
            stages.append(("ffn", l, 1))
            stages.append(("mix", l, 0))
            stages.append(("ffn", l, 2))
        for b in range(nb):
            cx.b = b
            if sc.dead:
                break
            setup_rope(cx, b)
            xin = T["x"][b * S:(b + 1) * S, :]
            xout = out[b * S:(b + 1) * S, :]
            first = True
            for si, (kind, l, which) in enumerate(stages):
                if stop_at == "globals":
                    break
                last = si == len(stages) - 1 or (stop_after is not None and si == stop_after)
                src = xin if first else X
                dst = xout if last else X
                if CVT and b == 0 and kind == "ffn" and which == 1 and l + 1 < n_layers:
                    convert_layer(cx, l + 1, now=False)
                if kind == "ffn":
                    ffn_phase(cx, l, which, src, dst)
                else:
                    mix_phase(cx, l, src, dst)
                first = False
                sc.barrier()
                if last:
                    break
        sc.finish()
    return nc


def ln_tile(cx, y, yreg, gt, bt, st, streg, junk, junkreg, greg="lng", breg="lnb"):
    sc = cx.sc
    s1, nm, s2, rs = st[:, 0:1], st[:, 1:2], st[:, 2:3], st[:, 3:4]
    sc.op("dve", lambda e: e.memset(st[:, 0:4], 0.0), writes=[streg])
    sc.op("act", lambda e: e.activation(out=junk, in_=y, func=AF.Identity, accum_out=s1),
          reads=[yreg], writes=[junkreg, streg])
    sc.op("dve", lambda e: e.tensor_scalar(out=nm, in0=s1, scalar1=-1.0 / D, scalar2=None, op0=ALU.mult),
          reads=[streg], writes=[streg])
    sc.op("act", lambda e: e.activation(out=junk, in_=y, func=AF.Square, bias=nm, scale=1.0, accum_out=s2),
          reads=[yreg, streg], writes=[junkreg, streg])
    sc.op("dve", lambda e: e.tensor_scalar(out=rs, in0=s2, scalar1=1.0 / D, scalar2=LN_EPS, op0=ALU.mult, op1=ALU.add),
          reads=[streg], writes=[streg])
    sc.op("act", lambda e: e.activation(out=rs, in_=rs, func=AF.Sqrt), reads=[streg], writes=[streg])
    sc.op("dve", lambda e: e.reciprocal(out=rs, in_=rs), reads=[streg], writes=[streg])
    sc.op("dve", lambda e: e.tensor_scalar(out=y, in0=y, scalar1=nm, scalar2=rs, op0=ALU.add, op1=ALU.mult),
          reads=[yreg, streg], writes=[yreg])
    sc.op("dve", lambda e: e.tensor_tensor(out=y, in0=y, in1=gt, op=ALU.mult), reads=[yreg, greg], writes=[yreg])
    sc.op("dve", lambda e: e.tensor_tensor(out=y, in0=y, in1=bt, op=ALU.add), reads=[yreg, breg], writes=[yreg])


def load_x_chunk(cx, src, r0, ntile, xt, xb, xT, tp, alpha_scale=True):
    sc = cx.sc
    sc.dma("sp", xt[:, :, :], src[r0:r0 + 128 * ntile, :].rearrange("(t p) d -> p t d", p=128),
           reads=["X%d" % (r0 // 256 + i) for i in range(max(1, ntile // 2))] if src is cx.X else [], writes=["xt"])
    for t in range(ntile):
        b = xb[0]
        breg = "xb0"
        sc.op("dve", lambda e: e.tensor_copy(out=b[:, :], in_=xt[:, t, :]), reads=["xt"], writes=[breg])
        if alpha_scale:
            sc.op("act", lambda e: e.mul(out=xt[:, t, :], in_=xt[:, t, :], mul=ALPHA), reads=["xt", breg], writes=["xt"])
        for h in range(2):
            p = tp[h]
            preg = "tp%d" % h
            sc.group("pe", [(lambda e, kk=kk: e.transpose(out=p[:, kk * 128:(kk + 1) * 128],
                                                          in_=b[:, (h * 8 + kk) * 128:(h * 8 + kk + 1) * 128],
                                                          identity=cx.ident[:, :])) for kk in range(8)],
                     reads=[breg, "ident"], writes=[preg])
            sc.op("act" if h == 0 else "dve",
                  (lambda e: e.activation(out=xT[:, h * 8:(h + 1) * 8, t * 128:(t + 1) * 128],
                                          in_=p[:, :].rearrange("p (k n) -> p k n", k=8), func=AF.Copy)) if h == 0 else
                  (lambda e: e.tensor_copy(out=xT[:, h * 8:(h + 1) * 8, t * 128:(t + 1) * 128],
                                           in_=p[:, :].rearrange("p (k n) -> p k n", k=8))),
                  reads=[preg], writes=["xT"])


def ffn_phase(cx, l, which, src, dst):
    nc, sc, T = cx.nc, cx.sc, cx.T
    wup, wreg = wsrc(cx, "ffn%d_up" % which, l)
    wdn, _ = wsrc(cx, "ffn%d_down" % which, l)
    lni = 0 if which == 1 else 2
    TC = 512
    NT = TC // 128
    with ExitStack() as es:
        def sb(name, shape, dt):
            return es.enter_context(nc.sbuf_tensor("%s_%d_%d_%d" % (name, cx.b, l, which), list(shape), dt))
        def ps(name, shape, dt):
            return es.enter_context(nc.psum_tensor("%s_%d_%d_%d" % (name, cx.b, l, which), list(shape), dt))
        xt = sb("f_xt", [128, NT, D], F32)
        xb = [sb("f_xb0", [128, D], BF16)]
        xb.append(xb[0])
        xT = sb("f_xT", [128, 16, TC], BF16)
        actT = sb("f_actT", [128, NFC, TC], BF16)
        wu = [sb("f_wu%d" % i, [128, 16, 512], BF16) for i in range(2)]
        wd = [sb("f_wd%d" % i, [128, NFC, 256], BF16) for i in range(2)]
        gt = sb("f_gt", [128, D], F32)
        bt = sb("f_bt", [128, D], F32)
        sg = [sb("f_sg%d" % i, [128, TC], BF16) for i in range(2)]
        st = sb("f_st", [128, 8], F32)
        junk = xb[0]
        tp = [ps("f_tp%d" % i, [128, 1024], BF16) for i in range(2)]
        pG = [ps("f_pG%d" % i, [128, 512], F32) for i in range(2)]
        pU = [ps("f_pU%d" % i, [128, 512], F32) for i in range(2)]
        pD = [ps("f_pD%d" % i, [128, 256], F32) for i in range(2)]

        sc.dma("sp", gt[:, :], T["ln_g"][l, lni:lni + 1, :].partition_broadcast(128), writes=["lng"])
        sc.dma("sp", bt[:, :], T["ln_b"][l, lni:lni + 1, :].partition_broadcast(128), writes=["lnb"])

        def load_wu(j):
            b = wu[j % 2]
            reg = "wu%d" % (j % 2)
            sc.dma("pool", b[:, :, 0:256], wup[:, j * 256:(j + 1) * 256].rearrange("(kc p) n -> p kc n", p=128), reads=wreg, writes=[reg])
            sc.dma("pool", b[:, :, 256:512], wup[:, DFF + j * 256:DFF + (j + 1) * 256].rearrange("(kc p) n -> p kc n", p=128), reads=wreg, writes=[reg])
            bg_step(cx)

        def load_wd(g):
            sc.dma("pool", wd[g % 2][:, :, :], wdn[:, g * 256:(g + 1) * 256].rearrange("(fc p) n -> p fc n", p=128),
                   reads=wreg, writes=["wd%d" % (g % 2)])
            bg_step(cx)

        for c in range(S // TC):
            r0 = c * TC
            load_x_chunk(cx, src, r0, NT, xt, xb, xT, tp)
            if c == 0:
                load_wu(0); load_wu(1)
            load_wd(0); load_wd(1)
            for j in range(NFC // 2):
                b = wu[j % 2]
                reg = "wu%d" % (j % 2)
                for jj in range(2):
                    f = 2 * j + jj
                    q = f % 2
                    sc.group("pe", [(lambda e, kc=kc: e.matmul(pG[q][:, :], b[:, kc, jj * 128:(jj + 1) * 128], xT[:, kc, :],
                                                               start=(kc == 0), stop=(kc == 15))) for kc in range(16)],
                             reads=[reg, "xT"], writes=["pG%d" % q])
                    sc.group("pe", [(lambda e, kc=kc: e.matmul(pU[q][:, :], b[:, kc, 256 + jj * 128:256 + (jj + 1) * 128], xT[:, kc, :],
                                                               start=(kc == 0), stop=(kc == 15))) for kc in range(16)],
                             reads=[reg, "xT"], writes=["pU%d" % q])
                    sc.op("act", lambda e: e.activation(out=sg[q][:, :], in_=pG[q][:, :], func=AF.Silu),
                          reads=["pG%d" % q], writes=["sg%d" % q])
                    sc.op("dve", lambda e: e.tensor_tensor(out=actT[:, f, :], in0=sg[q][:, :], in1=pU[q][:, :], op=ALU.mult),
                          reads=["sg%d" % q, "pU%d" % q], writes=["actT"])
                if j + 2 < NFC // 2:
                    load_wu(j + 2)
            for g in range(D // 256):
                b = wd[g % 2]
                reg = "wd%d" % (g % 2)
                for t in range(NT):
                    q = (g * NT + t) % 2
                    sc.group("pe", [(lambda e, f=f: e.matmul(pD[q][:, :], actT[:, f, t * 128:(t + 1) * 128], b[:, f, :],
                                                             start=(f == 0), stop=(f == NFC - 1))) for f in range(NFC)],
                             reads=[reg, "actT"], writes=["pD%d" % q])
                    sc.op("dve", lambda e: e.scalar_tensor_tensor(out=xt[:, t, g * 256:(g + 1) * 256], in0=pD[q][:, :], scalar=0.5,
                                                                  in1=xt[:, t, g * 256:(g + 1) * 256], op0=ALU.mult, op1=ALU.add),
                          reads=["pD%d" % q, "xt"], writes=["xt"])
                if g + 2 < D // 256:
                    load_wd(g + 2)
            if c + 1 < S // TC:
                load_wu(0); load_wu(1)
            for t in range(NT):
                ln_tile(cx, xt[:, t, :], "xt", gt[:, :], bt[:, :], st, "st", junk[:, :], "xb0")
            sc.dma("sp", dst[r0:r0 + TC, :].rearrange("(t p) d -> p t d", p=128), xt[:, :, :],
                   reads=["xt"], writes=["X%d" % (r0 // 256), "X%d" % (r0 // 256 + 1)])


CVT = True
BIGW = ["ffn1_up", "ffn1_down", "w_in", "w_uq", "w_ukv", "w_mem_kv", "w_branch", "w_out", "ffn2_up", "ffn2_down"]


def convert_layer(cx, l, now):
    T, sc = cx.T, cx.sc
    jobs = []
    first_ids = set()
    for nm in BIGW:
        src = T[nm][l]
        dstw = cx.Wb[nm][l]
        R = src.shape[0]
        for r0 in range(0, R, 128):
            jobs.append((dstw[r0:r0 + 128, :], src[r0:r0 + 128, :], l))
            if now and nm.startswith("ffn1"):
                first_ids.add(id(jobs[-1][0]))
    if now:
        for (o, i, ll) in jobs:
            sc.dma("pool", o, i, writes=["cv_L0a" if id(o) in first_ids else "cv_L%d" % ll])
    else:
        cx.bg.extend(jobs)


def bg_step(cx, n=1):
    for _ in range(n):
        if cx.bg:
            o, i, ll = cx.bg.pop(0)
            cx.sc.dma("pool", o, i, writes=["cv_L%d" % ll])


def wsrc(cx, nm, l):
    if cx.cvt:
        if l == 0 and nm.startswith("ffn1"):
            return cx.Wb[nm][l], ["cv_L0a"]
        return cx.Wb[nm][l], ["cv_L%d" % l]
    return cx.T[nm][l], []


def setup_globals(cx, es):
    nc, sc, T = cx.nc, cx.sc, cx.T
    def gsb(name, shape, dt):
        return es.enter_context(nc.sbuf_tensor(name, list(shape), dt))
    cx.ones = gsb("ones", [128, 128], BF16)
    sc.op("dve", lambda e: e.memset(cx.ones[:, :], 1.0), writes=["ones"])
    cx.cm = gsb("cm", [128, 2, 256], BF16)
    sc.dma("pool", cx.cm[:, :, :], T["c_cm"][:, :, :], writes=["cm"])
    cx.negtri = gsb("negtri", [128, 128], F32)
    sc.dma("sp", cx.negtri[:, :], T["c_negtri"][:, :], writes=["negtri"])
    cx.thrlow = gsb("thrlow", [128, 1], F32)
    sc.op("dve", lambda e: e.memset(cx.thrlow[:, :], -1.0e29), writes=["thrlow"])
    cx.CS = gsb("ropeCS", [64, S], BF16)
    cx.SN = gsb("ropeSN", [64, S], BF16)
    cx.TH = gsb("ebias", [128, A_H, 512], BF16)
    cx.rb31 = gsb("rb31", [128, A_H], F32)
    with ExitStack() as ts:
        def tsb(name, shape, dt):
            return ts.enter_context(nc.sbuf_tensor(name, list(shape), dt))
        RB = tsb("g_rb", [128, 32, A_H], F32)
        dRB = tsb("g_drb", [128, 32, A_H], F32)
        dist = tsb("g_dist", [128, 512], F32)
        G = tsb("g_G", [128, 512], F32)
        G2 = tsb("g_G2", [128, 512], F32)
        sc.dma("sp", RB[:, :, :], T["rel_bias"].rearrange("(o j) h -> o j h", o=1).partition_broadcast(128), writes=["RB"])
        sc.dma("sp", dist[:, :], T["c_dist"][:, :], writes=["dist"])
        sc.op("dve", lambda e: e.tensor_tensor(out=dRB[:, 1:32, :], in0=RB[:, 1:32, :], in1=RB[:, 0:31, :], op=ALU.subtract),
              reads=["RB"], writes=["dRB"])
        sc.op("dve", lambda e: e.tensor_copy(out=cx.rb31[:, :], in_=RB[:, 31, :]), reads=["RB"], writes=["rb31"])
        thr = bucket_thresholds()
        for h in range(A_H):
            for j in range(1, 32):
                if j == 1:
                    sc.op("dve", lambda e: e.tensor_scalar(out=G[:, :], in0=dist[:, :], scalar1=float(thr[j - 1]) - 0.5,
                                                           scalar2=dRB[:, j, h:h + 1], op0=ALU.is_ge, op1=ALU.mult),
                          reads=["dist", "dRB"], writes=["G"])
                else:
                    sc.op("dve", lambda e: e.tensor_scalar(out=G2[:, :], in0=dist[:, :], scalar1=float(thr[j - 1]) - 0.5,
                                                           scalar2=dRB[:, j, h:h + 1], op0=ALU.is_ge, op1=ALU.mult),
                          reads=["dist", "dRB"], writes=["G2"])
                    sc.op("dve", lambda e: e.tensor_tensor(out=G[:, :], in0=G[:, :], in1=G2[:, :], op=ALU.add),
                          reads=["G", "G2"], writes=["G"])
            sc.op("act", lambda e: e.activation(out=cx.TH[:, h, :], in_=G[:, :], func=AF.Exp, bias=RB[:, 0, h:h + 1], scale=1.0),
                  reads=["G", "RB"], writes=["ebias"])
        sc.barrier()


def setup_rope(cx, b):
    nc, sc, T = cx.nc, cx.sc, cx.T
    with ExitStack() as ts:
        def tsb(name, shape, dt):
            return ts.enter_context(nc.sbuf_tensor("%s_b%d" % (name, b), list(shape), dt))
        posi = tsb("g_posi", [64, S], I32)
        ang = tsb("g_ang", [64, S], F32)
        tmp = tsb("g_tmp", [64, S], F32)
        invf = tsb("g_invf", [64, 1], F32)
        sgn = tsb("g_sgn", [64, 1], F32)
        sc.dma("sp", posi[:, :], T["positions"][b:b + 1, :].partition_broadcast(64), writes=["posi"])
        sc.dma("sp", invf[:, :], T["c_invf"][:, :], writes=["invf"])
        sc.dma("sp", sgn[:, :], T["c_sgn"][:, :], writes=["sgn"])
        sc.op("dve", lambda e: e.tensor_copy(out=ang[:, :], in_=posi[:, :]), reads=["posi"], writes=["ang"])
        sc.op("dve", lambda e: e.tensor_scalar(out=ang[:, :], in0=ang[:, :], scalar1=invf[:, 0:1], scalar2=None, op0=ALU.mult),
              reads=["ang", "invf"], writes=["ang"])
        TWO_PI = 2 * math.pi
        def sin_of(shift, out_ap, out_reg, post_sgn):
            sc.op("dve", lambda e: e.tensor_scalar(out=tmp[:, :], in0=ang[:, :], scalar1=shift, scalar2=1.0 / TWO_PI, op0=ALU.add, op1=ALU.mult),
                  reads=["ang"], writes=["tmp"])
            sc.op("dve", lambda e: e.tensor_copy(out=posi[:, :], in_=tmp[:, :]), reads=["tmp"], writes=["posi"])
            sc.op("dve", lambda e: e.tensor_copy(out=tmp[:, :], in_=posi[:, :]), reads=["posi"], writes=["tmp"])
            sc.op("dve", lambda e: e.scalar_tensor_tensor(out=tmp[:, :], in0=tmp[:, :], scalar=-TWO_PI, in1=ang[:, :], op0=ALU.mult, op1=ALU.add),
                  reads=["tmp", "ang"], writes=["tmp"])
            sc.op("dve", lambda e: e.tensor_scalar(out=tmp[:, :], in0=tmp[:, :], scalar1=shift, scalar2=None, op0=ALU.add),
                  reads=["tmp"], writes=["tmp"])
            sc.op("dve", lambda e: e.tensor_scalar(out=tmp2[:, :], in0=tmp[:, :], scalar1=math.pi, scalar2=-TWO_PI, op0=ALU.is_gt, op1=ALU.mult),
                  reads=["tmp"], writes=["tmp2"])
            sc.op("dve", lambda e: e.tensor_tensor(out=tmp[:, :], in0=tmp[:, :], in1=tmp2[:, :], op=ALU.add), reads=["tmp", "tmp2"], writes=["tmp"])
            sc.op("dve", lambda e: e.tensor_scalar(out=tmp2[:, :], in0=tmp[:, :], scalar1=-math.pi, scalar2=TWO_PI, op0=ALU.is_lt, op1=ALU.mult),
                  reads=["tmp"], writes=["tmp2"])
            sc.op("dve", lambda e: e.tensor_tensor(out=tmp[:, :], in0=tmp[:, :], in1=tmp2[:, :], op=ALU.add), reads=["tmp", "tmp2"], writes=["tmp"])
            if post_sgn:
                sc.op("act", lambda e: e.activation(out=tmp[:, :], in_=tmp[:, :], func=AF.Sin), reads=["tmp"], writes=["tmp"])
                sc.op("dve", lambda e: e.tensor_scalar(out=out_ap, in0=tmp[:, :], scalar1=sgn[:, 0:1], scalar2=None, op0=ALU.mult),
                      reads=["tmp", "sgn"], writes=[out_reg])
            else:
                sc.op("act", lambda e: e.activation(out=out_ap, in_=tmp[:, :], func=AF.Sin), reads=["tmp"], writes=[out_reg])
        tmp2 = tsb("g_tmp2", [64, S], F32)
        sin_of(0.0, cx.SN[:, :], "ropeSN", True)
        sin_of(0.5 * math.pi, cx.CS[:, :], "ropeCS", False)
        sc.barrier()


def mix_phase(cx, l, src, dst):
    nc, sc, T = cx.nc, cx.sc, cx.T
    TC = 256
    NT = 2
    NCH = S // TC
    if "KN" not in cx.dram_cache:
        cx.dram_cache["KN"] = nc.dram_tensor("KNs", [B_H, 128, S], BF16, kind="Internal").ap()
        cx.dram_cache["VB"] = nc.dram_tensor("VBs", [B_H, S, 128], BF16, kind="Internal").ap()
    KN = cx.dram_cache["KN"]
    VBd = cx.dram_cache["VB"]
    uid = [0]
    def alloc(stack, name, shape, dt, psum=False):
        uid[0] += 1
        nm = "m%d_%d_%s_%d" % (cx.b, l, name, uid[0])
        if psum:
            return stack.enter_context(nc.psum_tensor(nm, list(shape), dt))
        return stack.enter_context(nc.sbuf_tensor(nm, list(shape), dt))
    with ExitStack() as es:
        sb = lambda name, shape, dt: alloc(es, name, shape, dt)
        ps = lambda name, shape, dt: alloc(es, name, shape, dt, True)
        KA_T = sb("KA_T", [128, S], BF16)
        VA = sb("VA", [128, S // 128, 128], BF16)
        IK_T2 = sb("IK_T2", [128, S], BF16)
        KR_T = sb("KR_T", [64, S], BF16)
        MK_T = sb("MK_T", [128, C_H, MEM], BF16)
        MV = sb("MV", [128, 2, 512], BF16)
        gq = sb("gq", [128, 4], F32)
        gkv = sb("gkv", [128, 4], F32)
        ones_f = sb("ones_f", [128, 128], F32)
        st = sb("st", [128, 8], F32)
        mx = sb("mx", [128, 16], F32)
        xt = sb("xt", [128, NT, D], F32)
        xT = sb("xT", [128, 16, TC], BF16)
        wbufs = [sb("wb%d" % i, [128, 16, 512], BF16) for i in range(2)]
        QA_T = sb("QA_T", [128, A_H, TC], BF16)
        IQ_T = sb("IQ_T", [128, 8, TC], BF16)
        IW = sb("IW", [128, NT, 16], F32)
        QN_T = sb("QN_T", [128, B_H, TC], BF16)
        QR_T = sb("QR_T", [64, B_H, TC], BF16)
        XQ_T = sb("XQ_T", [128, C_H, TC], BF16)
        OT = sb("OT", [128, 16, TC], BF16)
        MT = sb("MTk", [128, S // 128, TC], BF16)
        tp = ps("tp", [128, 1024], BF16)
        pj = [ps("pj%d" % i, [128, 512], F32) for i in range(2)]
        sT = [ps("sT%d" % i, [128, 512], F32) for i in range(2)]
        oT = ps("oT", [128, 512], F32)
        rsum = ps("rs", [128, 512], F32)
        pr = ps("pr", [128, 512], F32)

        sc.op("dve", lambda e: e.memset(ones_f[:, :], 1.0), writes=["ones_f"])
        for k in range(4):
            sc.dma("sp", gq[:, k:k + 1], T["q_norm"][l, k * 128:(k + 1) * 128].rearrange("(p o) -> p o", o=1), writes=["gq"])
            sc.dma("sp", gkv[:, k:k + 1], T["kv_norm"][l, k * 128:(k + 1) * 128].rearrange("(p o) -> p o", o=1), writes=["gkv"])

        if cx.stop_at == "mixalloc":
            sc.barrier()
            return
        wcount = [0]
        def wload(ranges, wname="w_in", kcn=16):
            i = wcount[0] % len(wbufs)
            wcount[0] += 1
            reg = "wb%d" % i
            wsrc_, wreg = wsrc(cx, wname, l)
            bview = wbufs[i][:, :, :] if kcn == 16 else wbufs[i][:, :, :].rearrange("p a b -> p (a b)").rearrange("p (k n) -> p k n", k=kcn)
            o = 0
            for (c0, c1) in ranges:
                sc.dma("pool", bview[:, :, o:o + (c1 - c0)], wsrc_[:, c0:c1].rearrange("(kc p) n -> p kc n", p=128), reads=wreg, writes=[reg])
                o += c1 - c0
            bg_step(cx)
            return bview, reg

        def evac(eng, out, in_, reads, writes):
            if eng == "act":
                sc.op("act", lambda e: e.activation(out=out, in_=in_, func=AF.Copy), reads=reads, writes=writes)
            else:
                sc.op("dve", lambda e: e.tensor_copy(out=out, in_=in_), reads=reads, writes=writes)

        pjc = [0]
        def proj_fm(bv, reg, col0, ncols, rhs, rhs_regs, kcn=16):
            q = pjc[0] % 2
            pjc[0] += 1
            p = pj[q]
            n = rhs(0).shape[-1]
            sc.group("pe", [(lambda e, kc=kc: e.matmul(p[0:ncols, 0:n], bv[:, kc, col0:col0 + ncols], rhs(kc),
                                                       start=(kc == 0), stop=(kc == kcn - 1))) for kc in range(kcn)],
                     reads=[reg] + rhs_regs, writes=["pj%d" % q])
            return p[0:ncols, 0:n], "pj%d" % q

        def proj_tm(bv, reg, col0, ncols, lhs, lhs_regs, kcn=16):
            q = pjc[0] % 2
            pjc[0] += 1
            p = pj[q]
            sc.group("pe", [(lambda e, kc=kc: e.matmul(p[:, 0:ncols], lhs(kc), bv[:, kc, col0:col0 + ncols],
                                                       start=(kc == 0), stop=(kc == kcn - 1))) for kc in range(kcn)],
                     reads=[reg] + lhs_regs, writes=["pj%d" % q])
            return p[:, 0:ncols], "pj%d" % q

        def transpose_rows(srcb, sreg, dstT, dreg, t):
            for h in range(2):
                sc.group("pe", [(lambda e, kk=kk: e.transpose(out=tp[:, kk * 128:(kk + 1) * 128],
                                                              in_=srcb[:, (h * 8 + kk) * 128:(h * 8 + kk + 1) * 128],
                                                              identity=cx.ident[:, :])) for kk in range(8)],
                         reads=[sreg, "ident"], writes=["tp"])
                evac("act" if h == 0 else "dve", dstT[:, h * 8:(h + 1) * 8, t * 128:(t + 1) * 128],
                     tp[:, :].rearrange("p (k n) -> p k n", k=8), ["tp"], [dreg])

        with ExitStack() as ms:
            memt = alloc(ms, "memt", [128, 2, D], F32)
            memb = alloc(ms, "memb", [128, D], BF16)
            memT = alloc(ms, "memT", [128, 16, MEM], BF16)
            sc.dma("sp", memt[:, :, :], T["mem"][cx.b * MEM:(cx.b + 1) * MEM, :].rearrange("(t p) d -> p t d", p=128), writes=["memt"])
            for t in range(2):
                sc.op("dve", lambda e: e.tensor_copy(out=memb[:, :], in_=memt[:, t, :]), reads=["memt"], writes=["memb"])
                transpose_rows(memb, "memb", memT, "memT", t)
            if cx.stop_at == "memT":
                sc.barrier()
                return
            bv, reg = wload([(0, 512)], "w_mem_kv")
            if cx.stop_at == "memW":
                sc.barrier()
                return
            for h in range(C_H):
                p, preg = proj_fm(bv, reg, h * 128, 128, lambda kc: memT[:, kc, :], ["memT"])
                evac("act" if h % 2 else "dve", MK_T[:, h, :], p, [preg], ["MK_T"])
            if cx.stop_at == "mk":
                sc.barrier()
                return
            bv, reg = wload([(512, 1024)], "w_mem_kv")
            for t in range(2):
                p, preg = proj_tm(bv, reg, 0, 512, lambda kc: memT[:, kc, t * 128:(t + 1) * 128], ["memT"])
                evac("act" if t % 2 else "dve", MV[:, t, :], p, [preg], ["MV"])
            sc.barrier()
        if cx.stop_at == "memkv":
            return

        acnt = [0]

        for c in range(NCH):
            r0 = c * TC
            nkt = 2 * c + 2
            xrhs = lambda kc: xT[:, kc, :]
            PIA = ExitStack()
            P = I_ = A_ = PIA
            if True:
                xb = alloc(P, "xb", [128, D], BF16)
                CN = [alloc(P, "CQN", [128, 4, TC], BF16), alloc(P, "CKVN", [128, 4, TC], BF16)]
                cf = alloc(P, "cf", [128, 4, TC], F32)
                sq = alloc(P, "sq", [128, 4, TC], BF16)
                rstd = alloc(P, "rstd", [128, TC], F32)
                rt = [alloc(P, "rt%d" % i, [64, TC], F32) for i in range(2)]
                vst = alloc(P, "vst", [128, 768], BF16)

                sc.dma("sp", xt[:, :, :], src[r0:r0 + TC, :].rearrange("(t p) d -> p t d", p=128),
                       reads=["X%d" % c] if src is cx.X else [], writes=["xt"])
                for t in range(NT):
                    sc.op("dve", lambda e: e.tensor_copy(out=xb[:, :], in_=xt[:, t, :]), reads=["xt"], writes=["xb"])
                    sc.op("act", lambda e: e.mul(out=xt[:, t, :], in_=xt[:, t, :], mul=ALPHA), reads=["xt"], writes=["xt"])
                    transpose_rows(xb, "xb", xT, "xT", t)

                def rope_combine(p1, r1, p2, r2, out, outreg):
                    sc.op("dve", lambda e: e.tensor_tensor(out=rt[0][:, :], in0=p1, in1=cx.CS[:, r0:r0 + TC], op=ALU.mult),
                          reads=[r1, "ropeCS"], writes=["rt0"])
                    sc.op("dve", lambda e: e.tensor_tensor(out=rt[1][:, :], in0=p2, in1=cx.SN[:, r0:r0 + TC], op=ALU.mult),
                          reads=[r2, "ropeSN"], writes=["rt1"])
                    sc.op("dve", lambda e: e.tensor_tensor(out=out, in0=rt[0][:, :], in1=rt[1][:, :], op=ALU.add),
                          reads=["rt0", "rt1"], writes=[outreg])

                ck(cx, "P0")
                bv, reg = wload([(0, 512)])
                for h in range(4):
                    p, preg = proj_fm(bv, reg, h * 128, 128, xrhs, ["xT"])
                    evac("act" if h % 2 else "dve", QA_T[:, h, :], p, [preg], ["QA_T"])
                bv, reg = wload([(512, 1024)])
                for h in range(4, 6):
                    p, preg = proj_fm(bv, reg, (h - 4) * 128, 128, xrhs, ["xT"])
                    evac("act" if h % 2 else "dve", QA_T[:, h, :], p, [preg], ["QA_T"])
                p, preg = proj_fm(bv, reg, 256, 128, xrhs, ["xT"])
                evac("act", KA_T[:, r0:r0 + TC], p, [preg], ["KA_T"])
                for t in range(NT):
                    p, preg = proj_tm(bv, reg, 384, 128, lambda kc: xT[:, kc, t * 128:(t + 1) * 128], ["xT"])
                    evac("dve", VA[:, 2 * c + t, :], p, [preg], ["VA"])
                ck(cx, "P1")
                for half in range(2):
                    bv, reg = wload([(O_IQ + half * 512, O_IQ + (half + 1) * 512)])
                    for k in range(4):
                        p, preg = proj_fm(bv, reg, k * 128, 128, xrhs, ["xT"])
                        evac("act" if k % 2 else "dve", IQ_T[:, half * 4 + k, :], p, [preg], ["IQ_T"])
                ck(cx, "P3")
                bv, reg = wload([(O_IK, O_IK + 64), (O_IK, O_IK + 64), (O_KR, O_KR + 64), (O_KR + 32, O_KR + 64), (O_KR, O_KR + 32),
                                 (O_IW, O_IW + 16)])
                p, preg = proj_fm(bv, reg, 0, 128, xrhs, ["xT"])
                evac("act", IK_T2[:, r0:r0 + TC], p, [preg], ["IK_T2"])
                p1, r1 = proj_fm(bv, reg, 128, 64, xrhs, ["xT"])
                p2, r2 = proj_fm(bv, reg, 192, 64, xrhs, ["xT"])
                rope_combine(p1, r1, p2, r2, KR_T[:, r0:r0 + TC], "KR_T")
                for t in range(NT):
                    p, preg = proj_tm(bv, reg, 256, 16, lambda kc: xT[:, kc, t * 128:(t + 1) * 128], ["xT"])
                    evac("dve", IW[:, t, :], p, [preg], ["IW"])
                def gen_ptail():
                    ck(cx, "P4")
                    for which, (o0, gcol, greg_) in enumerate(((O_CQ, gq, "gq"), (O_CKV, gkv, "gkv"))):
                        yield
                        bv, reg = wload([(o0, o0 + 512)])
                        for k in range(4):
                            yield
                            p, preg = proj_fm(bv, reg, k * 128, 128, xrhs, ["xT"])
                            sc.op("dve", lambda e: e.tensor_copy(out=cf[:, k, :], in_=p), reads=[preg], writes=["cf"])
                            sc.op("dve", lambda e: e.tensor_tensor(out=sq[:, k, :], in0=cf[:, k, :], in1=cf[:, k, :], op=ALU.mult),
                                  reads=["cf"], writes=["sq"])
                        ck(cx, "P5a")
                        sc.group("pe", [(lambda e, k=k: e.matmul(pr[:, 0:TC], cx.ones[:, :], sq[:, k, :], start=(k == 0), stop=(k == 3))) for k in range(4)],
                                 reads=["ones", "sq"], writes=["pr"])
                        sc.op("dve", lambda e: e.tensor_scalar(out=rstd[:, :], in0=pr[:, 0:TC], scalar1=1.0 / 512, scalar2=RMS_EPS,
                                                               op0=ALU.mult, op1=ALU.add), reads=["pr"], writes=["rstd"])
                        ck(cx, "P5b")
                        sc.op("act", lambda e: e.activation(out=rstd[:, :], in_=rstd[:, :], func=AF.Sqrt), reads=["rstd"], writes=["rstd"])
                        sc.op("dve", lambda e: e.reciprocal(out=rstd[:, :], in_=rstd[:, :]), reads=["rstd"], writes=["rstd"])
                        ck(cx, "P5c")
                        for k in range(4):
                            sc.op("dve", lambda e: e.scalar_tensor_tensor(out=CN[which][:, k, :], in0=cf[:, k, :], scalar=gcol[:, k:k + 1],
                                                                          in1=rstd[:, :], op0=ALU.mult, op1=ALU.mult),
                                  reads=["cf", "rstd", greg_], writes=["CN%d" % which])
                    ck(cx, "P6")
                    yield
                    bv, reg = wload([(O_XQ, O_XQ + 512)])
                    for h in range(C_H):
                        yield
                        p, preg = proj_fm(bv, reg, h * 128, 128, xrhs, ["xT"])
                        evac("act" if h % 2 else "dve", XQ_T[:, h, :], p, [preg], ["XQ_T"])
                    rngs = [(0, 1152)]
                    for h in range(B_H):
                        rngs += [(h * 192 + 160, h * 192 + 192), (h * 192 + 128, h * 192 + 160)]
                    yield
                    bv, reg = wload(rngs, "w_uq", kcn=4)
                    qrhs = lambda kc: CN[0][:, kc, :]
                    for h in range(B_H):
                        yield
                        p, preg = proj_fm(bv, reg, h * 192, 128, qrhs, ["CN0"], kcn=4)
                        evac("act", QN_T[:, h, :], p, [preg], ["QN_T"])
                        yield
                        p1, r1 = proj_fm(bv, reg, h * 192 + 128, 64, qrhs, ["CN0"], kcn=4)
                        p2, r2 = proj_fm(bv, reg, 1152 + h * 64, 64, qrhs, ["CN0"], kcn=4)
                        rope_combine(p1, r1, p2, r2, QR_T[:, h, :], "QR_T")
                    ck(cx, "P8")
                    rngs = [(h * 256, h * 256 + 128) for h in range(B_H)] + [(h * 256 + 128, h * 256 + 256) for h in range(B_H)]
                    yield
                    bv, reg = wload(rngs, "w_ukv", kcn=4)
                    krhs = lambda kc: CN[1][:, kc, :]
                    for h in range(B_H):
                        yield
                        p, preg = proj_fm(bv, reg, h * 128, 128, krhs, ["CN1"], kcn=4)
                        hv = (h % 2) * TC
                        evac("act" if h % 2 else "dve", vst[:, hv:hv + TC], p, [preg], ["vstk%d" % (h % 2)])
                        sc.dma("sp", KN[h, :, r0:r0 + TC], vst[:, hv:hv + TC], reads=["vstk%d" % (h % 2)], writes=["KN%d" % h])
                    for t in range(NT):
                        yield
                        p, preg = proj_tm(bv, reg, 768, 512, lambda kc: CN[1][:, kc, t * 128:(t + 1) * 128], ["CN1"], kcn=4)
                        evac("act", vst[:, 0:512], p, [preg], ["vstk0", "vstk1"])
                        yield
                        p, preg = proj_tm(bv, reg, 768 + 512, 256, lambda kc: CN[1][:, kc, t * 128:(t + 1) * 128], ["CN1"], kcn=4)
                        evac("dve", vst[:, 512:768], p, [preg], ["vstv"])
                        for h in range(B_H):
                            sc.dma("sp", VBd[h, r0 + t * 128:r0 + (t + 1) * 128, :], vst[:, h * 128:(h + 1) * 128],
                                   reads=["vstk0", "vstk1", "vstv"], writes=["VB%d" % h])

                    yield
            if True:
                acc = alloc(I_, "acc", [128, S], F32)
                wk = alloc(I_, "wk", [128, S], F32)
                Mk = alloc(I_, "Mk", [128, S], BF16)
                rl = [alloc(I_, "rl%d" % i, [128, 512], F32) for i in range(2)]
                def gen_I():
                    for tt in range(NT):
                        i = 2 * c + tt
                        L = (i + 1) * 128
                        for hh in range(I_H):
                            pb = (hh % 2) * 64
                            for k4 in range((L + 511) // 512):
                                n = min(512, L - k4 * 512)
                                yield
                                q = pjc[0] % 2
                                pjc[0] += 1
                                sc.group("pe", [lambda e: e.matmul(pj[q][:, 0:n], IQ_T[pb:pb + 64, hh // 2, tt * 128:(tt + 1) * 128],
                                                                   IK_T2[pb:pb + 64, k4 * 512:k4 * 512 + n], start=True, stop=True)],
                                         reads=["IQ_T", "IK_T2"], writes=["pj%d" % q])
                                sc.op("act", lambda e: e.activation(out=rl[q][:, 0:n], in_=pj[q][:, 0:n], func=AF.Relu),
                                      reads=["pj%d" % q], writes=["rl%d" % q])
                                a = acc[:, k4 * 512:k4 * 512 + n]
                                if hh == 0:
                                    sc.op("dve", lambda e: e.tensor_scalar(out=a, in0=rl[q][:, 0:n], scalar1=IW[:, tt, hh:hh + 1], scalar2=None,
                                                                           op0=ALU.mult), reads=["rl%d" % q, "IW"], writes=["acc"])
                                else:
                                    sc.op("dve", lambda e: e.scalar_tensor_tensor(out=a, in0=rl[q][:, 0:n], scalar=IW[:, tt, hh:hh + 1], in1=a,
                                                                                  op0=ALU.mult, op1=ALU.add),
                                          reads=["rl%d" % q, "IW", "acc"], writes=["acc"])
                        sc.op("dve", lambda e: e.tensor_tensor(out=acc[:, L - 128:L], in0=acc[:, L - 128:L], in1=cx.negtri[:, :], op=ALU.add),
                              reads=["acc", "negtri"], writes=["acc"])
                        if i >= 2:
                            for r in range(32):
                                srcv = acc if r == 0 else wk
                                sreg = "acc" if r == 0 else "wk"
                                yield
                                sc.op("dve", lambda e: e.max(out=mx[:, 0:8], in_=srcv[:, 0:L]), reads=[sreg], writes=["mx"])
                                if r < 31:
                                    sc.op("dve", lambda e: e.match_replace(out=wk[:, 0:L], in_to_replace=mx[:, 0:8], in_values=srcv[:, 0:L],
                                                                           imm_value=NEG), reads=[sreg, "mx"], writes=["wk"])
                            sc.op("dve", lambda e: e.tensor_reduce(out=mx[:, 8:9], in_=mx[:, 0:8], axis=mybir.AxisListType.X, op=ALU.min),
                                  reads=["mx"], writes=["mx"])
                            thr_ap, thr_reg = mx[:, 8:9], "mx"
                        else:
                            thr_ap, thr_reg = cx.thrlow[:, 0:1], "thrlow"
                        sc.op("dve", lambda e: e.tensor_scalar(out=Mk[:, 0:L], in0=acc[:, 0:L], scalar1=thr_ap, scalar2=None, op0=ALU.is_ge),
                              reads=["acc", thr_reg], writes=["Mk"])
                        for j0 in range(0, i + 1, 8):
                            nj = min(8, i + 1 - j0)
                            yield
                            sc.group("pe", [(lambda e, jj=jj: e.transpose(out=tp[:, jj * 128:(jj + 1) * 128],
                                                                          in_=Mk[:, (j0 + jj) * 128:(j0 + jj + 1) * 128],
                                                                          identity=cx.ident[:, :])) for jj in range(nj)],
                                     reads=["Mk", "ident"], writes=["tp"])
                            evac("act", MT[:, j0:j0 + nj, tt * 128:(tt + 1) * 128], tp[:, 0:nj * 128].rearrange("p (k n) -> p k n", k=nj),
                                 ["tp"], ["MT"])
                        if tt == 0:
                            sc.op("dve", lambda e: e.memset(MT[:, 2 * c + 1, 0:128], 0.0), writes=["MT"])

                    yield
            if True:
                pT = [alloc(A_, "pT%d" % i, [128, TC], BF16) for i in range(2)]
                rc = alloc(A_, "rc", [128, TC], F32)
                KNhs = [alloc(A_, "KNh%d" % i, [128, S], BF16) for i in range(2)]
                VBhs = [alloc(A_, "VBh%d" % i, [128, S // 128, 128], BF16) for i in range(2)]

                def attend(nk, qk_fns, qk_reads, v_of, v_reads, scale, post, out_ap):
                    for j in range(nk):
                        yield
                        q = acnt[0] % 2
                        acnt[0] += 1
                        sc.group("pe", qk_fns(j, sT[q][:, 0:TC]), reads=qk_reads, writes=["sT%d" % q])
                        masks, bias = post(j)
                        if bias is None:
                            sc.op("act", lambda e: e.activation(out=pT[q][:, :], in_=sT[q][:, 0:TC], func=AF.Exp, scale=scale),
                                  reads=["sT%d" % q], writes=["pT%d" % q])
                        else:
                            sc.op("act", lambda e: e.activation(out=pT[q][:, :], in_=sT[q][:, 0:TC], func=AF.Exp, bias=bias, scale=scale),
                                  reads=["sT%d" % q, "rb31"], writes=["pT%d" % q])
                        for (map_, mregs) in masks:
                            sc.op("dve", lambda e: e.tensor_tensor(out=pT[q][:, :], in0=pT[q][:, :], in1=map_, op=ALU.mult),
                                  reads=["pT%d" % q] + mregs, writes=["pT%d" % q])
                        sc.group("pe", [lambda e: e.matmul(oT[:, 0:TC], v_of(j), pT[q][:, :], start=(j == 0), stop=(j == nk - 1)),
                                        lambda e: e.matmul(rsum[:, 0:TC], cx.ones[:, :], pT[q][:, :], start=(j == 0), stop=(j == nk - 1))],
                                 reads=["pT%d" % q, "ones"] + v_reads, writes=["oT", "rsum"])
                    sc.op("dve", lambda e: e.reciprocal(out=rc[:, :], in_=rsum[:, 0:TC]), reads=["rsum"], writes=["rc"])
                    sc.op("dve", lambda e: e.tensor_tensor(out=out_ap, in0=oT[:, 0:TC], in1=rc[:, :], op=ALU.mult),
                          reads=["oT", "rc"], writes=["OT"])
                    yield

                sca = 128 ** -0.5
                scb = 192 ** -0.5
                def gen_BC():
                    for h in range(B_H):
                        KNh, VBh = KNhs[h % 2], VBhs[h % 2]
                        kreg, vreg = "KNh%d" % (h % 2), "VBh%d" % (h % 2)
                        sc.dma("sp", KNh[:, 0:nkt * 128], KN[h, :, 0:nkt * 128], reads=["KN%d" % h], writes=[kreg])
                        sc.dma("sp", VBh[:, 0:nkt, :], VBd[h, 0:nkt * 128, :].rearrange("(t p) d -> p t d", p=128), reads=["VB%d" % h], writes=[vreg])
                        def postb(j):
                            if j >= 2 * c:
                                return [(cx.cm[:, j - 2 * c, :], ["cm"])], None
                            return [], None
                        yield from attend(nkt, lambda j, o, h=h, KNh=KNh: [lambda e: e.matmul(o, KNh[:, j * 128:(j + 1) * 128], QN_T[:, h, :], start=True, stop=False),
                                                       lambda e: e.matmul(o, KR_T[:, j * 128:(j + 1) * 128], QR_T[:, h, :], start=False, stop=True)],
                               [kreg, "KR_T", "QN_T", "QR_T"], lambda j, VBh=VBh: VBh[:, j, :], [vreg], scb, postb, OT[:, 6 + h, :])
                    for h in range(C_H):
                        yield from attend(2, lambda j, o, h=h: [lambda e: e.matmul(o, MK_T[:, h, j * 128:(j + 1) * 128], XQ_T[:, h, :], start=True, stop=True)],
                               ["MK_T", "XQ_T"], lambda j, h=h: MV[:, j, h * 128:(h + 1) * 128], ["MV"], sca, lambda j: ([], None), OT[:, 12 + h, :])
                def run_all(g):
                    for _ in g:
                        pass
                def merge(*gens):
                    gens = list(gens)
                    while gens:
                        for g in list(gens):
                            try:
                                next(g)
                            except StopIteration:
                                gens.remove(g)
                def chain2(a, b):
                    yield from a
                    yield from b
                merge(gen_I(), chain2(gen_ptail(), gen_BC()))
                for h in range(A_H):
                    def post(j, h=h):
                        Dd = r0 - 128 * j
                        if Dd <= 128:
                            return [(cx.TH[:, h, Dd + 128:Dd + 128 + TC], ["ebias"]), (MT[:, j, :], ["MT"])], None
                        return [(MT[:, j, :], ["MT"])], cx.rb31[:, h:h + 1]
                    run_all(attend(nkt, lambda j, o, h=h: [lambda e: e.matmul(o, KA_T[:, j * 128:(j + 1) * 128], QA_T[:, h, :], start=True, stop=True)],
                           ["KA_T", "QA_T"], lambda j: VA[:, j, :], ["VA"], sca, post, OT[:, h, :]))
                sc.barrier()
            PIA.close()

            with ExitStack() as M_:
                wbufs.append(alloc(M_, "wm0", [128, 16, 512], BF16))
                wbufs.append(alloc(M_, "wm1", [128, 16, 512], BF16))
                Mm = alloc(M_, "Mm", [128, 16, TC], BF16)
                mblk = [alloc(M_, "mblk%d" % i, [128, 512], F32) for i in range(NT)]
                mbf = [alloc(M_, "mbf%d" % i, [128, 512], BF16) for i in range(NT)]
                gs = [alloc(M_, "gs%d" % i, [128, 512], F32) for i in range(NT)]
                mtmp = alloc(M_, "mtmp", [128, 512], F32)
                gt = alloc(M_, "gt", [128, D], F32)
                bt = alloc(M_, "bt", [128, D], F32)
                junk = alloc(M_, "junk", [128, D], BF16)
                sc.dma("sp", gt[:, :], T["ln_g"][l, 1:2, :].partition_broadcast(128), writes=["lng"])
                sc.dma("sp", bt[:, :], T["ln_b"][l, 1:2, :].partition_broadcast(128), writes=["lnb"])
                for n4 in range(4):
                    cols = (n4 * 512, (n4 + 1) * 512)
                    blocks = {}
                    blocks["g0"] = wload([(O_G + cols[0], O_G + cols[1])])
                    blocks["br"] = wload([cols], "w_branch")
                    blocks["g1"] = wload([(O_G + D + cols[0], O_G + D + cols[1])])
                    blocks["g2"] = wload([(O_G + 2 * D + cols[0], O_G + 2 * D + cols[1])])
                    wbr, wbr_reg = blocks["br"]
                    for b, (ra, rb_) in enumerate(((0, 6), (6, 12), (12, 16))):
                        bv, reg = blocks["g%d" % b]
                        for t in range(NT):
                            p, preg = proj_tm(bv, reg, 0, 512, lambda kc: xT[:, kc, t * 128:(t + 1) * 128], ["xT"])
                            g_ = gs[t]
                            greg_ = "gs%d" % t
                            sc.op("act", lambda e: e.activation(out=g_[:, :], in_=p, func=AF.Sigmoid), reads=[preg], writes=[greg_])
                            q = acnt[0] % 2
                            acnt[0] += 1
                            sc.group("pe", [(lambda e, r=r: e.matmul(sT[q][:, 0:512], OT[:, r, t * 128:(t + 1) * 128], wbr[:, r, :],
                                                                     start=(r == ra), stop=(r == rb_ - 1))) for r in range(ra, rb_)],
                                     reads=[wbr_reg, "OT"], writes=["sT%d" % q])
                            if b == 0:
                                sc.op("dve", lambda e: e.tensor_tensor(out=mblk[t][:, :], in0=g_[:, :], in1=sT[q][:, 0:512], op=ALU.mult),
                                      reads=[greg_, "sT%d" % q], writes=["mblk%d" % t])
                            else:
                                sc.op("dve", lambda e: e.tensor_tensor(out=mtmp[:, :], in0=g_[:, :], in1=sT[q][:, 0:512], op=ALU.mult),
                                      reads=[greg_, "sT%d" % q], writes=["mtmp"])
                                if b == 1:
                                    sc.op("dve", lambda e: e.tensor_tensor(out=mblk[t][:, :], in0=mblk[t][:, :], in1=mtmp[:, :], op=ALU.add),
                                          reads=["mblk%d" % t, "mtmp"], writes=["mblk%d" % t])
                                else:
                                    sc.op("dve", lambda e: e.tensor_tensor(out=mbf[t][:, :], in0=mblk[t][:, :], in1=mtmp[:, :], op=ALU.add),
                                          reads=["mblk%d" % t, "mtmp"], writes=["mbf%d" % t])
                    for t in range(NT):
                        sc.group("pe", [(lambda e, kk=kk: e.transpose(out=tp[:, kk * 128:(kk + 1) * 128], in_=mbf[t][:, kk * 128:(kk + 1) * 128],
                                                                      identity=cx.ident[:, :])) for kk in range(4)],
                                 reads=["mbf%d" % t, "ident"], writes=["tp"])
                        evac("act", Mm[:, n4 * 4:(n4 + 1) * 4, t * 128:(t + 1) * 128], tp[:, 0:512].rearrange("p (k n) -> p k n", k=4),
                             ["tp"], ["Mm"])
                for cg in range(4):
                    bv, reg = wload([(cg * 512, (cg + 1) * 512)], "w_out")
                    for t in range(NT):
                        p, preg = proj_tm(bv, reg, 0, 512, lambda kc: Mm[:, kc, t * 128:(t + 1) * 128], ["Mm"])
                        sc.op("dve", lambda e: e.tensor_tensor(out=xt[:, t, cg * 512:(cg + 1) * 512], in0=xt[:, t, cg * 512:(cg + 1) * 512],
                                                               in1=p, op=ALU.add), reads=[preg, "xt"], writes=["xt"])
                for t in range(NT):
                    ln_tile(cx, xt[:, t, :], "xt", gt[:, :], bt[:, :], st, "st", junk[:, :], "junk")
                sc.dma("sp", dst[r0:r0 + TC, :].rearrange("(t p) d -> p t d", p=128), xt[:, :, :], reads=["xt"], writes=["X%d" % c])
                sc.barrier()
                wbufs.pop()
                wbufs.pop()
            if cx.stop_at == "M":
                return


WNAMES = ["ln_g", "ln_b", "ffn1_up", "ffn1_down", "w_in", "q_norm", "kv_norm", "w_uq", "w_ukv", "w_mem_kv",
          "w_branch", "w_out", "ffn2_up", "ffn2_down"]


def run_cores(inputs, n_cores=8, n_layers=DEPTH, stop_after=None, trace=False, stop_at=None, nb=1):
    nc = build_program(n_layers, stop_after, stop_at, nb)
    hc = host_consts()
    shared = {"rel_bias": np.ascontiguousarray(inputs["rel_bias"], dtype=np.float32)}
    for k in WNAMES:
        if k in BIGW:
            for li in range(n_layers):
                shared["%s_%d" % (k, li)] = np.ascontiguousarray(inputs[k][li])
        else:
            shared[k] = np.ascontiguousarray(inputs[k][:n_layers])
    for k, v in hc.items():
        shared["c_" + k] = v
    in_maps = []
    for c in range(n_cores):
        m = dict(shared)
        m["x"] = np.ascontiguousarray(inputs["x"][c * nb:(c + 1) * nb]).reshape(nb * S, D)
        m["mem"] = np.ascontiguousarray(inputs["mem"][c * nb:(c + 1) * nb]).reshape(nb * MEM, D)
        m["positions"] = np.ascontiguousarray(inputs["positions"][c * nb:(c + 1) * nb]).astype(np.int32)
        in_maps.append(m)
    res = run_bass_kernel_spmd(nc, in_maps, core_ids=list(range(n_cores)), trace=trace)
    outs = np.concatenate([np.asarray(r["out"]).reshape(nb, S, D) for r in res.results], axis=0)
    return outs, res


N_CORES = 8
N_BATCH_PER_CORE = 1


def kernel(**inputs):
    outs, _ = run_cores(inputs, N_CORES, DEPTH, nb=N_BATCH_PER_CORE)
    return outs.astype(np.float32)
```

```python
import math
from contextlib import ExitStack

import numpy as np
import concourse.bass as bass
import concourse.mybir as mybir
from concourse.bass_utils import run_bass_kernel_spmd

F32 = mybir.dt.float32
BF16 = mybir.dt.bfloat16
I32 = mybir.dt.int32
AF = mybir.ActivationFunctionType
ALU = mybir.AluOpType

D = 2048
S = 2048
DEPTH = 4
DFF = 5632
NFC = DFF // 128
MEM = 256
A_H = 6
I_H = 16
I_D = 64
B_H = 6
C_H = 4
TOPK = 256
ALPHA = (2 * DEPTH) ** 0.25
LN_EPS = 1e-5
RMS_EPS = 1e-6
NEG = -1.0e30
O_AQ, O_AK, O_AV, O_IQ, O_IK, O_IW, O_CQ, O_CKV, O_KR, O_XQ, O_G = 0, 768, 896, 1024, 2048, 2112, 2128, 2640, 3152, 3216, 3728
IN_COLS = 9872


class Sched:
    def __init__(self, nc, es):
        self.nc = nc
        self.eng = {"pe": nc.tensor, "act": nc.scalar, "dve": nc.vector, "pool": nc.gpsimd, "sp": nc.sync}
        self.es = es
        self.semobj = {}
        self.cnt = {}
        for e in ("pe", "act", "dve", "pool"):
            self.semobj[e] = es.enter_context(nc.semaphore("s_" + e))
            self.cnt[e] = 0
        self.seen = {e: {} for e in self.eng}
        self.lw = {}
        self.rd = {}
        self.n_ops = 0
        self.dead = False

    def _deps(self, reads, writes):
        deps = {}
        def add(t):
            if t is not None and deps.get(t[0], 0) < t[1]:
                deps[t[0]] = t[1]
        for r in reads:
            add(self.lw.get(r))
        for w in writes:
            add(self.lw.get(w))
            for t in self.rd.get(w, ()):
                add(t)
        return deps

    def _wait(self, eng, deps):
        e = self.eng[eng]
        seen = self.seen[eng]
        for key, val in deps.items():
            if key == eng and eng == "pe":
                continue
            if seen.get(key, 0) >= val:
                continue
            e.wait_ge(self.semobj[key], val)
            seen[key] = val

    def _commit(self, tok, reads, writes):
        for r in reads:
            self.rd.setdefault(r, []).append(tok)
        for w in writes:
            self.lw[w] = tok
            self.rd[w] = []

    def op(self, eng, fn, reads=(), writes=()):
        if self.dead:
            return
        self._wait(eng, self._deps(reads, writes))
        ins = fn(self.eng[eng])
        self.cnt[eng] += 1
        ins.then_inc(self.semobj[eng], 1)
        self._commit((eng, self.cnt[eng]), reads, writes)
        self.n_ops += 1

    def group(self, eng, fns, reads=(), writes=()):
        if self.dead:
            return
        self._wait(eng, self._deps(reads, writes))
        ins = None
        for fn in fns:
            ins = fn(self.eng[eng])
        self.cnt[eng] += 1
        ins.then_inc(self.semobj[eng], 1)
        self._commit((eng, self.cnt[eng]), reads, writes)
        self.n_ops += len(fns)

    def dma(self, q, out, in_, reads=(), writes=()):
        if self.dead:
            return
        key = "d:" + writes[0]
        if key not in self.semobj:
            self.semobj[key] = self.es.enter_context(self.nc.semaphore("d_" + writes[0]))
            self.cnt[key] = 0
        self._wait(q, self._deps(reads, writes))
        self.eng[q].dma_start(out=out, in_=in_).then_inc(self.semobj[key], 16)
        self.cnt[key] += 16
        self._commit((key, self.cnt[key]), reads, writes)
        self.n_ops += 1

    def barrier(self):
        if self.dead:
            return
        allk = {k: v for k, v in self.cnt.items() if v > 0}
        for e in self.eng:
            self._wait(e, {k: v for k, v in allk.items() if not (k == e and e == "pe")})
        self.lw.clear()
        self.rd.clear()

    def finish(self):
        self.dead = False
        self._wait("sp", {k: v for k, v in self.cnt.items() if v > 0})


class Ctx:
    pass


class StopBuild(Exception):
    pass


def ck(cx, name):
    if cx.stop_at == name:
        cx.sc.barrier()
        cx.sc.dead = True


def bucket_thresholds():
    n = np.arange(0, 4096)
    max_exact = 16
    nf = np.maximum(n, 1).astype(np.float32)
    large = max_exact + (np.log(nf / np.float32(max_exact)) / np.float32(math.log(128 / max_exact))
                         * np.float32(32 - max_exact)).astype(np.int32)
    large = np.minimum(large, 31)
    b = np.where(n < max_exact, n, large)
    return [int(np.min(n[b >= j])) for j in range(1, 32)]


def host_consts():
    c = {}
    c["ident"] = np.eye(128, dtype=np.float32)
    sl = np.arange(128)[:, None, None]
    r = np.arange(2)[None, :, None]
    tl = np.arange(256)[None, None, :]
    c["cm"] = (tl - sl >= 128 * r).astype(np.float32)
    t = np.arange(128)[:, None]
    s = np.arange(128)[None, :]
    c["negtri"] = np.where(s <= t, 0.0, NEG).astype(np.float32)
    m = np.arange(512)[None, :]
    c["dist"] = (m - 128 - np.arange(128)[:, None]).astype(np.float32)
    inv = (10000.0 ** (-np.arange(0, 64, 2, dtype=np.float32) / 64)).astype(np.float32)
    c["invf"] = np.concatenate([inv, inv])[:, None].astype(np.float32)
    c["sgn"] = np.concatenate([-np.ones(32), np.ones(32)])[:, None].astype(np.float32)
    return c


def build_program(n_layers=DEPTH, stop_after=None, stop_at=None, nb=1):
    nc = bass.Bass("TRN2", target_bir_lowering=False)
    T = {}
    def din(name, shape, dt=F32):
        T[name] = nc.dram_tensor(name, list(shape), dt, kind="ExternalInput").ap()
        return T[name]
    din("x", [nb * S, D]); din("mem", [nb * MEM, D]); din("positions", [nb, S], I32); din("rel_bias", [32, A_H])
    NL = n_layers
    din("ln_g", [NL, 3, D]); din("ln_b", [NL, 3, D])
    din("q_norm", [NL, 512]); din("kv_norm", [NL, 512])
    for nm, shp in (("ffn1_up", [D, 2 * DFF]), ("ffn1_down", [DFF, D]), ("w_in", [D, IN_COLS]), ("w_uq", [512, 1152]),
                    ("w_ukv", [512, 1536]), ("w_mem_kv", [D, 1024]), ("w_branch", [D, D]), ("w_out", [D, D]),
                    ("ffn2_up", [D, 2 * DFF]), ("ffn2_down", [DFF, D])):
        T[nm] = [nc.dram_tensor("%s_%d" % (nm, li), shp, F32, kind="ExternalInput").ap() for li in range(NL)]
    hc = host_consts()
    for k, v in hc.items():
        din("c_" + k, v.shape)
    out = nc.dram_tensor("out", [nb * S, D], F32, kind="ExternalOutput").ap()
    X = nc.dram_tensor("Xs", [S, D], F32, kind="Internal").ap()

    with ExitStack() as es:
        sc = Sched(nc, es)
        cx = Ctx()
        cx.nc, cx.sc, cx.T, cx.X, cx.out = nc, sc, T, X, out
        cx.stop_at = stop_at
        cx.ident = es.enter_context(nc.sbuf_tensor("ident", [128, 128], BF16))
        sc.dma("pool", cx.ident[:], T["c_ident"][:, :], writes=["ident"])
        setup_globals(cx, es)
        sc.barrier()
        cx.dram_cache = {}
        cx.bg = []
        cx.cvt = CVT
        cx.Wb = {}
        if CVT:
            for nm in BIGW:
                shp = [NL] + list(T[nm][0].shape)
                cx.Wb[nm] = nc.dram_tensor(nm + "_bf", shp, BF16, kind="Internal").ap()
            convert_layer(cx, 0, now=True)
        stages = []
        for l in range(n_layers):
            stages.append(("ffn", l, 1))
            stages.append(("mix", l, 0))
            stages.append(("ffn", l, 2))
        for b in range(nb):
            cx.b = b
            if sc.dead:
                break
            setup_rope(cx, b)
            xin = T["x"][b * S:(b + 1) * S, :]
            xout = out[b * S:(b + 1) * S, :]
            first = True
            for si, (kind, l, which) in enumerate(stages):
                if stop_at == "globals":
                    break
                last = si == len(stages) - 1 or (stop_after is not None and si == stop_after)
                src = xin if first else X
                dst = xout if last else X
                if CVT and b == 0 and kind == "ffn" and which == 1 and l + 1 < n_layers:
                    convert_layer(cx, l + 1, now=False)
                if kind == "ffn":
                    ffn_phase(cx, l, which, src, dst)
                else:
                    mix_phase(cx, l, src, dst)
                first = False
                sc.barrier()
                if last:
                    break
        sc.finish()
    return nc


def ln_tile(cx, y, yreg, gt, bt, st, streg, junk, junkreg, greg="lng", breg="lnb"):
    sc = cx.sc
    s1, nm, s2, rs = st[:, 0:1], st[:, 1:2], st[:, 2:3], st[:, 3:4]
    sc.op("dve", lambda e: e.memset(st[:, 0:4], 0.0), writes=[streg])
    sc.op("act", lambda e: e.activation(out=junk, in_=y, func=AF.Identity, accum_out=s1),
          reads=[yreg], writes=[junkreg, streg])
    sc.op("dve", lambda e: e.tensor_scalar(out=nm, in0=s1, scalar1=-1.0 / D, scalar2=None, op0=ALU.mult),
          reads=[streg], writes=[streg])
    sc.op("act", lambda e: e.activation(out=junk, in_=y, func=AF.Square, bias=nm, scale=1.0, accum_out=s2),
          reads=[yreg, streg], writes=[junkreg, streg])
    sc.op("dve", lambda e: e.tensor_scalar(out=rs, in0=s2, scalar1=1.0 / D, scalar2=LN_EPS, op0=ALU.mult, op1=ALU.add),
          reads=[streg], writes=[streg])
    sc.op("act", lambda e: e.activation(out=rs, in_=rs, func=AF.Sqrt), reads=[streg], writes=[streg])
    sc.op("dve", lambda e: e.reciprocal(out=rs, in_=rs), reads=[streg], writes=[streg])
    sc.op("dve", lambda e: e.tensor_scalar(out=y, in0=y, scalar1=nm, scalar2=rs, op0=ALU.add, op1=ALU.mult),
          reads=[yreg, streg], writes=[yreg])
    sc.op("dve", lambda e: e.tensor_tensor(out=y, in0=y, in1=gt, op=ALU.mult), reads=[yreg, greg], writes=[yreg])
    sc.op("dve", lambda e: e.tensor_tensor(out=y, in0=y, in1=bt, op=ALU.add), reads=[yreg, breg], writes=[yreg])


def load_x_chunk(cx, src, r0, ntile, xt, xb, xT, tp, alpha_scale=True):
    sc = cx.sc
    sc.dma("sp", xt[:, :, :], src[r0:r0 + 128 * ntile, :].rearrange("(t p) d -> p t d", p=128),
           reads=["X%d" % (r0 // 256 + i) for i in range(max(1, ntile // 2))] if src is cx.X else [], writes=["xt"])
    for t in range(ntile):
        b = xb[0]
        breg = "xb0"
        sc.op("dve", lambda e: e.tensor_copy(out=b[:, :], in_=xt[:, t, :]), reads=["xt"], writes=[breg])
        if alpha_scale:
            sc.op("act", lambda e: e.mul(out=xt[:, t, :], in_=xt[:, t, :], mul=ALPHA), reads=["xt", breg], writes=["xt"])
        for h in range(2):
            p = tp[h]
            preg = "tp%d" % h
            sc.group("pe", [(lambda e, kk=kk: e.transpose(out=p[:, kk * 128:(kk + 1) * 128],
                                                          in_=b[:, (h * 8 + kk) * 128:(h * 8 + kk + 1) * 128],
                                                          identity=cx.ident[:, :])) for kk in range(8)],
                     reads=[breg, "ident"], writes=[preg])
            sc.op("act" if h == 0 else "dve",
                  (lambda e: e.activation(out=xT[:, h * 8:(h + 1) * 8, t * 128:(t + 1) * 128],
                                          in_=p[:, :].rearrange("p (k n) -> p k n", k=8), func=AF.Copy)) if h == 0 else
                  (lambda e: e.tensor_copy(out=xT[:, h * 8:(h + 1) * 8, t * 128:(t + 1) * 128],
                                           in_=p[:, :].rearrange("p (k n) -> p k n", k=8))),
                  reads=[preg], writes=["xT"])


def ffn_phase(cx, l, which, src, dst):
    nc, sc, T = cx.nc, cx.sc, cx.T
    wup, wreg = wsrc(cx, "ffn%d_up" % which, l)
    wdn, _ = wsrc(cx, "ffn%d_down" % which, l)
    lni = 0 if which == 1 else 2
    TC = 512
    NT = TC // 128
    with ExitStack() as es:
        def sb(name, shape, dt):
            return es.enter_context(nc.sbuf_tensor("%s_%d_%d_%d" % (name, cx.b, l, which), list(shape), dt))
        def ps(name, shape, dt):
            return es.enter_context(nc.psum_tensor("%s_%d_%d_%d" % (name, cx.b, l, which), list(shape), dt))
        xt = sb("f_xt", [128, NT, D], F32)
        xb = [sb("f_xb0", [128, D], BF16)]
        xb.append(xb[0])
        xT = sb("f_xT", [128, 16, TC], BF16)
        actT = sb("f_actT", [128, NFC, TC], BF16)
        wu = [sb("f_wu%d" % i, [128, 16, 512], BF16) for i in range(2)]
        wd = [sb("f_wd%d" % i, [128, NFC, 256], BF16) for i in range(2)]
        gt = sb("f_gt", [128, D], F32)
        bt = sb("f_bt", [128, D], F32)
        sg = [sb("f_sg%d" % i, [128, TC], BF16) for i in range(2)]
        st = sb("f_st", [128, 8], F32)
        junk = xb[0]
        tp = [ps("f_tp%d" % i, [128, 1024], BF16) for i in range(2)]
        pG = [ps("f_pG%d" % i, [128, 512], F32) for i in range(2)]
        pU = [ps("f_pU%d" % i, [128, 512], F32) for i in range(2)]
        pD = [ps("f_pD%d" % i, [128, 256], F32) for i in range(2)]

        sc.dma("sp", gt[:, :], T["ln_g"][l, lni:lni + 1, :].partition_broadcast(128), writes=["lng"])
        sc.dma("sp", bt[:, :], T["ln_b"][l, lni:lni + 1, :].partition_broadcast(128), writes=["lnb"])

        def load_wu(j):
            b = wu[j % 2]
            reg = "wu%d" % (j % 2)
            sc.dma("pool", b[:, :, 0:256], wup[:, j * 256:(j + 1) * 256].rearrange("(kc p) n -> p kc n", p=128), reads=wreg, writes=[reg])
            sc.dma("pool", b[:, :, 256:512], wup[:, DFF + j * 256:DFF + (j + 1) * 256].rearrange("(kc p) n -> p kc n", p=128), reads=wreg, writes=[reg])
            bg_step(cx)

        def load_wd(g):
            sc.dma("pool", wd[g % 2][:, :, :], wdn[:, g * 256:(g + 1) * 256].rearrange("(fc p) n -> p fc n", p=128),
                   reads=wreg, writes=["wd%d" % (g % 2)])
            bg_step(cx)

        for c in range(S // TC):
            r0 = c * TC
            load_x_chunk(cx, src, r0, NT, xt, xb, xT, tp)
            if c == 0:
                load_wu(0); load_wu(1)
            load_wd(0); load_wd(1)
            for j in range(NFC // 2):
                b = wu[j % 2]
                reg = "wu%d" % (j % 2)
                for jj in range(2):
                    f = 2 * j + jj
                    q = f % 2
                    sc.group("pe", [(lambda e, kc=kc: e.matmul(pG[q][:, :], b[:, kc, jj * 128:(jj + 1) * 128], xT[:, kc, :],
                                                               start=(kc == 0), stop=(kc == 15))) for kc in range(16)],
                             reads=[reg, "xT"], writes=["pG%d" % q])
                    sc.group("pe", [(lambda e, kc=kc: e.matmul(pU[q][:, :], b[:, kc, 256 + jj * 128:256 + (jj + 1) * 128], xT[:, kc, :],
                                                               start=(kc == 0), stop=(kc == 15))) for kc in range(16)],
                             reads=[reg, "xT"], writes=["pU%d" % q])
                    sc.op("act", lambda e: e.activation(out=sg[q][:, :], in_=pG[q][:, :], func=AF.Silu),
                          reads=["pG%d" % q], writes=["sg%d" % q])
                    sc.op("dve", lambda e: e.tensor_tensor(out=actT[:, f, :], in0=sg[q][:, :], in1=pU[q][:, :], op=ALU.mult),
                          reads=["sg%d" % q, "pU%d" % q], writes=["actT"])
                if j + 2 < NFC // 2:
                    load_wu(j + 2)
            for g in range(D // 256):
                b = wd[g % 2]
                reg = "wd%d" % (g % 2)
                for t in range(NT):
                    q = (g * NT + t) % 2
                    sc.group("pe", [(lambda e, f=f: e.matmul(pD[q][:, :], actT[:, f, t * 128:(t + 1) * 128], b[:, f, :],
                                                             start=(f == 0), stop=(f == NFC - 1))) for f in range(NFC)],
                             reads=[reg, "actT"], writes=["pD%d" % q])
                    sc.op("dve", lambda e: e.scalar_tensor_tensor(out=xt[:, t, g * 256:(g + 1) * 256], in0=pD[q][:, :], scalar=0.5,
                                                                  in1=xt[:, t, g * 256:(g + 1) * 256], op0=ALU.mult, op1=ALU.add),
                          reads=["pD%d" % q, "xt"], writes=["xt"])
                if g + 2 < D // 256:
                    load_wd(g + 2)
            if c + 1 < S // TC:
                load_wu(0); load_wu(1)
            for t in range(NT):
                ln_tile(cx, xt[:, t, :], "xt", gt[:, :], bt[:, :], st, "st", junk[:, :], "xb0")
            sc.dma("sp", dst[r0:r0 + TC, :].rearrange("(t p) d -> p t d", p=128), xt[:, :, :],
                   reads=["xt"], writes=["X%d" % (r0 // 256), "X%d" % (r0 // 256 + 1)])


CVT = True
BIGW = ["ffn1_up", "ffn1_down", "w_in", "w_uq", "w_ukv", "w_mem_kv", "w_branch", "w_out", "ffn2_up", "ffn2_down"]


def convert_layer(cx, l, now):
    T, sc = cx.T, cx.sc
    jobs = []
    first_ids = set()
    for nm in BIGW:
        src = T[nm][l]
        dstw = cx.Wb[nm][l]
        R = src.shape[0]
        for r0 in range(0, R, 128):
            jobs.append((dstw[r0:r0 + 128, :], src[r0:r0 + 128, :], l))
            if now and nm.startswith("ffn1"):
                first_ids.add(id(jobs[-1][0]))
    if now:
        for (o, i, ll) in jobs:
            if id(o) in first_ids:
                sc.dma("pool", o, i, writes=["cv_L0a"])
            else:
                cx.bg.append((o, i, ll))
    else:
        cx.bg.extend(jobs)


def bg_step(cx, n=1):
    for _ in range(n):
        if cx.bg:
            o, i, ll = cx.bg.pop(0)
            cx.sc.dma("pool", o, i, writes=["cv_L%d" % ll])


def wsrc(cx, nm, l):
    if cx.cvt:
        if l == 0 and nm.startswith("ffn1"):
            return cx.Wb[nm][l], ["cv_L0a"]
        return cx.Wb[nm][l], ["cv_L%d" % l]
    return cx.T[nm][l], []


def setup_globals(cx, es):
    nc, sc, T = cx.nc, cx.sc, cx.T
    def gsb(name, shape, dt):
        return es.enter_context(nc.sbuf_tensor(name, list(shape), dt))
    cx.ones = gsb("ones", [128, 128], BF16)
    sc.op("dve", lambda e: e.memset(cx.ones[:, :], 1.0), writes=["ones"])
    cx.cm = gsb("cm", [128, 2, 256], BF16)
    sc.dma("pool", cx.cm[:, :, :], T["c_cm"][:, :, :], writes=["cm"])
    cx.negtri = gsb("negtri", [128, 128], F32)
    sc.dma("sp", cx.negtri[:, :], T["c_negtri"][:, :], writes=["negtri"])
    cx.thrlow = gsb("thrlow", [128, 1], F32)
    sc.op("dve", lambda e: e.memset(cx.thrlow[:, :], -1.0e29), writes=["thrlow"])
    cx.CS = gsb("ropeCS", [64, S], BF16)
    cx.SN = gsb("ropeSN", [64, S], BF16)
    cx.TH = gsb("ebias", [128, A_H, 512], BF16)
    cx.rb31 = gsb("rb31", [128, A_H], F32)
    with ExitStack() as ts:
        def tsb(name, shape, dt):
            return ts.enter_context(nc.sbuf_tensor(name, list(shape), dt))
        RB = tsb("g_rb", [128, 32, A_H], F32)
        dRB = tsb("g_drb", [128, 32, A_H], F32)
        dist = tsb("g_dist", [128, 512], F32)
        G = tsb("g_G", [128, 512], F32)
        G2 = tsb("g_G2", [128, 512], F32)
        sc.dma("sp", RB[:, :, :], T["rel_bias"].rearrange("(o j) h -> o j h", o=1).partition_broadcast(128), writes=["RB"])
        sc.dma("sp", dist[:, :], T["c_dist"][:, :], writes=["dist"])
        sc.op("dve", lambda e: e.tensor_tensor(out=dRB[:, 1:32, :], in0=RB[:, 1:32, :], in1=RB[:, 0:31, :], op=ALU.subtract),
              reads=["RB"], writes=["dRB"])
        sc.op("dve", lambda e: e.tensor_copy(out=cx.rb31[:, :], in_=RB[:, 31, :]), reads=["RB"], writes=["rb31"])
        thr = bucket_thresholds()
        for h in range(A_H):
            for j in range(1, 32):
                if j == 1:
                    sc.op("dve", lambda e: e.tensor_scalar(out=G[:, :], in0=dist[:, :], scalar1=float(thr[j - 1]) - 0.5,
                                                           scalar2=dRB[:, j, h:h + 1], op0=ALU.is_ge, op1=ALU.mult),
                          reads=["dist", "dRB"], writes=["G"])
                else:
                    sc.op("dve", lambda e: e.tensor_scalar(out=G2[:, :], in0=dist[:, :], scalar1=float(thr[j - 1]) - 0.5,
                                                           scalar2=dRB[:, j, h:h + 1], op0=ALU.is_ge, op1=ALU.mult),
                          reads=["dist", "dRB"], writes=["G2"])
                    sc.op("dve", lambda e: e.tensor_tensor(out=G[:, :], in0=G[:, :], in1=G2[:, :], op=ALU.add),
                          reads=["G", "G2"], writes=["G"])
            sc.op("act", lambda e: e.activation(out=cx.TH[:, h, :], in_=G[:, :], func=AF.Exp, bias=RB[:, 0, h:h + 1], scale=1.0),
                  reads=["G", "RB"], writes=["ebias"])
        sc.barrier()


def setup_rope(cx, b):
    nc, sc, T = cx.nc, cx.sc, cx.T
    with ExitStack() as ts:
        def tsb(name, shape, dt):
            return ts.enter_context(nc.sbuf_tensor("%s_b%d" % (name, b), list(shape), dt))
        posi = tsb("g_posi", [64, S], I32)
        ang = tsb("g_ang", [64, S], F32)
        tmp = tsb("g_tmp", [64, S], F32)
        invf = tsb("g_invf", [64, 1], F32)
        sgn = tsb("g_sgn", [64, 1], F32)
        sc.dma("sp", posi[:, :], T["positions"][b:b + 1, :].partition_broadcast(64), writes=["posi"])
        sc.dma("sp", invf[:, :], T["c_invf"][:, :], writes=["invf"])
        sc.dma("sp", sgn[:, :], T["c_sgn"][:, :], writes=["sgn"])
        sc.op("dve", lambda e: e.tensor_copy(out=ang[:, :], in_=posi[:, :]), reads=["posi"], writes=["ang"])
        sc.op("dve", lambda e: e.tensor_scalar(out=ang[:, :], in0=ang[:, :], scalar1=invf[:, 0:1], scalar2=None, op0=ALU.mult),
              reads=["ang", "invf"], writes=["ang"])
        TWO_PI = 2 * math.pi
        def sin_of(shift, out_ap, out_reg, post_sgn):
            sc.op("dve", lambda e: e.tensor_scalar(out=tmp[:, :], in0=ang[:, :], scalar1=shift, scalar2=1.0 / TWO_PI, op0=ALU.add, op1=ALU.mult),
                  reads=["ang"], writes=["tmp"])
            sc.op("dve", lambda e: e.tensor_copy(out=posi[:, :], in_=tmp[:, :]), reads=["tmp"], writes=["posi"])
            sc.op("dve", lambda e: e.tensor_copy(out=tmp[:, :], in_=posi[:, :]), reads=["posi"], writes=["tmp"])
            sc.op("dve", lambda e: e.scalar_tensor_tensor(out=tmp[:, :], in0=tmp[:, :], scalar=-TWO_PI, in1=ang[:, :], op0=ALU.mult, op1=ALU.add),
                  reads=["tmp", "ang"], writes=["tmp"])
            sc.op("dve", lambda e: e.tensor_scalar(out=tmp[:, :], in0=tmp[:, :], scalar1=shift, scalar2=None, op0=ALU.add),
                  reads=["tmp"], writes=["tmp"])
            sc.op("dve", lambda e: e.tensor_scalar(out=tmp2[:, :], in0=tmp[:, :], scalar1=math.pi, scalar2=-TWO_PI, op0=ALU.is_gt, op1=ALU.mult),
                  reads=["tmp"], writes=["tmp2"])
            sc.op("dve", lambda e: e.tensor_tensor(out=tmp[:, :], in0=tmp[:, :], in1=tmp2[:, :], op=ALU.add), reads=["tmp", "tmp2"], writes=["tmp"])
            sc.op("dve", lambda e: e.tensor_scalar(out=tmp2[:, :], in0=tmp[:, :], scalar1=-math.pi, scalar2=TWO_PI, op0=ALU.is_lt, op1=ALU.mult),
                  reads=["tmp"], writes=["tmp2"])
            sc.op("dve", lambda e: e.tensor_tensor(out=tmp[:, :], in0=tmp[:, :], in1=tmp2[:, :], op=ALU.add), reads=["tmp", "tmp2"], writes=["tmp"])
            if post_sgn:
                sc.op("act", lambda e: e.activation(out=tmp[:, :], in_=tmp[:, :], func=AF.Sin), reads=["tmp"], writes=["tmp"])
                sc.op("dve", lambda e: e.tensor_scalar(out=out_ap, in0=tmp[:, :], scalar1=sgn[:, 0:1], scalar2=None, op0=ALU.mult),
                      reads=["tmp", "sgn"], writes=[out_reg])
            else:
                sc.op("act", lambda e: e.activation(out=out_ap, in_=tmp[:, :], func=AF.Sin), reads=["tmp"], writes=[out_reg])
        tmp2 = tsb("g_tmp2", [64, S], F32)
        sin_of(0.0, cx.SN[:, :], "ropeSN", True)
        sin_of(0.5 * math.pi, cx.CS[:, :], "ropeCS", False)
        sc.barrier()


def mix_phase(cx, l, src, dst):
    nc, sc, T = cx.nc, cx.sc, cx.T
    TC = 256
    NT = 2
    NCH = S // TC
    if "KN" not in cx.dram_cache:
        cx.dram_cache["KN"] = nc.dram_tensor("KNs", [B_H, 128, S], BF16, kind="Internal").ap()
        cx.dram_cache["VB"] = nc.dram_tensor("VBs", [B_H, S, 128], BF16, kind="Internal").ap()
    KN = cx.dram_cache["KN"]
    VBd = cx.dram_cache["VB"]
    uid = [0]
    def alloc(stack, name, shape, dt, psum=False):
        uid[0] += 1
        nm = "m%d_%d_%s_%d" % (cx.b, l, name, uid[0])
        if psum:
            return stack.enter_context(nc.psum_tensor(nm, list(shape), dt))
        return stack.enter_context(nc.sbuf_tensor(nm, list(shape), dt))
    with ExitStack() as es:
        sb = lambda name, shape, dt: alloc(es, name, shape, dt)
        ps = lambda name, shape, dt: alloc(es, name, shape, dt, True)
        KA_T = sb("KA_T", [128, S], BF16)
        VA = sb("VA", [128, S // 128, 128], BF16)
        IK_T2 = sb("IK_T2", [128, S], BF16)
        KR_T = sb("KR_T", [64, S], BF16)
        MK_T = sb("MK_T", [128, C_H, MEM], BF16)
        MV = sb("MV", [128, 2, 512], BF16)
        gq = sb("gq", [128, 4], F32)
        gkv = sb("gkv", [128, 4], F32)
        ones_f = sb("ones_f", [128, 128], F32)
        st = sb("st", [128, 8], F32)
        mx = sb("mx", [128, 16], F32)
        xt = sb("xt", [128, NT, D], F32)
        xT = sb("xT", [128, 16, TC], BF16)
        wbufs = [sb("wb%d" % i, [128, 16, 512], BF16) for i in range(2)]
        QA_T = sb("QA_T", [128, A_H, TC], BF16)
        IQ_T = sb("IQ_T", [128, 8, TC], BF16)
        IW = sb("IW", [128, NT, 16], F32)
        QN_T = sb("QN_T", [128, B_H, TC], BF16)
        QR_T = sb("QR_T", [64, B_H, TC], BF16)
        XQ_T = sb("XQ_T", [128, C_H, TC], BF16)
        OT = sb("OT", [128, 16, TC], BF16)
        MT = sb("MTk", [128, S // 128, TC], BF16)
        Gt = sb("Gt", [128, NT, 3 * D], BF16)
        tp = ps("tp", [128, 1024], BF16)
        pj = [ps("pj%d" % i, [128, 512], F32) for i in range(2)]
        sT = [ps("sT%d" % i, [128, 512], F32) for i in range(2)]
        oT = ps("oT", [128, 512], F32)
        rsum = ps("rs", [128, 512], F32)
        pr = ps("pr", [128, 512], F32)

        sc.op("dve", lambda e: e.memset(ones_f[:, :], 1.0), writes=["ones_f"])
        for k in range(4):
            sc.dma("sp", gq[:, k:k + 1], T["q_norm"][l, k * 128:(k + 1) * 128].rearrange("(p o) -> p o", o=1), writes=["gq"])
            sc.dma("sp", gkv[:, k:k + 1], T["kv_norm"][l, k * 128:(k + 1) * 128].rearrange("(p o) -> p o", o=1), writes=["gkv"])

        if cx.stop_at == "mixalloc":
            sc.barrier()
            return
        wcount = [0]
        def wload(ranges, wname="w_in", kcn=16):
            i = wcount[0] % len(wbufs)
            wcount[0] += 1
            reg = "wb%d" % i
            wsrc_, wreg = wsrc(cx, wname, l)
            bview = wbufs[i][:, :, :] if kcn == 16 else wbufs[i][:, :, :].rearrange("p a b -> p (a b)").rearrange("p (k n) -> p k n", k=kcn)
            o = 0
            for (c0, c1) in ranges:
                sc.dma("pool", bview[:, :, o:o + (c1 - c0)], wsrc_[:, c0:c1].rearrange("(kc p) n -> p kc n", p=128), reads=wreg, writes=[reg])
                o += c1 - c0
            bg_step(cx)
            return bview, reg

        def evac(eng, out, in_, reads, writes):
            if eng == "act":
                sc.op("act", lambda e: e.activation(out=out, in_=in_, func=AF.Copy), reads=reads, writes=writes)
            else:
                sc.op("dve", lambda e: e.tensor_copy(out=out, in_=in_), reads=reads, writes=writes)

        pjc = [0]
        def proj_fm(bv, reg, col0, ncols, rhs, rhs_regs, kcn=16):
            q = pjc[0] % 2
            pjc[0] += 1
            p = pj[q]
            n = rhs(0).shape[-1]
            sc.group("pe", [(lambda e, kc=kc: e.matmul(p[0:ncols, 0:n], bv[:, kc, col0:col0 + ncols], rhs(kc),
                                                       start=(kc == 0), stop=(kc == kcn - 1))) for kc in range(kcn)],
                     reads=[reg] + rhs_regs, writes=["pj%d" % q])
            return p[0:ncols, 0:n], "pj%d" % q

        def proj_tm(bv, reg, col0, ncols, lhs, lhs_regs, kcn=16):
            q = pjc[0] % 2
            pjc[0] += 1
            p = pj[q]
            sc.group("pe", [(lambda e, kc=kc: e.matmul(p[:, 0:ncols], lhs(kc), bv[:, kc, col0:col0 + ncols],
                                                       start=(kc == 0), stop=(kc == kcn - 1))) for kc in range(kcn)],
                     reads=[reg] + lhs_regs, writes=["pj%d" % q])
            return p[:, 0:ncols], "pj%d" % q

        def transpose_rows(srcb, sreg, dstT, dreg, t):
            for h in range(2):
                sc.group("pe", [(lambda e, kk=kk: e.transpose(out=tp[:, kk * 128:(kk + 1) * 128],
                                                              in_=srcb[:, (h * 8 + kk) * 128:(h * 8 + kk + 1) * 128],
                                                              identity=cx.ident[:, :])) for kk in range(8)],
                         reads=[sreg, "ident"], writes=["tp"])
                evac("act" if h == 0 else "dve", dstT[:, h * 8:(h + 1) * 8, t * 128:(t + 1) * 128],
                     tp[:, :].rearrange("p (k n) -> p k n", k=8), ["tp"], [dreg])

        with ExitStack() as ms:
            memt = alloc(ms, "memt", [128, 2, D], F32)
            memb = alloc(ms, "memb", [128, D], BF16)
            memT = alloc(ms, "memT", [128, 16, MEM], BF16)
            sc.dma("sp", memt[:, :, :], T["mem"][cx.b * MEM:(cx.b + 1) * MEM, :].rearrange("(t p) d -> p t d", p=128), writes=["memt"])
            for t in range(2):
                sc.op("dve", lambda e: e.tensor_copy(out=memb[:, :], in_=memt[:, t, :]), reads=["memt"], writes=["memb"])
                transpose_rows(memb, "memb", memT, "memT", t)
            if cx.stop_at == "memT":
                sc.barrier()
                return
            bv, reg = wload([(0, 512)], "w_mem_kv")
            if cx.stop_at == "memW":
                sc.barrier()
                return
            for h in range(C_H):
                p, preg = proj_fm(bv, reg, h * 128, 128, lambda kc: memT[:, kc, :], ["memT"])
                evac("act" if h % 2 else "dve", MK_T[:, h, :], p, [preg], ["MK_T"])
            if cx.stop_at == "mk":
                sc.barrier()
                return
            bv, reg = wload([(512, 1024)], "w_mem_kv")
            for t in range(2):
                p, preg = proj_tm(bv, reg, 0, 512, lambda kc: memT[:, kc, t * 128:(t + 1) * 128], ["memT"])
                evac("act" if t % 2 else "dve", MV[:, t, :], p, [preg], ["MV"])
            sc.barrier()
        if cx.stop_at == "memkv":
            return

        acnt = [0]

        for c in range(NCH):
            r0 = c * TC
            nkt = 2 * c + 2
            xrhs = lambda kc: xT[:, kc, :]
            PIA = ExitStack()
            P = I_ = A_ = PIA
            if True:
                xb = alloc(P, "xb", [128, D], BF16)
                CN = [alloc(P, "CQN", [128, 4, TC], BF16), alloc(P, "CKVN", [128, 4, TC], BF16)]
                cf = alloc(P, "cf", [128, 4, TC], F32)
                sq = alloc(P, "sq", [128, 4, TC], BF16)
                rstd = alloc(P, "rstd", [128, TC], F32)
                rt = [alloc(P, "rt%d" % i, [64, TC], F32) for i in range(2)]
                vst = alloc(P, "vst", [128, 768], BF16)

                sc.dma("sp", xt[:, :, :], src[r0:r0 + TC, :].rearrange("(t p) d -> p t d", p=128),
                       reads=["X%d" % c] if src is cx.X else [], writes=["xt"])
                for t in range(NT):
                    sc.op("dve", lambda e: e.tensor_copy(out=xb[:, :], in_=xt[:, t, :]), reads=["xt"], writes=["xb"])
                    sc.op("act", lambda e: e.mul(out=xt[:, t, :], in_=xt[:, t, :], mul=ALPHA), reads=["xt"], writes=["xt"])
                    transpose_rows(xb, "xb", xT, "xT", t)

                def rope_combine(p1, r1, p2, r2, out, outreg):
                    sc.op("dve", lambda e: e.tensor_tensor(out=rt[0][:, :], in0=p1, in1=cx.CS[:, r0:r0 + TC], op=ALU.mult),
                          reads=[r1, "ropeCS"], writes=["rt0"])
                    sc.op("dve", lambda e: e.tensor_tensor(out=rt[1][:, :], in0=p2, in1=cx.SN[:, r0:r0 + TC], op=ALU.mult),
                          reads=[r2, "ropeSN"], writes=["rt1"])
                    sc.op("dve", lambda e: e.tensor_tensor(out=out, in0=rt[0][:, :], in1=rt[1][:, :], op=ALU.add),
                          reads=["rt0", "rt1"], writes=[outreg])

                ck(cx, "P0")
                bv, reg = wload([(0, 512)])
                for h in range(4):
                    p, preg = proj_fm(bv, reg, h * 128, 128, xrhs, ["xT"])
                    evac("act" if h % 2 else "dve", QA_T[:, h, :], p, [preg], ["QA_T"])
                bv, reg = wload([(512, 1024)])
                for h in range(4, 6):
                    p, preg = proj_fm(bv, reg, (h - 4) * 128, 128, xrhs, ["xT"])
                    evac("act" if h % 2 else "dve", QA_T[:, h, :], p, [preg], ["QA_T"])
                p, preg = proj_fm(bv, reg, 256, 128, xrhs, ["xT"])
                evac("act", KA_T[:, r0:r0 + TC], p, [preg], ["KA_T"])
                for t in range(NT):
                    p, preg = proj_tm(bv, reg, 384, 128, lambda kc: xT[:, kc, t * 128:(t + 1) * 128], ["xT"])
                    evac("dve", VA[:, 2 * c + t, :], p, [preg], ["VA"])
                ck(cx, "P1")
                for half in range(2):
                    bv, reg = wload([(O_IQ + half * 512, O_IQ + (half + 1) * 512)])
                    for k in range(4):
                        p, preg = proj_fm(bv, reg, k * 128, 128, xrhs, ["xT"])
                        evac("act" if k % 2 else "dve", IQ_T[:, half * 4 + k, :], p, [preg], ["IQ_T"])
                ck(cx, "P3")
                bv, reg = wload([(O_IK, O_IK + 64), (O_IK, O_IK + 64), (O_KR, O_KR + 64), (O_KR + 32, O_KR + 64), (O_KR, O_KR + 32),
                                 (O_IW, O_IW + 16)])
                p, preg = proj_fm(bv, reg, 0, 128, xrhs, ["xT"])
                evac("act", IK_T2[:, r0:r0 + TC], p, [preg], ["IK_T2"])
                p1, r1 = proj_fm(bv, reg, 128, 64, xrhs, ["xT"])
                p2, r2 = proj_fm(bv, reg, 192, 64, xrhs, ["xT"])
                rope_combine(p1, r1, p2, r2, KR_T[:, r0:r0 + TC], "KR_T")
                for t in range(NT):
                    p, preg = proj_tm(bv, reg, 256, 16, lambda kc: xT[:, kc, t * 128:(t + 1) * 128], ["xT"])
                    evac("dve", IW[:, t, :], p, [preg], ["IW"])
                def gen_ptail():
                    ck(cx, "P4")
                    for which, (o0, gcol, greg_) in enumerate(((O_CQ, gq, "gq"), (O_CKV, gkv, "gkv"))):
                        yield
                        bv, reg = wload([(o0, o0 + 512)])
                        for k in range(4):
                            yield
                            p, preg = proj_fm(bv, reg, k * 128, 128, xrhs, ["xT"])
                            sc.op("dve", lambda e: e.tensor_copy(out=cf[:, k, :], in_=p), reads=[preg], writes=["cf"])
                            sc.op("dve", lambda e: e.tensor_tensor(out=sq[:, k, :], in0=cf[:, k, :], in1=cf[:, k, :], op=ALU.mult),
                                  reads=["cf"], writes=["sq"])
                        ck(cx, "P5a")
                        sc.group("pe", [(lambda e, k=k: e.matmul(pr[:, 0:TC], cx.ones[:, :], sq[:, k, :], start=(k == 0), stop=(k == 3))) for k in range(4)],
                                 reads=["ones", "sq"], writes=["pr"])
                        sc.op("dve", lambda e: e.tensor_scalar(out=rstd[:, :], in0=pr[:, 0:TC], scalar1=1.0 / 512, scalar2=RMS_EPS,
                                                               op0=ALU.mult, op1=ALU.add), reads=["pr"], writes=["rstd"])
                        ck(cx, "P5b")
                        sc.op("act", lambda e: e.activation(out=rstd[:, :], in_=rstd[:, :], func=AF.Sqrt), reads=["rstd"], writes=["rstd"])
                        sc.op("dve", lambda e: e.reciprocal(out=rstd[:, :], in_=rstd[:, :]), reads=["rstd"], writes=["rstd"])
                        ck(cx, "P5c")
                        for k in range(4):
                            sc.op("dve", lambda e: e.scalar_tensor_tensor(out=CN[which][:, k, :], in0=cf[:, k, :], scalar=gcol[:, k:k + 1],
                                                                          in1=rstd[:, :], op0=ALU.mult, op1=ALU.mult),
                                  reads=["cf", "rstd", greg_], writes=["CN%d" % which])
                    ck(cx, "P6")
                    yield
                    bv, reg = wload([(O_XQ, O_XQ + 512)])
                    for h in range(C_H):
                        yield
                        p, preg = proj_fm(bv, reg, h * 128, 128, xrhs, ["xT"])
                        evac("act" if h % 2 else "dve", XQ_T[:, h, :], p, [preg], ["XQ_T"])
                    rngs = [(0, 1152)]
                    for h in range(B_H):
                        rngs += [(h * 192 + 160, h * 192 + 192), (h * 192 + 128, h * 192 + 160)]
                    yield
                    bv, reg = wload(rngs, "w_uq", kcn=4)
                    qrhs = lambda kc: CN[0][:, kc, :]
                    for h in range(B_H):
                        yield
                        p, preg = proj_fm(bv, reg, h * 192, 128, qrhs, ["CN0"], kcn=4)
                        evac("act", QN_T[:, h, :], p, [preg], ["QN_T"])
                        yield
                        p1, r1 = proj_fm(bv, reg, h * 192 + 128, 64, qrhs, ["CN0"], kcn=4)
                        p2, r2 = proj_fm(bv, reg, 1152 + h * 64, 64, qrhs, ["CN0"], kcn=4)
                        rope_combine(p1, r1, p2, r2, QR_T[:, h, :], "QR_T")
                    ck(cx, "P8")
                    rngs = [(h * 256, h * 256 + 128) for h in range(B_H)] + [(h * 256 + 128, h * 256 + 256) for h in range(B_H)]
                    yield
                    bv, reg = wload(rngs, "w_ukv", kcn=4)
                    krhs = lambda kc: CN[1][:, kc, :]
                    for h in range(B_H):
                        yield
                        p, preg = proj_fm(bv, reg, h * 128, 128, krhs, ["CN1"], kcn=4)
                        hv = (h % 2) * TC
                        evac("act" if h % 2 else "dve", vst[:, hv:hv + TC], p, [preg], ["vstk%d" % (h % 2)])
                        sc.dma("sp", KN[h, :, r0:r0 + TC], vst[:, hv:hv + TC], reads=["vstk%d" % (h % 2)], writes=["KN%d" % h])
                    for t in range(NT):
                        yield
                        p, preg = proj_tm(bv, reg, 768, 512, lambda kc: CN[1][:, kc, t * 128:(t + 1) * 128], ["CN1"], kcn=4)
                        evac("act", vst[:, 0:512], p, [preg], ["vstk0", "vstk1"])
                        yield
                        p, preg = proj_tm(bv, reg, 768 + 512, 256, lambda kc: CN[1][:, kc, t * 128:(t + 1) * 128], ["CN1"], kcn=4)
                        evac("dve", vst[:, 512:768], p, [preg], ["vstv"])
                        for h in range(B_H):
                            sc.dma("sp", VBd[h, r0 + t * 128:r0 + (t + 1) * 128, :], vst[:, h * 128:(h + 1) * 128],
                                   reads=["vstk0", "vstk1", "vstv"], writes=["VB%d" % h])

                    yield
            if True:
                acc = alloc(I_, "acc", [128, S], F32)
                wk = alloc(I_, "wk", [128, S], F32)
                Mk = alloc(I_, "Mk", [128, S], BF16)
                rl = [alloc(I_, "rl%d" % i, [128, 512], F32) for i in range(2)]
                def gen_I():
                    for tt in range(NT):
                        i = 2 * c + tt
                        L = (i + 1) * 128
                        for hh in range(I_H):
                            pb = (hh % 2) * 64
                            for k4 in range((L + 511) // 512):
                                n = min(512, L - k4 * 512)
                                yield
                                q = pjc[0] % 2
                                pjc[0] += 1
                                sc.group("pe", [lambda e: e.matmul(pj[q][:, 0:n], IQ_T[pb:pb + 64, hh // 2, tt * 128:(tt + 1) * 128],
                                                                   IK_T2[pb:pb + 64, k4 * 512:k4 * 512 + n], start=True, stop=True)],
                                         reads=["IQ_T", "IK_T2"], writes=["pj%d" % q])
                                sc.op("act", lambda e: e.activation(out=rl[q][:, 0:n], in_=pj[q][:, 0:n], func=AF.Relu),
                                      reads=["pj%d" % q], writes=["rl%d" % q])
                                a = acc[:, k4 * 512:k4 * 512 + n]
                                if hh == 0:
                                    sc.op("dve", lambda e: e.tensor_scalar(out=a, in0=rl[q][:, 0:n], scalar1=IW[:, tt, hh:hh + 1], scalar2=None,
                                                                           op0=ALU.mult), reads=["rl%d" % q, "IW"], writes=["acc"])
                                else:
                                    sc.op("dve", lambda e: e.scalar_tensor_tensor(out=a, in0=rl[q][:, 0:n], scalar=IW[:, tt, hh:hh + 1], in1=a,
                                                                                  op0=ALU.mult, op1=ALU.add),
                                          reads=["rl%d" % q, "IW", "acc"], writes=["acc"])
                        sc.op("dve", lambda e: e.tensor_tensor(out=acc[:, L - 128:L], in0=acc[:, L - 128:L], in1=cx.negtri[:, :], op=ALU.add),
                              reads=["acc", "negtri"], writes=["acc"])
                        if i >= 2:
                            for r in range(32):
                                srcv = acc if r == 0 else wk
                                sreg = "acc" if r == 0 else "wk"
                                yield
                                sc.op("dve", lambda e: e.max(out=mx[:, 0:8], in_=srcv[:, 0:L]), reads=[sreg], writes=["mx"])
                                if r < 31:
                                    sc.op("dve", lambda e: e.match_replace(out=wk[:, 0:L], in_to_replace=mx[:, 0:8], in_values=srcv[:, 0:L],
                                                                           imm_value=NEG), reads=[sreg, "mx"], writes=["wk"])
                            sc.op("dve", lambda e: e.tensor_reduce(out=mx[:, 8:9], in_=mx[:, 0:8], axis=mybir.AxisListType.X, op=ALU.min),
                                  reads=["mx"], writes=["mx"])
                            thr_ap, thr_reg = mx[:, 8:9], "mx"
                        else:
                            thr_ap, thr_reg = cx.thrlow[:, 0:1], "thrlow"
                        sc.op("dve", lambda e: e.tensor_scalar(out=Mk[:, 0:L], in0=acc[:, 0:L], scalar1=thr_ap, scalar2=None, op0=ALU.is_ge),
                              reads=["acc", thr_reg], writes=["Mk"])
                        for j0 in range(0, i + 1, 8):
                            nj = min(8, i + 1 - j0)
                            yield
                            sc.group("pe", [(lambda e, jj=jj: e.transpose(out=tp[:, jj * 128:(jj + 1) * 128],
                                                                          in_=Mk[:, (j0 + jj) * 128:(j0 + jj + 1) * 128],
                                                                          identity=cx.ident[:, :])) for jj in range(nj)],
                                     reads=["Mk", "ident"], writes=["tp"])
                            evac("act", MT[:, j0:j0 + nj, tt * 128:(tt + 1) * 128], tp[:, 0:nj * 128].rearrange("p (k n) -> p k n", k=nj),
                                 ["tp"], ["MT"])
                        if tt == 0:
                            sc.op("dve", lambda e: e.memset(MT[:, 2 * c + 1, 0:128], 0.0), writes=["MT"])

                    yield
            if True:
                pT = [alloc(A_, "pT%d" % i, [128, TC], BF16) for i in range(2)]
                rc = alloc(A_, "rc", [128, TC], F32)
                KNhs = [alloc(A_, "KNh0", [128, S], BF16)]
                KNhs.append(KNhs[0])
                VBhs = [alloc(A_, "VBh%d" % i, [128, S // 128, 128], BF16) for i in range(2)]

                def attend(nk, qk_fns, qk_reads, v_of, v_reads, scale, post, out_ap):
                    for j in range(nk):
                        yield
                        q = acnt[0] % 2
                        acnt[0] += 1
                        sc.group("pe", qk_fns(j, sT[q][:, 0:TC]), reads=qk_reads, writes=["sT%d" % q])
                        masks, bias = post(j)
                        if bias is None:
                            sc.op("act", lambda e: e.activation(out=pT[q][:, :], in_=sT[q][:, 0:TC], func=AF.Exp, scale=scale),
                                  reads=["sT%d" % q], writes=["pT%d" % q])
                        else:
                            sc.op("act", lambda e: e.activation(out=pT[q][:, :], in_=sT[q][:, 0:TC], func=AF.Exp, bias=bias, scale=scale),
                                  reads=["sT%d" % q, "rb31"], writes=["pT%d" % q])
                        for (map_, mregs) in masks:
                            sc.op("dve", lambda e: e.tensor_tensor(out=pT[q][:, :], in0=pT[q][:, :], in1=map_, op=ALU.mult),
                                  reads=["pT%d" % q] + mregs, writes=["pT%d" % q])
                        sc.group("pe", [lambda e: e.matmul(oT[:, 0:TC], v_of(j), pT[q][:, :], start=(j == 0), stop=(j == nk - 1)),
                                        lambda e: e.matmul(rsum[:, 0:TC], cx.ones[:, :], pT[q][:, :], start=(j == 0), stop=(j == nk - 1))],
                                 reads=["pT%d" % q, "ones"] + v_reads, writes=["oT", "rsum"])
                    sc.op("dve", lambda e: e.reciprocal(out=rc[:, :], in_=rsum[:, 0:TC]), reads=["rsum"], writes=["rc"])
                    sc.op("dve", lambda e: e.tensor_tensor(out=out_ap, in0=oT[:, 0:TC], in1=rc[:, :], op=ALU.mult),
                          reads=["oT", "rc"], writes=["OT"])
                    yield

                sca = 128 ** -0.5
                scb = 192 ** -0.5
                def gen_BC():
                    for h in range(B_H):
                        KNh, VBh = KNhs[h % 2], VBhs[h % 2]
                        kreg, vreg = "KNh0", "VBh%d" % (h % 2)
                        sc.dma("sp", KNh[:, 0:nkt * 128], KN[h, :, 0:nkt * 128], reads=["KN%d" % h], writes=[kreg])
                        sc.dma("sp", VBh[:, 0:nkt, :], VBd[h, 0:nkt * 128, :].rearrange("(t p) d -> p t d", p=128), reads=["VB%d" % h], writes=[vreg])
                        def postb(j):
                            if j >= 2 * c:
                                return [(cx.cm[:, j - 2 * c, :], ["cm"])], None
                            return [], None
                        yield from attend(nkt, lambda j, o, h=h, KNh=KNh: [lambda e: e.matmul(o, KNh[:, j * 128:(j + 1) * 128], QN_T[:, h, :], start=True, stop=False),
                                                       lambda e: e.matmul(o, KR_T[:, j * 128:(j + 1) * 128], QR_T[:, h, :], start=False, stop=True)],
                               [kreg, "KR_T", "QN_T", "QR_T"], lambda j, VBh=VBh: VBh[:, j, :], [vreg], scb, postb, OT[:, 6 + h, :])
                    for h in range(C_H):
                        yield from attend(2, lambda j, o, h=h: [lambda e: e.matmul(o, MK_T[:, h, j * 128:(j + 1) * 128], XQ_T[:, h, :], start=True, stop=True)],
                               ["MK_T", "XQ_T"], lambda j, h=h: MV[:, j, h * 128:(h + 1) * 128], ["MV"], sca, lambda j: ([], None), OT[:, 12 + h, :])
                def run_all(g):
                    for _ in g:
                        pass
                def merge(*gens):
                    gens = list(gens)
                    while gens:
                        for g in list(gens):
                            try:
                                next(g)
                            except StopIteration:
                                gens.remove(g)
                def chain2(a, b):
                    yield from a
                    yield from b
                def gen_gates():
                    for b in range(3):
                        for n4 in range(4):
                            bv, reg = wload([(O_G + b * D + n4 * 512, O_G + b * D + (n4 + 1) * 512)])
                            for t in range(NT):
                                p, preg = proj_tm(bv, reg, 0, 512, lambda kc: xT[:, kc, t * 128:(t + 1) * 128], ["xT"])
                                sc.op("act", lambda e: e.activation(out=Gt[:, t, b * D + n4 * 512:b * D + (n4 + 1) * 512], in_=p, func=AF.Sigmoid),
                                      reads=[preg], writes=["Gt"])
                                yield
                merge(gen_I(), chain2(chain2(gen_ptail(), gen_BC()), gen_gates()))
                for h in range(A_H):
                    def post(j, h=h):
                        Dd = r0 - 128 * j
                        if Dd <= 128:
                            return [(cx.TH[:, h, Dd + 128:Dd + 128 + TC], ["ebias"]), (MT[:, j, :], ["MT"])], None
                        return [(MT[:, j, :], ["MT"])], cx.rb31[:, h:h + 1]
                    run_all(attend(nkt, lambda j, o, h=h: [lambda e: e.matmul(o, KA_T[:, j * 128:(j + 1) * 128], QA_T[:, h, :], start=True, stop=True)],
                           ["KA_T", "QA_T"], lambda j: VA[:, j, :], ["VA"], sca, post, OT[:, h, :]))
                sc.barrier()
            PIA.close()

            with ExitStack() as M_:
                Mm = alloc(M_, "Mm", [128, 16, TC], BF16)
                mblk = [alloc(M_, "mblk%d" % i, [128, 512], F32) for i in range(NT)]
                mbf = [alloc(M_, "mbf%d" % i, [128, 512], BF16) for i in range(NT)]
                mtmp = alloc(M_, "mtmp", [128, 512], F32)
                gt = alloc(M_, "gt", [128, D], F32)
                bt = alloc(M_, "bt", [128, D], F32)
                junk = alloc(M_, "junk", [128, D], BF16)
                sc.dma("sp", gt[:, :], T["ln_g"][l, 1:2, :].partition_broadcast(128), writes=["lng"])
                sc.dma("sp", bt[:, :], T["ln_b"][l, 1:2, :].partition_broadcast(128), writes=["lnb"])
                for n4 in range(4):
                    cols = (n4 * 512, (n4 + 1) * 512)
                    wbr, wbr_reg = wload([cols], "w_branch")
                    for b, (ra, rb_) in enumerate(((0, 6), (6, 12), (12, 16))):
                        for t in range(NT):
                            g_ = Gt[:, t, b * D + cols[0]:b * D + cols[1]]
                            q = acnt[0] % 2
                            acnt[0] += 1
                            sc.group("pe", [(lambda e, r=r: e.matmul(sT[q][:, 0:512], OT[:, r, t * 128:(t + 1) * 128], wbr[:, r, :],
                                                                     start=(r == ra), stop=(r == rb_ - 1))) for r in range(ra, rb_)],
                                     reads=[wbr_reg, "OT"], writes=["sT%d" % q])
                            if b == 0:
                                sc.op("dve", lambda e: e.tensor_tensor(out=mblk[t][:, :], in0=sT[q][:, 0:512], in1=g_, op=ALU.mult),
                                      reads=["Gt", "sT%d" % q], writes=["mblk%d" % t])
                            else:
                                sc.op("dve", lambda e: e.tensor_tensor(out=mtmp[:, :], in0=sT[q][:, 0:512], in1=g_, op=ALU.mult),
                                      reads=["Gt", "sT%d" % q], writes=["mtmp"])
                                if b == 1:
                                    sc.op("dve", lambda e: e.tensor_tensor(out=mblk[t][:, :], in0=mblk[t][:, :], in1=mtmp[:, :], op=ALU.add),
                                          reads=["mblk%d" % t, "mtmp"], writes=["mblk%d" % t])
                                else:
                                    sc.op("dve", lambda e: e.tensor_tensor(out=mbf[t][:, :], in0=mblk[t][:, :], in1=mtmp[:, :], op=ALU.add),
                                          reads=["mblk%d" % t, "mtmp"], writes=["mbf%d" % t])
                    for t in range(NT):
                        sc.group("pe", [(lambda e, kk=kk: e.transpose(out=tp[:, kk * 128:(kk + 1) * 128], in_=mbf[t][:, kk * 128:(kk + 1) * 128],
                                                                      identity=cx.ident[:, :])) for kk in range(4)],
                                 reads=["mbf%d" % t, "ident"], writes=["tp"])
                        evac("act", Mm[:, n4 * 4:(n4 + 1) * 4, t * 128:(t + 1) * 128], tp[:, 0:512].rearrange("p (k n) -> p k n", k=4),
                             ["tp"], ["Mm"])
                for cg in range(4):
                    bv, reg = wload([(cg * 512, (cg + 1) * 512)], "w_out")
                    for t in range(NT):
                        p, preg = proj_tm(bv, reg, 0, 512, lambda kc: Mm[:, kc, t * 128:(t + 1) * 128], ["Mm"])
                        sc.op("dve", lambda e: e.tensor_tensor(out=xt[:, t, cg * 512:(cg + 1) * 512], in0=xt[:, t, cg * 512:(cg + 1) * 512],
                                                               in1=p, op=ALU.add), reads=[preg, "xt"], writes=["xt"])
                for t in range(NT):
                    ln_tile(cx, xt[:, t, :], "xt", gt[:, :], bt[:, :], st, "st", junk[:, :], "junk")
                sc.dma("sp", dst[r0:r0 + TC, :].rearrange("(t p) d -> p t d", p=128), xt[:, :, :], reads=["xt"], writes=["X%d" % c])
                sc.barrier()
            if cx.stop_at == "M":
                return


WNAMES = ["ln_g", "ln_b", "ffn1_up", "ffn1_down", "w_in", "q_norm", "kv_norm", "w_uq", "w_ukv", "w_mem_kv",
          "w_branch", "w_out", "ffn2_up", "ffn2_down"]


def run_cores(inputs, n_cores=8, n_layers=DEPTH, stop_after=None, trace=False, stop_at=None, nb=1):
    nc = build_program(n_layers, stop_after, stop_at, nb)
    hc = host_consts()
    shared = {"rel_bias": np.ascontiguousarray(inputs["rel_bias"], dtype=np.float32)}
    for k in WNAMES:
        if k in BIGW:
            for li in range(n_layers):
                shared["%s_%d" % (k, li)] = np.ascontiguousarray(inputs[k][li])
        else:
            shared[k] = np.ascontiguousarray(inputs[k][:n_layers])
    for k, v in hc.items():
        shared["c_" + k] = v
    in_maps = []
    for c in range(n_cores):
        m = dict(shared)
        m["x"] = np.ascontiguousarray(inputs["x"][c * nb:(c + 1) * nb]).reshape(nb * S, D)
        m["mem"] = np.ascontiguousarray(inputs["mem"][c * nb:(c + 1) * nb]).reshape(nb * MEM, D)
        m["positions"] = np.ascontiguousarray(inputs["positions"][c * nb:(c + 1) * nb]).astype(np.int32)
        in_maps.append(m)
    res = run_bass_kernel_spmd(nc, in_maps, core_ids=list(range(n_cores)), trace=trace)
    outs = np.concatenate([np.asarray(r["out"]).reshape(nb, S, D) for r in res.results], axis=0)
    return outs, res


N_CORES = 8
N_BATCH_PER_CORE = 1


def kernel(**inputs):
    outs, _ = run_cores(inputs, N_CORES, DEPTH, nb=N_BATCH_PER_CORE)
    return outs.astype(np.float32)
```
